# Optimizing a Trainium2 kernel written in Bass

```python
import jax, jax.numpy as jnp
from jax import lax
import numpy as np

D_MODEL = 1024
BATCH = 4
SEQ = 4096
DEPTH = 2

CHUNK = 64
SGU_GROUPS = 4
SGU_GROUP_DIM = 64
SGU_WIDTH = SGU_GROUPS * SGU_GROUP_DIM
SGU_BLOCK = 128
RET_HEADS = 4
RET_KDIM = 64
RET_VDIM = 64
POOL_WINDOWS = (2, 4, 8, 16)
POOL_GROUP_DIM = 64
POOL_WIDTH = 4 * POOL_GROUP_DIM
MLA_HEADS = 4
MLA_Q_RANK = 256
MLA_KV_RANK = 128
MLA_NOPE_DIM = 64
MLA_ROPE_DIM = 32
MLA_V_DIM = 64
Q_BLOCK = 128
ROPE_BASE = 10000.0
N_BRANCH = 4
BRANCH_WIDTH = 256
D_FF = 2816
PLE_DIM = 256
ALPHA = (2 * DEPTH) ** 0.25
BETA = (8 * DEPTH) ** -0.25
LN_EPS = 1e-5
RMS_EPS = 1e-6
GN_EPS = 1e-5
IN_SIZES = (SGU_WIDTH, SGU_WIDTH,
            RET_HEADS * RET_KDIM, RET_HEADS * RET_KDIM, RET_HEADS * RET_VDIM, RET_HEADS * RET_VDIM,
            POOL_WIDTH,
            MLA_Q_RANK, MLA_KV_RANK, MLA_ROPE_DIM,
            N_BRANCH * D_MODEL)
IN_COLS = sum(IN_SIZES)

kernel_name = 'hybrid_gated_streaming_block'


def _split(h, sizes):
    parts, start = [], 0
    for size in sizes:
        parts.append(h[..., start:start + size])
        start += size
    return parts


def layer_norm(x, g, b):
    xf = x.astype(jnp.float32)
    mu = jnp.mean(xf, axis=-1, keepdims=True)
    var = jnp.mean(jnp.square(xf - mu), axis=-1, keepdims=True)
    y = (xf - mu) * lax.rsqrt(var + LN_EPS) * g.astype(jnp.float32) + b.astype(jnp.float32)
    return y.astype(x.dtype)


def rms_norm(x, g):
    xf = x.astype(jnp.float32)
    y = xf * lax.rsqrt(jnp.mean(jnp.square(xf), axis=-1, keepdims=True) + RMS_EPS) * g.astype(jnp.float32)
    return y.astype(x.dtype)


def rope_tables(positions, dim):
    inv_freq = ROPE_BASE ** (-jnp.arange(0, dim, 2, dtype=jnp.float32) / dim)
    ang = positions.astype(jnp.float32)[..., None] * inv_freq
    return jnp.cos(ang), jnp.sin(ang)


def apply_rope(x, cos, sin):
    half = x.shape[-1] // 2
    xf = x.astype(jnp.float32)
    x1, x2 = xf[..., :half], xf[..., half:]
    c, s = cos[:, :, None, :], sin[:, :, None, :]
    return jnp.concatenate([x1 * c - x2 * s, x1 * s + x2 * c], axis=-1).astype(x.dtype)


def swiglu(x, w_up, w_down):
    a, b = jnp.split(x @ w_up, 2, axis=-1)
    return (jax.nn.silu(a) * b) @ w_down


def sgu_mixer(u, v, ln_g, ln_b, w_s, b_s):
    bsz, seq, _ = u.shape
    u = jax.nn.gelu(u)
    v = layer_norm(jax.nn.gelu(v), ln_g, ln_b)
    vb = v.reshape(bsz, seq // SGU_BLOCK, SGU_BLOCK, SGU_GROUPS, SGU_GROUP_DIM)
    causal = jnp.tril(jnp.ones((SGU_BLOCK, SGU_BLOCK), dtype=bool))
    w = jnp.where(causal[None], w_s, jnp.zeros_like(w_s))
    mixed = jnp.einsum('gts,bnsgc->bntgc', w, vb) + b_s.T[None, None, :, :, None]
    return u * mixed.reshape(bsz, seq, SGU_WIDTH)


def retention_mixer(q, k, v, g, cos, sin):
    bsz, seq, _ = q.shape
    n_chunks = seq // CHUNK
    f32 = jnp.float32
    q = apply_rope(q.reshape(bsz, seq, RET_HEADS, RET_KDIM), cos, sin).astype(f32)
    k = apply_rope(k.reshape(bsz, seq, RET_HEADS, RET_KDIM), cos, sin).astype(f32) * RET_KDIM ** -0.5
    v = v.reshape(bsz, seq, RET_HEADS, RET_VDIM).astype(f32)
    log_gamma = jnp.log1p(-jnp.exp2(-5.0 - jnp.arange(RET_HEADS, dtype=f32)))
    pos = jnp.arange(CHUNK, dtype=f32)
    intra_decay = jnp.exp(log_gamma[:, None, None] * jnp.abs(pos[:, None] - pos[None, :]))
    key_to_end = jnp.exp(log_gamma[:, None] * (CHUNK - 1 - pos)[None, :])
    start_to_query = jnp.exp(log_gamma[:, None] * (pos + 1)[None, :])
    chunk_decay = jnp.exp(log_gamma * CHUNK)
    qc = q.reshape(bsz, n_chunks, CHUNK, RET_HEADS, RET_KDIM)
    kc = k.reshape(bsz, n_chunks, CHUNK, RET_HEADS, RET_KDIM)
    vc = v.reshape(bsz, n_chunks, CHUNK, RET_HEADS, RET_VDIM)
    scores = jnp.einsum('bnihd,bnjhd->bnhij', qc, kc) * intra_decay
    y = jnp.einsum('bnhij,bnjhe->bnihe', scores, vc)
    kv = jnp.einsum('bnjhd,hj,bnjhe->nbhde', kc, key_to_end, vc)

    def step(state, kv_chunk):
        return state * chunk_decay[None, :, None, None] + kv_chunk, state

    _, prev = lax.scan(step, jnp.zeros((bsz, RET_HEADS, RET_KDIM, RET_VDIM), f32), kv)
    y = y + jnp.einsum('bnihd,nbhde,hi->bnihe', qc, prev, start_to_query)
    y = y.reshape(bsz, seq, RET_HEADS, RET_VDIM)
    mu = jnp.mean(y, axis=-1, keepdims=True)
    var = jnp.mean(jnp.square(y - mu), axis=-1, keepdims=True)
    y = (y - mu) * lax.rsqrt(var + GN_EPS)
    return (jax.nn.silu(g.astype(f32)) * y.reshape(bsz, seq, RET_HEADS * RET_VDIM)).astype(g.dtype)


def pool_mixer(z, w_pool, scale):
    bsz, seq, _ = z.shape
    zf = z.astype(jnp.float32)
    csum = jnp.concatenate([jnp.zeros((bsz, 1, POOL_WIDTH), jnp.float32), jnp.cumsum(zf, axis=1)], axis=1)
    t = jnp.arange(seq)
    groups = []
    for gi, window in enumerate(POOL_WINDOWS):
        ch = slice(gi * POOL_GROUP_DIM, (gi + 1) * POOL_GROUP_DIM)
        lo = jnp.maximum(t + 1 - window, 0)
        count = (t + 1 - lo).astype(jnp.float32)[None, :, None]
        mean = (csum[:, 1:, ch] - csum[:, lo, ch]) / count
        groups.append(mean - zf[:, :, ch])
    pooled = jnp.stack(groups, axis=2).astype(z.dtype)
    y = jnp.einsum('bsgc,gcd->bsgd', pooled, w_pool).reshape(bsz, seq, POOL_WIDTH)
    return y * scale


def mla_mixer(c_q, c_kv, k_rope, q_norm_g, kv_norm_g, w_uq, w_ukv, cos, sin):
    bsz, seq, _ = c_q.shape
    qk_dim = MLA_NOPE_DIM + MLA_ROPE_DIM
    q = (rms_norm(c_q, q_norm_g) @ w_uq).reshape(bsz, seq, MLA_HEADS, qk_dim)
    kv = (rms_norm(c_kv, kv_norm_g) @ w_ukv).reshape(bsz, seq, MLA_HEADS, MLA_NOPE_DIM + MLA_V_DIM)
    k_nope, v = kv[..., :MLA_NOPE_DIM], kv[..., MLA_NOPE_DIM:]
    q = jnp.concatenate([q[..., :MLA_NOPE_DIM], apply_rope(q[..., MLA_NOPE_DIM:], cos, sin)], axis=-1) * qk_dim ** -0.5
    k_pe = apply_rope(k_rope[:, :, None, :], cos, sin)
    k = jnp.concatenate([k_nope, jnp.broadcast_to(k_pe, (bsz, seq, MLA_HEADS, MLA_ROPE_DIM))], axis=-1)
    n_blocks = seq // Q_BLOCK
    q_blocks = jnp.moveaxis(q.reshape(bsz, n_blocks, Q_BLOCK, MLA_HEADS, qk_dim), 1, 0)
    key_chunk = jnp.arange(seq) // CHUNK

    def attend(args):
        q_blk, blk = args
        query_chunk = (blk * Q_BLOCK + jnp.arange(Q_BLOCK)) // CHUNK
        s = jnp.einsum('bqhd,bkhd->bhqk', q_blk, k).astype(jnp.float32)
        s = jnp.where(key_chunk[None, :] <= query_chunk[:, None], s, -jnp.inf)
        probs = jax.nn.softmax(s, axis=-1).astype(v.dtype)
        return jnp.einsum('bhqk,bkhe->bqhe', probs, v)

    out = lax.map(attend, (q_blocks, jnp.arange(n_blocks)))
    return jnp.moveaxis(out, 0, 1).reshape(bsz, seq, MLA_HEADS * MLA_V_DIM)


def token_mix(x, w_in, sgu_ln_g, sgu_ln_b, sgu_w, sgu_b, pool_w, pool_scale,
              mla_q_norm, mla_kv_norm, mla_w_uq, mla_w_ukv, w_branch, w_out,
              ret_cos, ret_sin, mla_cos, mla_sin):
    bsz, seq, dm = x.shape
    (sgu_u, sgu_v, ret_q, ret_k, ret_v, ret_g, pool_in,
     c_q, c_kv, k_rope, gate_logits) = _split(x @ w_in, IN_SIZES)
    ys = (sgu_mixer(sgu_u, sgu_v, sgu_ln_g, sgu_ln_b, sgu_w, sgu_b).astype(x.dtype),
          retention_mixer(ret_q, ret_k, ret_v, ret_g, ret_cos, ret_sin).astype(x.dtype),
          pool_mixer(pool_in, pool_w, pool_scale).astype(x.dtype),
          mla_mixer(c_q, c_kv, k_rope, mla_q_norm, mla_kv_norm, mla_w_uq, mla_w_ukv, mla_cos, mla_sin).astype(x.dtype))
    gates = jax.nn.sigmoid(gate_logits).reshape(bsz, seq, N_BRANCH, dm)
    merged = gates[:, :, 0] * (ys[0] @ w_branch[0])
    for n in range(1, N_BRANCH):
        merged = merged + gates[:, :, n] * (ys[n] @ w_branch[n])
    return merged @ w_out


def setup_inputs(seed: int = 0) -> dict:
    key = jax.random.key(seed)
    ks = jax.random.split(key, 32)

    def nrm(k, shape, scale):
        return jax.random.normal(k, shape, jnp.float32) * scale

    def gain(k, shape):
        return 1.0 + 0.05 * jax.random.normal(k, shape, jnp.float32)

    def bias(k, shape):
        return 0.02 * jax.random.normal(k, shape, jnp.float32)

    offset = jax.random.randint(ks[2], (BATCH, 1), 0, 64, dtype=jnp.int32) * CHUNK
    positions = (offset + jnp.arange(SEQ, dtype=jnp.int32)[None, :]).astype(jnp.int32)
    L, D = DEPTH, D_MODEL
    return {
        'x': nrm(ks[0], (BATCH, SEQ, D), 1.0),
        'p': nrm(ks[1], (DEPTH, BATCH, SEQ, PLE_DIM), 1.0),
        'positions': positions,
        'ffn1_up': nrm(ks[3], (L, D, 2 * D_FF), D ** -0.5),
        'ffn1_down': nrm(ks[4], (L, D_FF, D), BETA * D_FF ** -0.5),
        'ln1_g': gain(ks[5], (L, D)),
        'ln1_b': bias(ks[6], (L, D)),
        'w_in': nrm(ks[7], (L, D, IN_COLS), D ** -0.5),
        'sgu_ln_g': gain(ks[8], (L, SGU_WIDTH)),
        'sgu_ln_b': bias(ks[9], (L, SGU_WIDTH)),
        'sgu_w': nrm(ks[10], (L, SGU_GROUPS, SGU_BLOCK, SGU_BLOCK), SGU_BLOCK ** -0.5),
        'sgu_b': 1.0 + 0.1 * jax.random.normal(ks[11], (L, SGU_GROUPS, SGU_BLOCK), jnp.float32),
        'pool_w': nrm(ks[12], (L, 4, POOL_GROUP_DIM, POOL_GROUP_DIM), POOL_GROUP_DIM ** -0.5),
        'pool_scale': gain(ks[13], (L, POOL_WIDTH)),
        'mla_q_norm': gain(ks[14], (L, MLA_Q_RANK)),
        'mla_kv_norm': gain(ks[15], (L, MLA_KV_RANK)),
        'mla_w_uq': nrm(ks[16], (L, MLA_Q_RANK, MLA_HEADS * (MLA_NOPE_DIM + MLA_ROPE_DIM)), MLA_Q_RANK ** -0.5),
        'mla_w_ukv': nrm(ks[17], (L, MLA_KV_RANK, MLA_HEADS * (MLA_NOPE_DIM + MLA_V_DIM)), MLA_KV_RANK ** -0.5),
        'w_branch': nrm(ks[18], (L, N_BRANCH, BRANCH_WIDTH, D), BRANCH_WIDTH ** -0.5),
        'w_out': nrm(ks[19], (L, D, D), BETA * D ** -0.5),
        'ln2_g': gain(ks[20], (L, D)),
        'ln2_b': bias(ks[21], (L, D)),
        'ffn2_up': nrm(ks[22], (L, D, 2 * D_FF), D ** -0.5),
        'ffn2_down': nrm(ks[23], (L, D_FF, D), BETA * D_FF ** -0.5),
        'w_ple_gate': nrm(ks[24], (L, D, D), D ** -0.5),
        'w_ple': nrm(ks[25], (L, PLE_DIM, D), PLE_DIM ** -0.5),
        'ln3_g': gain(ks[26], (L, D)),
        'ln3_b': bias(ks[27], (L, D)),
    }


def reference(x, p, positions, ffn1_up, ffn1_down, ln1_g, ln1_b, w_in, sgu_ln_g, sgu_ln_b,
              sgu_w, sgu_b, pool_w, pool_scale, mla_q_norm, mla_kv_norm, mla_w_uq, mla_w_ukv,
              w_branch, w_out, ln2_g, ln2_b, ffn2_up, ffn2_down, w_ple_gate, w_ple, ln3_g, ln3_b):
    ret_cos, ret_sin = rope_tables(positions, RET_KDIM)
    mla_cos, mla_sin = rope_tables(positions, MLA_ROPE_DIM)
    for i in range(DEPTH):
        x = layer_norm(ALPHA * x + 0.5 * swiglu(x, ffn1_up[i], ffn1_down[i]), ln1_g[i], ln1_b[i])
        mix = token_mix(x, w_in[i], sgu_ln_g[i], sgu_ln_b[i], sgu_w[i], sgu_b[i], pool_w[i], pool_scale[i],
                        mla_q_norm[i], mla_kv_norm[i], mla_w_uq[i], mla_w_ukv[i], w_branch[i], w_out[i],
                        ret_cos, ret_sin, mla_cos, mla_sin)
        x = layer_norm(ALPHA * x + mix, ln2_g[i], ln2_b[i])
        ple = jax.nn.sigmoid(x @ w_ple_gate[i]) * (p[i] @ w_ple[i])
        x = layer_norm(ALPHA * x + 0.5 * swiglu(x, ffn2_up[i], ffn2_down[i]) + ple, ln3_g[i], ln3_b[i])
    return x
```

```python
import os
import math
from contextlib import ExitStack

import numpy as np
import concourse.bass as bass
import concourse.mybir as mybir
from concourse.bass_utils import run_bass_kernel_spmd

ACT = mybir.ActivationFunctionType
ALU = mybir.AluOpType
F32 = mybir.dt.float32
BF16 = mybir.dt.bfloat16
I32 = mybir.dt.int32
AX = mybir.AxisListType

D = 1024
SEQ = 4096
HALF = 2048
NT = 16
DFF = 2816
ALPHA = 4 ** 0.25
LN_EPS = 1e-5
RMS_EPS = 1e-6
GN_EPS = 1e-5
SPLITS = [(0, 8), (8, 7), (15, 7)]
NCORES = int(os.environ.get("MK_NCORES", "4"))
DEBUG = os.environ.get("MK_DEBUG", "")


class Op:
    __slots__ = ("eng", "fn", "raw", "war", "dma", "needs_inc", "token", "idx")

    def __init__(self, eng, fn, dma, idx):
        self.eng = eng
        self.fn = fn
        self.raw = set()
        self.war = set()
        self.dma = dma
        self.needs_inc = False
        self.token = None
        self.idx = idx


COMPUTE = ("pe", "act", "dve", "pool")


class Sched:
    NDSEM = {"sp": 8, "pool": 8, "act": 2}

    def __init__(self):
        self.ops = []
        self.lastw = {}
        self.readers = {}
        self.last_of = {}
        self.open_dma = []

    def add(self, eng, fn, r=(), w=(), dma=False, after=()):
        idx = len(self.ops)
        op = Op(eng, fn, dma, idx)
        for res in r:
            lw = self.lastw.get(res)
            if lw is not None:
                op.raw.add(lw)
            if isinstance(res, tuple) and res[0] == "pf" or res == "pbt":
                rd = self.readers.get(res)
                if rd:
                    for k_, x in rd.items():
                        if k_ != eng and not isinstance(x, list):
                            op.raw.add(x)
        for res in w:
            lw = self.lastw.get(res)
            if lw is not None:
                op.war.add(lw)
            rd = self.readers.get(res)
            if rd:
                for x in rd.values():
                    if isinstance(x, list):
                        op.war.update(x)
                    else:
                        op.war.add(x)
        for a in after:
            op.raw.add(a)
        for res in w:
            self.lastw[res] = idx
            self.readers[res] = {}
        for res in r:
            if res in w:
                continue
            rd = self.readers.setdefault(res, {})
            if dma:
                rd.setdefault(("dma", eng), []).append(idx)
            else:
                rd[eng] = idx
        self.ops.append(op)
        if fn is not None:
            if dma:
                self.open_dma.append(idx)
            else:
                self.last_of[eng] = idx
        return idx

    def barrier(self):
        deps = list(self.last_of.values()) + list(self.open_dma)
        self.open_dma = []
        for e in ("pe", "act", "dve", "pool", "sp"):
            self.add(e, None, after=deps)
        self.lastw = {}
        self.readers = {}

    def emit(self, nc, stack):
        ops = self.ops
        sems = {e: stack.enter_context(nc.semaphore("s_" + e)) for e in COMPUTE}
        dsems = {
            q: [stack.enter_context(nc.semaphore("d_%s%d" % (q, i))) for i in range(n)]
            for q, n in self.NDSEM.items()
        }
        rr = {q: 0 for q in dsems}
        ncnt = {q: [0] * len(dsems[q]) for q in dsems}
        lastd = {q: [None] * len(dsems[q]) for q in dsems}
        for op in ops:
            if op.dma:
                q = op.eng
                k = rr[q] % len(dsems[q])
                rr[q] += 1
                ncnt[q][k] += 1
                op.token = (dsems[q][k], 16 * ncnt[q][k])
                if lastd[q][k] is not None:
                    op.raw.add(lastd[q][k])
                lastd[q][k] = op.idx
        for op in ops:
            deps = set()
            for d in op.raw:
                p = ops[d]
                if p.dma or op.dma or p.eng != op.eng or op.eng != "pe":
                    deps.add(d)
            for d in op.war:
                p = ops[d]
                if p.dma or op.dma or p.eng != op.eng or op.eng != "pe":
                    deps.add(d)
            deps.discard(op.idx)
            op.raw = deps
            for d in deps:
                ops[d].needs_inc = True
        cnt = {e: 0 for e in COMPUTE}
        for op in ops:
            if op.dma or op.fn is None:
                continue
            if op.needs_inc:
                cnt[op.eng] += 1
                op.token = (sems[op.eng], cnt[op.eng])
        self.stats = dict(cnt)
        self.stats["nops"] = len(ops)
        per_eng = {}
        for op in ops:
            per_eng.setdefault(op.eng, []).append(op)

        def emit_engine(ename, e):
            waited = {}
            for op in per_eng.get(ename, []):
                need = {}
                for d in op.raw:
                    tok = ops[d].token
                    if tok is None:
                        continue
                    s, v = tok
                    if waited.get(id(s), 0) >= v:
                        continue
                    if need.get(id(s), (None, 0))[1] < v:
                        need[id(s)] = (s, v)
                for s, v in need.values():
                    e.wait_ge(s, v)
                    waited[id(s)] = v
                if op.fn is None:
                    continue
                ins = op.fn(e)
                if op.dma:
                    ins.then_inc(op.token[0], 16)
                elif op.needs_inc:
                    ins.then_inc(op.token[0], 1)

        with nc.Block() as block:

            @block.tensor
            def _(e):
                emit_engine("pe", e)

            @block.scalar
            def _(e):
                emit_engine("act", e)

            @block.vector
            def _(e):
                emit_engine("dve", e)

            @block.gpsimd
            def _(e):
                emit_engine("pool", e)

            @block.sync
            def _(e):
                emit_engine("sp", e)


def host_consts():
    c = {}
    c["ident"] = np.eye(128, dtype=np.float32)
    lg = np.log1p(-np.exp2(-5.0 - np.arange(4, dtype=np.float64)))
    j = np.arange(128)[:, None]
    i = np.arange(128)[None, :]
    same = (j // 64) == (i // 64)
    dt = np.zeros((128, 4, 128), np.float64)
    for h in range(4):
        m = np.where(same, np.exp(lg[h] * np.abs(i - j)), np.where(j < i, np.exp(lg[h] * (i - j)), 0.0))
        dt[:, h, :] = m * (64 ** -0.5)
    c["ret_dt"] = dt.astype(np.float32)
    tl = np.arange(128)[:, None]
    dec = np.zeros((128, 8), np.float64)
    dec[:, 0:4] = np.exp(lg[None, :] * (tl + 1))
    dec[:, 4:8] = np.exp(lg[None, :] * (127 - tl)) * (64 ** -0.5)
    c["ret_dec"] = dec.astype(np.float32)
    c["g128"] = [float(np.exp(lg[h] * 128)) for h in range(4)]
    c["sgu_mask"] = (i >= j).astype(np.float32)
    invf = (10000.0 ** (-np.arange(0, 64, 2, dtype=np.float32) / 64)).astype(np.float32)
    c["invf"] = np.broadcast_to(invf[None, :], (128, 32)).copy()
    band = np.zeros((128, 3, 4, 128), np.float64)
    for wi, w in enumerate((2, 4, 8, 16)):
        dts = i - j
        band[:, 0, wi, :] = np.where((dts >= 0) & (dts < w), 1.0 / w, 0.0) - (dts == 0)
        band[:, 1, wi, :] = np.where((dts + 128 >= 0) & (dts + 128 < w), 1.0 / w, 0.0)
        cnt = np.minimum(i + 1, w)
        band[:, 2, wi, :] = np.where((dts >= 0) & (dts < w), 1.0 / cnt, 0.0) - (dts == 0)
    c["pool_band"] = band.astype(np.float32).reshape(128, 12 * 128)
    return c


def _bc(ap, dims):
    return bass.AP(ap.tensor, ap.offset, [list(ap.ap[0])] + [[s, n] for s, n in dims])


class Builder:
    def __init__(self):
        self.nc = bass.Bass("TRN2", target_bir_lowering=False)
        self.S = Sched()
        self.din = {}

    def dram_in(self, name, shape, dt=F32):
        t = self.nc.dram_tensor(name, list(shape), dt, kind="ExternalInput")
        self.din[name] = (tuple(shape), dt)
        return t.ap()

    def mm(self, out, lhsT, rhs, start, stop, r, w, **kw):
        self.S.add("pe", lambda e: e.matmul(out, lhsT=lhsT, rhs=rhs, start=start, stop=stop, **kw), r=r, w=w)

    def tr(self, out, in_, r, w):
        ident = self.identb
        n = in_.shape[0]
        self.S.add("pe", lambda e: e.transpose(out, in_, ident[0:n, 0:n]), r=list(r), w=w)

    def act(self, out, in_, func, r, w, **kw):
        self.S.add("act", lambda e: e.activation(out=out, in_=in_, func=func, **kw), r=r, w=w)

    def op(self, eng, name, r, w, *a, **kw):
        self.S.add(eng, lambda e: getattr(e, name)(*a, **kw), r=r, w=w)

    def dma(self, q, out, in_, r, w):
        self.S.add(q, lambda e: e.dma_start(out=out, in_=in_), r=r, w=w, dma=True)

    def carve(self, off_bytes, shape, dt):
        esz = 4 if dt in (F32, I32) else 2
        n = int(np.prod(shape[1:]))
        assert off_bytes % 4 == 0 and off_bytes + n * esz <= self.arena_bytes, (off_bytes, shape)
        a = self.arena[:, off_bytes // 2: off_bytes // 2 + n * esz // 2]
        if esz == 4:
            a = a.bitcast(dt)
        if len(shape) == 3:
            a = a.rearrange("p (a b) -> p a b", a=shape[1])
        elif len(shape) == 4:
            a = a.rearrange("p (a b c) -> p a b c", a=shape[1], b=shape[2])
        return a

    def build(self):
        nc = self.nc
        S = self.S
        B = self
        HC = host_consts()
        G128 = HC["g128"]
        x_d = B.dram_in("x", [SEQ, D])
        xT_d = B.dram_in("xT", [D, SEQ])
        pT_d = B.dram_in("pT", [2, 256, SEQ])
        pos_d = B.dram_in("pos", [128, 32], I32)
        wup_d = [B.dram_in("wup1", [2, 22, 128, 8 * 256]), B.dram_in("wup2", [2, 22, 128, 8 * 256])]
        wdn_d = [B.dram_in("wdn1", [2, 128, 22, 1024]), B.dram_in("wdn2", [2, 128, 22, 1024])]
        lng_d = B.dram_in("lng", [2, 3, 128, D])
        lnb_d = B.dram_in("lnb", [2, 3, 128, D])
        wpg_d = B.dram_in("wpg", [2, 128, 8, 1024])
        wpl_d = B.dram_in("wpl", [2, 128, 2, 1024])
        wtm_d = B.dram_in("wtm", [2, 128, 8, 2208])
        wg_d = B.dram_in("wg", [2, 8, 128, 8, 512])
        wbr_d = B.dram_in("wbr", [2, 128, 8, 1024])
        wout_d = B.dram_in("wout", [2, 128, 8, 1024])
        sguw_d = B.dram_in("sguw", [2, 128, 4 * 128])
        sgun_d = B.dram_in("sgun", [2, 2, 128, 256])
        sgub_d = B.dram_in("sgub", [2, 128, 4])
        poolw_d = B.dram_in("poolw", [2, 128, 2 * 128])
        poolsc_d = B.dram_in("poolsc", [2, 128, 2])
        wuq_d = B.dram_in("wuq", [2, 128, 2 * 384])
        gq_d = B.dram_in("gq", [2, 128, 2])
        wukv_d = B.dram_in("wukv", [2, 128, 512])
        gkv_d = B.dram_in("gkv", [2, 128, 1])
        ident_d = B.dram_in("ident", [128, 128])
        retdt_d = B.dram_in("ret_dt", [128, 4 * 128])
        retdec_d = B.dram_in("ret_dec", [128, 8])
        mask_d = B.dram_in("sgu_mask", [128, 128])
        invf_d = B.dram_in("invf", [128, 32])
        band_d = B.dram_in("pool_band", [128, 12 * 128])
        out_d = nc.dram_tensor("out", [SEQ, D], F32, kind="ExternalOutput").ap()
        xsp_d = nc.dram_tensor("xspill", [128, NT, D], F32, kind="Internal").ap()
        xl_d = nc.dram_tensor("xlayer", [2, 128, NT, D], F32, kind="Internal").ap()
        xtl_d = nc.dram_tensor("xtlayer", [2, 128, 8, HALF], BF16, kind="Internal").ap()
        kt_d = nc.dram_tensor("ktspill", [128, 4, HALF], BF16, kind="Internal").ap()
        vp_d = nc.dram_tensor("vpspill", [128, NT, 260], BF16, kind="Internal").ap()
        dbg_d = nc.dram_tensor("dbg", [HALF, D], F32, kind="ExternalOutput").ap() if DEBUG else None

        with ExitStack() as st:
            ec = st.enter_context

            def sb(name, shape, dt=F32):
                return ec(nc.sbuf_tensor(name, shape, dt))

            XT = sb("XT", [128, 8, HALF], BF16)
            self.identb = sb("identb", [128, 128], BF16)
            ST6 = [sb("ST6_%d" % i, [128, 2, 6]) for i in range(2)]
            MV = [sb("MV%d" % i, [128, 4]) for i in range(2)]
            COS = sb("COS", [128, 32, 32])
            SIN = sb("SIN", [128, 32, 32])
            DT = sb("DT", [128, 4, 128], BF16)
            DEC = sb("DEC", [128, 8])
            MASK = sb("MASK", [128, 128], BF16)
            WST = sb("WST", [128, 4, 128], BF16)
            SGN = sb("SGN", [128, 2, 256])
            SGB = sb("SGB", [128, 4])
            BAND = sb("BAND", [128, 12, 128], BF16)
            WPOOL = sb("WPOOL", [128, 2, 128], BF16)
            PSC = sb("PSC", [128, 2])
            WUQ = sb("WUQ", [128, 2, 384], BF16)
            GQ = sb("GQ", [128, 2])
            WUKV = sb("WUKV", [128, 512], BF16)
            GKV = sb("GKV", [128, 1])
            STATE = sb("STATE", [64, 4, 64])
            STATEB = sb("STATEB", [64, 4, 64], BF16)
            ZB = [sb("ZB%d" % i, [128, 256], BF16) for i in range(2)]
            SM4 = sb("SM4", [128, 16])
            self.arena_bytes = 152 * 1024
            self.arena = sb("arena", [128, self.arena_bytes // 2], BF16)
            X = B.carve(0, [128, NT, D], F32)
            A0 = 65536
            pf = [ec(nc.psum_tensor("pf%d" % i, [128, 512], F32)) for i in range(7)]
            pbt = ec(nc.psum_tensor("pbt", [128, 1024], BF16))
            PF = lambda i: ("pf", i)

            B.dma("pool", self.identb[:], ident_d, r=[], w=["ident"])
            if not os.environ.get("MK_NOROPE"):
                B.dma("pool", MASK[:], mask_d, r=[], w=["mask"])
                B.dma("pool", BAND[:].rearrange("p a b -> p (a b)"), band_d, r=[], w=["band"])
                B.dma("pool", DT[:].rearrange("p a b -> p (a b)"), retdt_d, r=[], w=["dt"])
                B.dma("sp", DEC[:], retdec_d, r=[], w=["dec"])
                POSI = B.carve(A0, [128, 32], I32)
                POSF = B.carve(A0 + 128, [128, 32], F32)
                INVF = B.carve(A0 + 256, [128, 32], F32)
                ANG = B.carve(A0 + 4096, [128, 32, 32], F32)
                T_A = B.carve(A0 + 8192, [128, 32, 32], F32)
                T_I = B.carve(A0 + 12288, [128, 32, 32], I32)
                T_F = B.carve(A0 + 16384, [128, 32, 32], F32)
                B.dma("sp", POSI, pos_d, r=[], w=["posi"])
                B.dma("sp", INVF, invf_d, r=[], w=["invf"])
                B.op("dve", "tensor_copy", ["posi"], ["posf"], out=POSF, in_=POSI)
                B.op("dve", "tensor_tensor", ["posf", "invf"], ["ang"], out=ANG,
                     in0=_bc(POSF, [(1, 32), (0, 32)]), in1=_bc(INVF, [(0, 32), (1, 32)]), op=ALU.mult)
                C1 = 6.28125
                C2 = 2.0 * math.pi - 6.28125
                for tab, shift in ((SIN, 0.0), (COS, math.pi / 2)):
                    B.op("dve", "tensor_scalar_add", ["ang"], ["ta"], out=T_A, in0=ANG, scalar1=shift)
                    B.op("dve", "tensor_scalar_mul", ["ta"], ["tf"], out=T_F, in0=T_A, scalar1=1.0 / (2 * math.pi))
                    B.op("dve", "tensor_copy", ["tf"], ["ti"], out=T_I, in_=T_F)
                    B.op("dve", "tensor_copy", ["ti"], ["tf"], out=T_F, in_=T_I)
                    B.op("dve", "scalar_tensor_tensor", ["tf", "ta"], ["ta"], out=T_A, in0=T_F, scalar=-C1, in1=T_A,
                         op0=ALU.mult, op1=ALU.add)
                    B.op("dve", "scalar_tensor_tensor", ["tf", "ta"], ["ta"], out=T_A, in0=T_F, scalar=-C2, in1=T_A,
                         op0=ALU.mult, op1=ALU.add)
                    B.op("dve", "tensor_scalar", ["ta"], ["ta"], out=T_A, in0=T_A, scalar1=3.141592, scalar2=-3.141592,
                         op0=ALU.min, op1=ALU.max)
                    B.act(tab[:], T_A, ACT.Sin, ["ta"], ["rope"])
            S.barrier()

            def set_ln(off):
                self.LG = B.carve(off, [128, D], F32)
                self.LB = B.carve(off + 4096, [128, D], F32)
                self.XN = [B.carve(off + 8192 + i * 4096, [128, D], F32) for i in range(2)]
                self.XB = [B.carve(off + 16384 + i * 2048, [128, D], BF16) for i in range(2)]

            def layer_norm_tile(L, which, tt, seg_h, final):
                i2 = tt % 2
                LG, LB, XN, XB = self.LG, self.LB, self.XN, self.XB
                xr = ("X", tt)
                for c2 in range(2):
                    B.op("dve", "bn_stats", [xr], [("st6", i2)], out=ST6[i2][:, c2, :], in_=X[:, tt, c2 * 512:(c2 + 1) * 512])
                B.op("dve", "bn_aggr", [("st6", i2)], [("mv", i2)], out=MV[i2][:, 0:2],
                     in_=ST6[i2][:].rearrange("p a b -> p (a b)"))
                B.op("dve", "tensor_scalar_add", [("mv", i2)], [("mv", i2)], out=MV[i2][:, 1:2], in0=MV[i2][:, 1:2], scalar1=LN_EPS)
                B.act(MV[i2][:, 2:3], MV[i2][:, 1:2], ACT.Sqrt, [("mv", i2)], [("mv", i2)])
                B.op("dve", "reciprocal", [("mv", i2)], [("mv", i2)], out=MV[i2][:, 2:3], in_=MV[i2][:, 2:3])
                B.op("dve", "scalar_tensor_tensor", [("mv", i2)], [("mv", i2)], out=MV[i2][:, 3:4], in0=MV[i2][:, 0:1],
                     scalar=-1.0, in1=MV[i2][:, 2:3], op0=ALU.mult, op1=ALU.mult)
                B.act(XN[i2], X[:, tt, :], ACT.Identity, [xr, ("mv", i2)], [("xn", i2)],
                      scale=MV[i2][:, 2:3], bias=MV[i2][:, 3:4])
                B.op("dve", "tensor_tensor", [("xn", i2), "LG"], [("xn", i2)], out=XN[i2], in0=XN[i2], in1=LG, op=ALU.mult)
                B.op("pool", "tensor_tensor", [("xn", i2), "LB"], [("xn", i2)], out=XN[i2], in0=XN[i2], in1=LB, op=ALU.add)
                if final:
                    r0 = seg_h * HALF + tt * 128
                    B.dma("sp", out_d[r0:r0 + 128, :], XN[i2], r=[("xn", i2)], w=["out"])
                    return
                B.op("pool", "tensor_scalar_mul", [("xn", i2)], [xr], out=X[:, tt, :], in0=XN[i2], scalar1=ALPHA)
                B.op("dve", "tensor_copy", [("xn", i2)], [("xb", i2)], out=XB[i2], in_=XN[i2])
                for kc in range(8):
                    B.tr(pbt[:, kc * 128:(kc + 1) * 128], XB[i2][:, kc * 128:(kc + 1) * 128], [("xb", i2)], ["pbt"])
                B.op("act", "copy", ["pbt"], [("XT", tt)], out=XT[:, :, tt * 128:(tt + 1) * 128],
                     in_=pbt[:].rearrange("p (k t) -> p k t", k=8))

            def load_ln(L, which):
                B.dma("sp", self.LG, lng_d[L, which], r=[], w=["LG"])
                B.dma("sp", self.LB, lnb_d[L, which], r=[], w=["LB"])

            def ffn(L, which, seg_h, final):
                lnw = 0 if which == 0 else 2
                S.barrier()
                HT = B.carve(A0, [128, 8, HALF], BF16)
                WU = [B.carve(A0 + 32768 + i * 4096, [128, 8, 256], BF16) for i in range(3)]
                WD = B.carve(A0 + 32768 + 12288, [128, 8, 1024], BF16)
                SIL = [B.carve(A0 + 32768 + 12288 + 16384 + i * 2048, [128, 512], F32) for i in range(2)]
                set_ln(131072)
                load_ln(L, lnw)
                ci = 0
                for si, (j0, nj) in enumerate(SPLITS):
                    last = si == len(SPLITS) - 1
                    for jc in range(nj):
                        j = j0 + jc
                        wb = WU[ci % 3]
                        wr = ("WU", ci % 3)
                        ci += 1
                        B.dma("pool", wb.rearrange("p a b -> p (a b)"), wup_d[which][L, j], r=[], w=[wr])
                        if jc == min(2, nj - 1):
                            for q in range(nj):
                                B.dma("pool", WD[:, q, :], wdn_d[which][L, :, j0 + q, :], r=[], w=[("WD", q)])
                        for tg in range(4):
                            pa = pf[(tg % 2) * 2]
                            pb_ = pf[(tg % 2) * 2 + 1]
                            ra = PF((tg % 2) * 2)
                            rb = PF((tg % 2) * 2 + 1)
                            tsl = slice(tg * 512, (tg + 1) * 512)
                            xr = [("XT", t) for t in range(tg * 4, tg * 4 + 4)]
                            for kc in range(8):
                                B.mm(pa[:], wb[:, kc, 0:128], XT[:, kc, tsl], kc == 0, kc == 7, r=[wr] + xr, w=[ra])
                            for kc in range(8):
                                B.mm(pb_[:], wb[:, kc, 128:256], XT[:, kc, tsl], kc == 0, kc == 7, r=[wr] + xr, w=[rb])
                            sl = SIL[tg % 2]
                            B.act(sl[:], pa[:], ACT.Silu, [ra], [("sil", tg % 2)])
                            B.op("dve", "tensor_tensor", [("sil", tg % 2), rb], [("HT", jc, tg)],
                                 out=HT[:, jc, tsl], in0=sl[:], in1=pb_[:], op=ALU.mult)
                    for tt in range(NT):
                        for nh in range(2):
                            po = pf[4 + nh]
                            rp = PF(4 + nh)
                            for jc in range(nj):
                                B.mm(po[:], HT[:, jc, tt * 128:(tt + 1) * 128], WD[:, jc, nh * 512:(nh + 1) * 512],
                                     jc == 0, jc == nj - 1, r=[("HT", jc, tt // 4), ("WD", jc)], w=[rp])
                            B.op("dve", "scalar_tensor_tensor", [rp, ("X", tt)], [("X", tt)],
                                 out=X[:, tt, nh * 512:(nh + 1) * 512], in0=po[:], scalar=0.5,
                                 in1=X[:, tt, nh * 512:(nh + 1) * 512], op0=ALU.mult, op1=ALU.add)
                        if last:
                            layer_norm_tile(L, lnw, tt, seg_h, final)

            def ple(L, h):
                S.barrier()
                WPG = B.carve(A0, [128, 8, 1024], BF16)
                WPL = B.carve(A0 + 16384, [128, 2, 1024], BF16)
                PT = B.carve(A0 + 20480, [128, 2, HALF], BF16)
                SG = [B.carve(A0 + 28672 + i * 2048, [128, 512], F32) for i in range(3)]
                for kc in range(8):
                    B.dma("pool", WPG[:, kc, :], wpg_d[L, :, kc, :], r=[], w=[("wpg", kc)])
                for kc in range(2):
                    B.dma("pool", WPL[:, kc, :], wpl_d[L, :, kc, :], r=[], w=[("wpl", kc)])
                    B.dma("pool", PT[:, kc, :], pT_d[L, kc * 128:(kc + 1) * 128, h * HALF:(h + 1) * HALF], r=[], w=[("pt", kc)])
                for tt in range(NT):
                    for nh in range(2):
                        k2 = (tt * 2 + nh) % 3
                        pg, pp = pf[k2 * 2], pf[k2 * 2 + 1]
                        for kc in range(8):
                            B.mm(pg[:], XT[:, kc, tt * 128:(tt + 1) * 128], WPG[:, kc, nh * 512:(nh + 1) * 512], kc == 0, kc == 7,
                                 r=[("XT", tt), ("wpg", kc)], w=[PF(k2 * 2)])
                        for kc in range(2):
                            B.mm(pp[:], PT[:, kc, tt * 128:(tt + 1) * 128], WPL[:, kc, nh * 512:(nh + 1) * 512], kc == 0, kc == 1,
                                 r=[("pt", kc), ("wpl", kc)], w=[PF(k2 * 2 + 1)])
                        B.act(SG[k2][:], pg[:], ACT.Sigmoid, [PF(k2 * 2)], [("sg", k2)])
                        B.op("dve", "tensor_tensor", [("sg", k2), PF(k2 * 2 + 1)], [("sg", k2)], out=SG[k2][:], in0=SG[k2][:], in1=pp[:], op=ALU.mult)
                        B.op("pool", "tensor_tensor", [("sg", k2), ("X", tt)], [("X", tt)], out=X[:, tt, nh * 512:(nh + 1) * 512],
                             in0=X[:, tt, nh * 512:(nh + 1) * 512], in1=SG[k2][:], op=ALU.add)

            def load_layer_consts(L):
                S.barrier()
                TMPW = B.carve(A0, [128, 4, 128], F32)
                B.dma("sp", TMPW.rearrange("p a b -> p (a b)"), sguw_d[L], r=[], w=["tmpw"])
                B.op("dve", "tensor_tensor", ["tmpw", "mask"], ["wst"], out=WST[:], in0=TMPW,
                     in1=_bc(MASK[:], [(0, 4), (1, 128)]), op=ALU.mult)
                B.dma("sp", SGN[:, 0, :], sgun_d[L, 0], r=[], w=["sgn0"])
                B.dma("sp", SGN[:, 1, :], sgun_d[L, 1], r=[], w=["sgn1"])
                B.dma("sp", SGB[:], sgub_d[L], r=[], w=["sgb"])
                B.dma("pool", WPOOL[:].rearrange("p a b -> p (a b)"), poolw_d[L], r=[], w=["wpool"])
                B.dma("sp", PSC[:], poolsc_d[L], r=[], w=["psc"])
                TQ = B.carve(A0 + 4096, [128, 2, 384], F32)
                TKV = B.carve(A0 + 8192, [128, 512], F32)
                B.dma("sp", TQ.rearrange("p a b -> p (a b)"), wuq_d[L], r=[], w=["tq"])
                B.dma("sp", GQ[:], gq_d[L], r=[], w=["gq"])
                B.dma("sp", TKV, wukv_d[L], r=[], w=["tkv"])
                B.dma("sp", GKV[:], gkv_d[L], r=[], w=["gkv"])
                for kc in range(2):
                    B.op("dve", "tensor_scalar", ["tq", "gq"], [("wuq", kc)], out=WUQ[:, kc, :], in0=TQ[:, kc, :],
                         scalar1=GQ[:, kc:kc + 1], scalar2=96 ** -0.5, op0=ALU.mult, op1=ALU.mult)
                B.op("dve", "tensor_scalar", ["tkv", "gkv"], ["wukv"], out=WUKV[:], in0=TKV, scalar1=GKV[:, 0:1], scalar2=None, op0=ALU.mult)
                B.op("pool", "memset", [], ["state"], STATE[:], 0.0)
                B.op("pool", "memset", [], ["stateb"], STATEB[:], 0.0)
                S.barrier()

            def mixer(L, h):
                S.barrier()
                for tt in range(NT):
                    B.dma("sp", xsp_d[:, tt, :], X[:, tt, :], r=[], w=[("xsp", tt)])
                S.barrier()
                YS = B.carve(0, [128, 4, 2, HALF], BF16)
                KT = B.carve(32768, [128, 4, SEQ], BF16)
                WG = [B.carve(32768 + i * 8192, [128, 8, 512], BF16) for i in range(2)]
                WBRn = [B.carve(32768 + 16384 + i * 2048, [128, 8, 128], BF16) for i in range(2)]
                MS = [B.carve(32768 + 20480 + i * 2048, [128, 512], F32) for i in range(3)]
                VP = B.carve(65536, [128, 32, 260], BF16)
                MERG = B.carve(65536, [128, 8, HALF], BF16)
                WO = B.carve(98304, [128, 8, 1024], BF16)
                WTM = B.carve(65536 + 16640, [128, 8, 2208], BF16)
                o = 65536 + 16640 + 35328
                QT = B.carve(o, [128, 4, 512], BF16); o += 4096
                ET = [B.carve(o + i * 1024, [128, 512], BF16) for i in range(3)]; o += 3072
                t0 = o

                def tmp(shape, dt):
                    nonlocal o
                    a = B.carve(o, shape, dt)
                    o += int(np.prod(shape[1:])) * (4 if dt in (F32, I32) else 2)
                    o = (o + 31) // 32 * 32
                    return a

                GU = tmp([128, 256], F32); GV = tmp([128, 256], F32); VBs = tmp([128, 256], BF16); Y0 = tmp([128, 256], BF16)
                QKR = tmp([128, 8, 64], F32); RT = [tmp([128, 8, 32], F32) for _ in range(4)]
                QKB = tmp([128, 8, 64], BF16); QSB = tmp([128, 4, 64], BF16); KDB = tmp([128, 4, 64], BF16)
                VBr = tmp([128, 256], BF16); TRS = tmp([128, 12, 128], BF16); SMK = tmp([128, 4, 128], BF16)
                YN = tmp([128, 256], F32); SGT = tmp([128, 256], F32); Y1 = tmp([128, 256], BF16)
                GNS = tmp([128, 4, 6], F32); GNM = tmp([128, 4, 2], F32); GNR = tmp([128, 4], F32)
                PLT = tmp([128, 2, 128], BF16)
                SQJ = tmp([128, 256], F32); CQN = tmp([128, 256], BF16); CKVN = tmp([128, 128], BF16)
                CT = tmp([128, 3, 128], BF16); QF = tmp([128, 4, 96], BF16); KF = tmp([128, 4, 96], BF16)
                MR = [tmp([128, 4, 16], F32) for _ in range(4)]; KR_ = [tmp([128, 16], F32) for _ in range(4)]
                KPE = tmp([128, 32], BF16); RD = tmp([128, 4], F32); Y3 = tmp([128, 4, 64], BF16)
                assert o <= self.arena_bytes, o

                for kc in range(8):
                    B.dma("pool", WTM[:, kc, 0:2048], wtm_d[L, :, kc, 0:2048], r=[], w=[("wtm", kc, 0)])
                    B.dma("pool", WTM[:, kc, 2048:2208], wtm_d[L, :, kc, 2048:2208], r=[], w=[("wtm", kc, 1)])
                B.op("pool", "memset", [], ["vp1"], VP[:, h * NT:(h + 1) * NT, :], 1.0)
                if h == 1:
                    for hd in range(4):
                        B.dma("sp", KT[0:96, hd, 0:HALF], kt_d[0:96, hd, :], r=[], w=[("ktl", hd)])
                    B.dma("sp", VP[:, 0:NT, :], vp_d, r=[], w=["vpl"])
                S.barrier()
                RW = lambda blk: [("wtm", kc_, i_) for kc_ in range(8) for i_ in range(2)]
                BC0 = [0, 512, 1024, 1536, 1792]

                def proj_tm(tt, blk, bank, ncols=512):
                    for kc in range(8):
                        B.mm(pf[bank][:, 0:ncols], XT[:, kc, tt * 128:(tt + 1) * 128], WTM[:, kc, BC0[blk]:BC0[blk] + ncols], kc == 0, kc == 7,
                             r=[("XT", tt)] + RW(blk), w=[PF(bank)])

                def rstd_of(var_ap, eps, res):
                    B.op("dve", "tensor_scalar_add", [res], [res], out=var_ap, in0=var_ap, scalar1=eps)
                    B.act(var_ap, var_ap, ACT.Sqrt, [res], [res])
                    B.op("dve", "reciprocal", [res], [res], out=var_ap, in_=var_ap)

                def to_fm(src_bf, dst, r, w, col0=0):
                    n = dst.shape[1]
                    for c in range(n):
                        B.tr(pbt[:, col0 + c * 128: col0 + (c + 1) * 128], src_bf[:, c * 128:(c + 1) * 128], r, ["pbt"])
                    B.op("act", "copy", ["pbt"], w, out=dst, in_=pbt[:, col0:col0 + n * 128].rearrange("p (k t) -> p k t", k=n))

                SKIP = os.environ.get("MK_SKIP", "").split(",")
                STOPT = int(os.environ.get("MK_STOPT", "99"))
                for tt in range(NT):
                    if tt > STOPT:
                        break
                    gt = h * NT + tt
                    tq = tt % 4
                    tsl = slice(tt * 128, (tt + 1) * 128)
                    cosr = _bc(COS[:, gt, :], [(0, 8), (1, 32)])
                    sinr = _bc(SIN[:, gt, :], [(0, 8), (1, 32)])
                    if 'sgu' not in SKIP:
                        proj_tm(tt, 0, 0)
                        B.act(GU[:], pf[0][:, 0:256], ACT.Gelu, [PF(0)], ["gu"])
                        B.act(GV[:], pf[0][:, 256:512], ACT.Gelu, [PF(0)], ["gv"])
                        B.op("dve", "bn_stats", ["gv"], ["gns"], out=GNS[:, 0, :], in_=GV[:])
                        B.op("dve", "bn_aggr", ["gns"], ["gnm"], out=GNM[:, 0, :], in_=GNS[:, 0, :])
                        rstd_of(GNM[:, 0, 1:2], LN_EPS, "gnm")
                        B.op("dve", "tensor_scalar", ["gv", "gnm"], ["gv"], out=GV[:], in0=GV[:], scalar1=GNM[:, 0, 0:1],
                             scalar2=GNM[:, 0, 1:2], op0=ALU.subtract, op1=ALU.mult)
                        B.op("pool", "tensor_tensor", ["gv", "sgn0"], ["gv"], out=GV[:], in0=GV[:], in1=SGN[:, 0, :], op=ALU.mult)
                        B.op("dve", "tensor_tensor", ["gv", "sgn1"], ["vbs"], out=VBs[:], in0=GV[:], in1=SGN[:, 1, :], op=ALU.add)
                        for g in range(4):
                            B.mm(pf[1][:, g * 64:(g + 1) * 64], WST[:, g, :], VBs[:, g * 64:(g + 1) * 64], True, True,
                                 r=["vbs", "wst"], w=[PF(1)])
                        for g in range(4):
                            B.op("dve", "scalar_tensor_tensor", [PF(1), "gu", "sgb"], ["y0"], out=Y0[:, g * 64:(g + 1) * 64],
                                 in0=pf[1][:, g * 64:(g + 1) * 64], scalar=SGB[:, g:g + 1], in1=GU[:, g * 64:(g + 1) * 64],
                                 op0=ALU.add, op1=ALU.mult)
                        to_fm(Y0, YS[:, 0, :, tsl], ["y0"], [("ys", 0, tt)])
                    if 'ret' not in SKIP:
                        proj_tm(tt, 1, 2)
                        proj_tm(tt, 2, 3)
                        qk = pf[2][:].rearrange("p (h two d) -> p h two d", h=8, two=2)
                        x1, x2 = qk[:, :, 0, :], qk[:, :, 1, :]
                        qkr = QKR[:].rearrange("p h (two d) -> p h two d", two=2)
                        B.op("dve", "tensor_tensor", [PF(2)], ["rt0"], out=RT[0][:], in0=x1, in1=cosr, op=ALU.mult)
                        B.op("dve", "tensor_tensor", [PF(2)], ["rt1"], out=RT[1][:], in0=x2, in1=sinr, op=ALU.mult)
                        B.op("dve", "tensor_tensor", [PF(2)], ["rt2"], out=RT[2][:], in0=x1, in1=sinr, op=ALU.mult)
                        B.op("dve", "tensor_tensor", [PF(2)], ["rt3"], out=RT[3][:], in0=x2, in1=cosr, op=ALU.mult)
                        B.op("dve", "tensor_tensor", ["rt0", "rt1"], ["qkr0"], out=qkr[:, :, 0, :], in0=RT[0][:], in1=RT[1][:], op=ALU.subtract)
                        B.op("dve", "tensor_tensor", ["rt2", "rt3"], ["qkr1"], out=qkr[:, :, 1, :], in0=RT[2][:], in1=RT[3][:], op=ALU.add)
                        B.op("dve", "tensor_copy", ["qkr0", "qkr1"], ["qkb"], out=QKB[:], in_=QKR[:])
                        for hd in range(4):
                            B.op("dve", "tensor_scalar", ["qkr0", "qkr1", "dec"], ["qsb"], out=QSB[:, hd, :], in0=QKR[:, hd, :],
                                 scalar1=DEC[:, hd:hd + 1], scalar2=None, op0=ALU.mult)
                            B.op("dve", "tensor_scalar", ["qkr0", "qkr1", "dec"], ["kdb"], out=KDB[:, hd, :], in0=QKR[:, 4 + hd, :],
                                 scalar1=DEC[:, 4 + hd:5 + hd], scalar2=None, op0=ALU.mult)
                        B.op("act", "copy", [PF(3)], ["vbr"], out=VBr[:], in_=pf[3][:, 0:256])
                        B.act(SGT[:], pf[3][:, 256:512], ACT.Silu, [PF(3)], ["sgt"])
                        if int(os.environ.get("MK_RETCUT", "9")) <= 1:
                            continue
                        for c in range(8):
                            B.tr(pbt[0:64, c * 128:(c + 1) * 128], QKB[:, c, :], ["qkb"], ["pbt"])
                        B.op("act", "copy", ["pbt"], ["trs"], out=TRS[0:64, 0:8, :], in_=pbt[0:64, :].rearrange("p (k t) -> p k t", k=8))
                        for c in range(4):
                            B.tr(pbt[0:64, c * 128:(c + 1) * 128], QSB[:, c, :], ["qsb"], ["pbt"])
                        B.op("act", "copy", ["pbt"], ["trs2"], out=TRS[0:64, 8:12, :], in_=pbt[0:64, 0:512].rearrange("p (k t) -> p k t", k=4))
                        if int(os.environ.get("MK_RETCUT", "9")) <= 2:
                            continue
                        for hd in range(4):
                            B.mm(pf[4][:, hd * 128:(hd + 1) * 128], TRS[0:64, 4 + hd, :], TRS[0:64, hd, :], True, True,
                                 r=["trs"], w=[PF(4)])
                        B.op("dve", "tensor_tensor", [PF(4), "dt"], ["smk"], out=SMK[:].rearrange("p a b -> p (a b)"), in0=pf[4][:],
                             in1=DT[:].rearrange("p a b -> p (a b)"), op=ALU.mult)
                        if int(os.environ.get("MK_RETCUT", "9")) <= 3:
                            continue
                        for hd in range(4):
                            B.mm(pf[5][:, hd * 64:(hd + 1) * 64], SMK[:, hd, :], VBr[:, hd * 64:(hd + 1) * 64], True, False,
                                 r=["smk", "vbr"], w=[PF(5)])
                            B.mm(pf[5][:, hd * 64:(hd + 1) * 64], TRS[0:64, 8 + hd, :], STATEB[:, hd, :], False, True,
                                 r=["trs2", "stateb"], w=[PF(5)])
                        if int(os.environ.get("MK_RETCUT", "9")) <= 4:
                            continue
                        for hd in range(4):
                            B.mm(pf[6][0:64, hd * 64:(hd + 1) * 64], KDB[:, hd, :], VBr[:, hd * 64:(hd + 1) * 64], True, True,
                                 r=["kdb", "vbr"], w=[PF(6)])
                        for hd in range(4):
                            B.op("dve", "scalar_tensor_tensor", [PF(6), "state"], ["state"], out=STATE[:, hd, :],
                                 in0=STATE[:, hd, :], scalar=G128[hd], in1=pf[6][0:64, hd * 64:(hd + 1) * 64],
                                 op0=ALU.mult, op1=ALU.add)
                        B.op("dve", "tensor_copy", ["state"], ["stateb"], out=STATEB[:], in_=STATE[:])
                        if int(os.environ.get("MK_RETCUT", "9")) <= 5:
                            continue
                        for hd in range(4):
                            B.op("dve", "bn_stats", [PF(5)], ["gns"], out=GNS[:, hd, :], in_=pf[5][:, hd * 64:(hd + 1) * 64])
                        for hd in range(4):
                            B.op("dve", "bn_aggr", ["gns"], ["gnm"], out=GNM[:, hd, :], in_=GNS[:, hd, :])
                        B.op("dve", "tensor_copy", ["gnm"], ["gnr"], out=GNR[:], in_=GNM[:, :, 1])
                        rstd_of(GNR[:], GN_EPS, "gnr")
                        for hd in range(4):
                            B.op("dve", "tensor_scalar", [PF(5), "gnm", "gnr"], ["yn"], out=YN[:, hd * 64:(hd + 1) * 64],
                                 in0=pf[5][:, hd * 64:(hd + 1) * 64], scalar1=GNM[:, hd, 0:1], scalar2=GNR[:, hd:hd + 1],
                                 op0=ALU.subtract, op1=ALU.mult)
                        B.op("dve", "tensor_tensor", ["yn", "sgt"], ["y1"], out=Y1[:], in0=YN[:], in1=SGT[:], op=ALU.mult)
                        to_fm(Y1, YS[:, 1, :, tsl], ["y1"], [("ys", 1, tt)])
                    if 'pool' not in SKIP:
                        proj_tm(tt, 3, 0, 256)
                        zc, zp = ZB[gt % 2], ZB[(gt + 1) % 2]
                        B.op("act", "copy", [PF(0)], [("zb", gt % 2)], out=zc[:], in_=pf[0][:, 0:256])
                        for c in range(2):
                            for g2 in range(2):
                                wi = c * 2 + g2
                                osl = pf[1][g2 * 64:(g2 + 1) * 64, c * 128:(c + 1) * 128]
                                kind = 2 if gt == 0 else 0
                                B.mm(osl, zc[:, wi * 64:(wi + 1) * 64], BAND[:, kind * 4 + wi, :], True, gt == 0,
                                     r=[("zb", gt % 2), "band"], w=[PF(1)], tile_position=(0, g2 * 64))
                                if gt > 0:
                                    B.mm(osl, zp[:, wi * 64:(wi + 1) * 64], BAND[:, 4 + wi, :], False, True,
                                         r=[("zb", (gt + 1) % 2), "band"], w=[PF(1)], tile_position=(0, g2 * 64))
                        B.op("act", "copy", [PF(1)], ["plt"], out=PLT[:], in_=pf[1][:, 0:256].rearrange("p (c t) -> p c t", c=2))
                        for c in range(2):
                            B.mm(pf[0][:, 256 + c * 128:256 + (c + 1) * 128], WPOOL[:, c, :], PLT[:, c, :], True, True, r=["plt", "wpool"], w=[PF(0)])
                            B.op("dve", "tensor_scalar", [PF(0), "psc"], [("ys", 2, tt)], out=YS[:, 2, c, tsl],
                                 in0=pf[0][:, 256 + c * 128:256 + (c + 1) * 128], scalar1=PSC[:, c:c + 1], scalar2=None, op0=ALU.mult)
                    if 'mla' not in SKIP:
                        proj_tm(tt, 4, 2, 416)
                        B.act(SQJ[:, 0:256], pf[2][:, 0:256], ACT.Square, [PF(2)], ["sm4"], accum_out=SM4[:, 0:1])
                        B.act(SQJ[:, 0:128], pf[2][:, 256:384], ACT.Square, [PF(2), "sm4"], ["sm4"], accum_out=SM4[:, 1:2])
                        B.op("dve", "tensor_scalar", ["sm4"], ["sm4"], out=SM4[:, 0:1], in0=SM4[:, 0:1], scalar1=1.0 / 256, scalar2=RMS_EPS, op0=ALU.mult, op1=ALU.add)
                        B.op("dve", "tensor_scalar", ["sm4"], ["sm4"], out=SM4[:, 1:2], in0=SM4[:, 1:2], scalar1=1.0 / 128, scalar2=RMS_EPS, op0=ALU.mult, op1=ALU.add)
                        B.act(SM4[:, 0:2], SM4[:, 0:2], ACT.Sqrt, ["sm4"], ["sm4"])
                        B.op("dve", "reciprocal", ["sm4"], ["sm4"], out=SM4[:, 0:2], in_=SM4[:, 0:2])
                        B.op("dve", "tensor_scalar", [PF(2), "sm4"], ["cqn"], out=CQN[:], in0=pf[2][:, 0:256], scalar1=SM4[:, 0:1], scalar2=None, op0=ALU.mult)
                        B.op("dve", "tensor_scalar", [PF(2), "sm4"], ["ckvn"], out=CKVN[:], in0=pf[2][:, 256:384], scalar1=SM4[:, 1:2], scalar2=None, op0=ALU.mult)
                        if int(os.environ.get("MK_MLACUT", "9")) <= 1:
                            continue
                        kr = pf[2][:, 384:416]
                        cm = _bc(COS[:, gt, :], [(2, 16)])
                        sm_ = _bc(SIN[:, gt, :], [(2, 16)])
                        B.op("dve", "tensor_tensor", [PF(2)], ["kr0"], out=KR_[0][:], in0=kr[:, 0:16], in1=cm, op=ALU.mult)
                        B.op("dve", "tensor_tensor", [PF(2)], ["kr1"], out=KR_[1][:], in0=kr[:, 16:32], in1=sm_, op=ALU.mult)
                        B.op("dve", "tensor_tensor", [PF(2)], ["kr2"], out=KR_[2][:], in0=kr[:, 0:16], in1=sm_, op=ALU.mult)
                        B.op("dve", "tensor_tensor", [PF(2)], ["kr3"], out=KR_[3][:], in0=kr[:, 16:32], in1=cm, op=ALU.mult)
                        B.op("dve", "tensor_tensor", ["kr0", "kr1"], ["kpe0"], out=KPE[:, 0:16], in0=KR_[0][:], in1=KR_[1][:], op=ALU.subtract)
                        B.op("dve", "tensor_tensor", ["kr2", "kr3"], ["kpe1"], out=KPE[:, 16:32], in0=KR_[2][:], in1=KR_[3][:], op=ALU.add)
                        if int(os.environ.get("MK_MLACUT", "9")) <= 2:
                            continue
                        for c in range(2):
                            B.tr(pbt[:, c * 128:(c + 1) * 128], CQN[:, c * 128:(c + 1) * 128], ["cqn"], ["pbt"])
                        B.tr(pbt[:, 256:384], CKVN[:], ["ckvn"], ["pbt"])
                        B.op("act", "copy", ["pbt"], ["ct"], out=CT[:], in_=pbt[:, 0:384].rearrange("p (k t) -> p k t", k=3))
                        for kc in range(2):
                            B.mm(pf[3][:, 0:384], CT[:, kc, :], WUQ[:, kc, :], kc == 0, kc == 1, r=["ct", ("wuq", 0), ("wuq", 1)], w=[PF(3)])
                        B.mm(pf[4][:], CT[:, 2, :], WUKV[:], True, True, r=["ct", "wukv"], w=[PF(4)])
                        if int(os.environ.get("MK_MLACUT", "9")) <= 3:
                            continue
                        q4 = pf[3][:, 0:384].rearrange("p (h d) -> p h d", h=4)
                        cm4 = _bc(COS[:, gt, :], [(0, 4), (2, 16)])
                        sm4_ = _bc(SIN[:, gt, :], [(0, 4), (2, 16)])
                        B.op("dve", "tensor_tensor", [PF(3)], ["mr0"], out=MR[0][:], in0=q4[:, :, 0:16], in1=cm4, op=ALU.mult)
                        B.op("dve", "tensor_tensor", [PF(3)], ["mr1"], out=MR[1][:], in0=q4[:, :, 16:32], in1=sm4_, op=ALU.mult)
                        B.op("dve", "tensor_tensor", [PF(3)], ["mr2"], out=MR[2][:], in0=q4[:, :, 0:16], in1=sm4_, op=ALU.mult)
                        B.op("dve", "tensor_tensor", [PF(3)], ["mr3"], out=MR[3][:], in0=q4[:, :, 16:32], in1=cm4, op=ALU.mult)
                        B.op("dve", "tensor_tensor", ["mr0", "mr1"], ["qf0"], out=QF[:, :, 0:16], in0=MR[0][:], in1=MR[1][:], op=ALU.subtract)
                        B.op("dve", "tensor_tensor", ["mr2", "mr3"], ["qf1"], out=QF[:, :, 16:32], in0=MR[2][:], in1=MR[3][:], op=ALU.add)
                        for hd in range(4):
                            B.op("act", "copy", [PF(3)], ["qf2"], out=QF[:, hd, 32:96], in_=pf[3][:, hd * 96 + 32:hd * 96 + 96])
                        for hd in range(4):
                            B.op("dve", "tensor_copy", ["kpe0", "kpe1"], ["kf0"], out=KF[:, hd, 0:32], in_=KPE[:])
                        for hd in range(4):
                            B.op("act", "copy", [PF(4)], ["kf1"], out=KF[:, hd, 32:96], in_=pf[4][:, hd * 64:(hd + 1) * 64])
                        for hd in range(4):
                            B.op("dve", "tensor_copy", [PF(4)], [("vp", gt)], out=VP[:, gt, hd * 65:hd * 65 + 64], in_=pf[4][:, 256 + hd * 64:256 + (hd + 1) * 64])
                        if int(os.environ.get("MK_MLACUT", "9")) <= 4:
                            continue
                        for hd in range(4):
                            B.tr(pbt[0:96, hd * 128:(hd + 1) * 128], QF[:, hd, :], ["qf0", "qf1", "qf2"], ["pbt"])
                        for hd in range(4):
                            B.tr(pbt[0:96, (4 + hd) * 128:(5 + hd) * 128], KF[:, hd, :], ["kf0", "kf1"], ["pbt"])
                        tq = tt % 4
                        B.op("act", "copy", ["pbt"], [("qt", tq)], out=QT[0:96, :, tq * 128:(tq + 1) * 128],
                             in_=pbt[0:96, 0:512].rearrange("p (k t) -> p k t", k=4))
                        B.op("act", "copy", ["pbt"], [("kt", gt)], out=KT[0:96, :, gt * 128:(gt + 1) * 128],
                             in_=pbt[0:96, 512:1024].rearrange("p (k t) -> p k t", k=4))
                    if 'attn' not in SKIP:
                        if tq == 3:
                            Qb = tt // 4
                            g0 = h * NT + Qb * 4
                            ob = [3, 4, 5, 6]
                            ei = 0
                            for hd in range(4):
                                for ki in range(g0 + 4):
                                    rr_ = ki - g0
                                    q0 = max(rr_, 0)
                                    ncol = (4 - q0) * 128
                                    sb_ = ei % 2
                                    et = ET[ei % 3]
                                    er = ("et", ei % 3)
                                    ei += 1
                                    B.mm(pf[sb_][:, 0:ncol], KT[0:96, hd, ki * 128:(ki + 1) * 128], QT[0:96, hd, q0 * 128:512], True, True,
                                         r=[("kt", ki), ("ktl", hd)] + [("qt", q) for q in range(q0, 4)], w=[PF(sb_)])
                                    B.act(et[:, 0:ncol], pf[sb_][:, 0:ncol], ACT.Exp, [PF(sb_)], [er])
                                    if rr_ >= 0:
                                        B.op("pool", "memset", [er], [er], et[64:128, 0:64], 0.0)
                                    for qs in range(q0, 4):
                                        B.mm(pf[ob[qs]][:, hd * 65:(hd + 1) * 65], et[:, (qs - q0) * 128:(qs - q0 + 1) * 128], VP[:, ki, hd * 65:(hd + 1) * 65],
                                             ki == 0, ki == g0 + qs, r=[er, ("vp", ki), "vp1", "vpl"], w=[PF(ob[qs])])
                            for qs in range(4):
                                t2 = Qb * 4 + qs
                                o4 = pf[ob[qs]][:, 0:260].rearrange("p (h e) -> p h e", h=4)
                                B.op("dve", "reciprocal", [PF(ob[qs])], ["rd"], out=RD[:], in_=o4[:, :, 64])
                                for hd in range(4):
                                    B.op("dve", "tensor_scalar", [PF(ob[qs]), "rd"], ["y3"], out=Y3[:, hd, :], in0=pf[ob[qs]][:, hd * 65:hd * 65 + 64],
                                         scalar1=RD[:, hd:hd + 1], scalar2=None, op0=ALU.mult)
                                to_fm(Y3[:].rearrange("p h e -> p (h e)"), YS[:, 3, :, t2 * 128:(t2 + 1) * 128], ["y3"], [("ys", 3, t2)])
                if STOPT < 99:
                    S.barrier()
                    return
                if h == 0:
                    S.barrier()
                    for hd in range(4):
                        B.dma("sp", kt_d[0:96, hd, :], KT[0:96, hd, 0:HALF], r=[], w=[("ktd", hd)])
                    B.dma("sp", vp_d, VP[:, 0:NT, :], r=[], w=["vpd"])
                S.barrier()
                gi = 0
                for n_ in range(8):
                    wg = WG[n_ % 2]
                    for kc in range(0, 8, 4):
                        B.dma("pool", wg[:, kc:kc + 4, :], wg_d[L, n_, :, kc:kc + 4, :], r=[], w=[("wg", n_ % 2, kc)])
                    wbn = WBRn[n_ % 2]
                    B.dma("pool", wbn, wbr_d[L, :, :, n_ * 128:(n_ + 1) * 128], r=[], w=[("wbr", n_ % 2)])
                    for tg in range(4):
                        tsl = slice(tg * 512, (tg + 1) * 512)
                        xr = [("XT", t) for t in range(tg * 4, tg * 4 + 4)]
                        ms = MS[(n_ * 4 + tg) % 2]
                        msr = ("ms", (n_ * 4 + tg) % 2)
                        for b in range(4):
                            k2 = gi % 2
                            gi += 1
                            pg, pp = pf[k2 * 2], pf[k2 * 2 + 1]
                            for kc in range(8):
                                B.mm(pg[:], wg[:, kc, b * 128:(b + 1) * 128], XT[:, kc, tsl], kc == 0, kc == 7,
                                     r=xr + [("wg", n_ % 2, 0), ("wg", n_ % 2, 4)], w=[PF(k2 * 2)])
                            for kc in range(2):
                                B.mm(pp[:], wbn[:, b * 2 + kc, :], YS[:, b, kc, tsl], kc == 0, kc == 1,
                                     r=[("wbr", n_ % 2)], w=[PF(k2 * 2 + 1)])
                            sg = MS[2]
                            B.act(sg[:], pg[:], ACT.Sigmoid, [PF(k2 * 2)], ["sgm"])
                            if b == 0:
                                B.op("dve", "tensor_tensor", ["sgm", PF(k2 * 2 + 1)], [msr], out=ms[:], in0=sg[:], in1=pp[:], op=ALU.mult)
                            else:
                                B.op("dve", "tensor_tensor", ["sgm", PF(k2 * 2 + 1)], ["sgm"], out=sg[:], in0=sg[:], in1=pp[:], op=ALU.mult)
                                if b < 3:
                                    B.op("pool", "tensor_tensor", ["sgm", msr], [msr], out=ms[:], in0=ms[:], in1=sg[:], op=ALU.add)
                                else:
                                    B.op("pool", "tensor_tensor", ["sgm", msr], [("merg", n_, tg)], out=MERG[:, n_, tsl], in0=ms[:], in1=sg[:], op=ALU.add)
                S.barrier()
                for kc in range(8):
                    B.dma("pool", WO[:, kc, :], wout_d[L, :, kc, :], r=[], w=[("wo", kc)])
                for tt in range(NT):
                    B.dma("sp", X[:, tt, :], xsp_d[:, tt, :], r=[], w=[("X", tt)])
                set_ln(114688)
                load_ln(L, 1)
                for tt in range(NT):
                    for nh in range(2):
                        po = pf[4 + nh]
                        for kc in range(8):
                            B.mm(po[:], MERG[:, kc, tt * 128:(tt + 1) * 128], WO[:, kc, nh * 512:(nh + 1) * 512], kc == 0, kc == 7,
                                 r=[("wo", kc)], w=[PF(4 + nh)])
                        B.op("dve", "tensor_tensor", [PF(4 + nh), ("X", tt)], [("X", tt)], out=X[:, tt, nh * 512:(nh + 1) * 512],
                             in0=X[:, tt, nh * 512:(nh + 1) * 512], in1=po[:], op=ALU.add)
                    layer_norm_tile(L, 1, tt, h, False)

            def dump_dbg():
                S.barrier()
                for tt in range(NT):
                    B.dma("sp", dbg_d[tt * 128:(tt + 1) * 128, :], X[:, tt, :], r=[], w=["dbg"])
                S.barrier()

            stages = DEBUG.split(",") if DEBUG else []
            for L in range(2):
                if not os.environ.get("MK_NOCONST"):
                    load_layer_consts(L)
                for h in range(2):
                    S.barrier()
                    if L == 0:
                        for kc in range(8):
                            B.dma("pool", XT[:, kc, :], xT_d[kc * 128:(kc + 1) * 128, h * HALF:(h + 1) * HALF], r=[], w=[("XTl", kc)])
                        for tt in range(NT):
                            r0 = h * HALF + tt * 128
                            B.dma("sp", X[:, tt, :], x_d[r0:r0 + 128, :], r=[], w=[("X", tt)])
                            B.op("pool", "tensor_scalar_mul", [("X", tt)], [("X", tt)], out=X[:, tt, :], in0=X[:, tt, :], scalar1=ALPHA)
                    else:
                        for tt in range(NT):
                            B.dma("sp", X[:, tt, :], xl_d[h, :, tt, :], r=[], w=[("X", tt)])
                        for kc in range(8):
                            B.dma("sp", XT[:, kc, :], xtl_d[h, :, kc, :], r=[], w=[("XTl", kc)])
                    S.barrier()
                    def stop_here(tag):
                        if stages and stages[-1] == tag % (L, h):
                            dump_dbg()
                            S.barrier()
                            S.emit(nc, st)
                            return True
                        return False

                    ffn(L, 0, h, False)
                    if stop_here("ffn1_%d%d"):
                        return nc
                    mixer(L, h)
                    if stop_here("mix_%d%d"):
                        return nc
                    ple(L, h)
                    if stop_here("ple_%d%d"):
                        return nc
                    ffn(L, 1, h, L == 1)
                    if stop_here("end_%d%d"):
                        return nc
                    if L == 0:
                        S.barrier()
                        for tt in range(NT):
                            B.dma("sp", xl_d[h, :, tt, :], X[:, tt, :], r=[], w=[("xl", tt)])
                        for kc in range(8):
                            B.dma("sp", xtl_d[h, :, kc, :], XT[:, kc, :], r=[], w=[("xtl", kc)])
            S.barrier()
            S.emit(nc, st)
        return nc


_CACHE = {}


def prep_inputs(inputs):
    f = lambda a: np.ascontiguousarray(np.asarray(a, dtype=np.float32))
    hc = host_consts()
    shared = {k: hc[k] for k in ("ident", "ret_dec", "sgu_mask", "invf", "pool_band")}
    shared["ret_dt"] = f(hc["ret_dt"].reshape(128, 512))
    g = lambda k: np.asarray(inputs[k])

    def up(W):
        W = np.asarray(W).reshape(2, 8, 128, 2, 22, 128)
        return f(W.transpose(0, 4, 2, 1, 3, 5).reshape(2, 22, 128, 8 * 256))

    def dn(W):
        return f(np.asarray(W).reshape(2, 22, 128, 1024).transpose(0, 2, 1, 3))

    def kp(W, K):
        W = np.asarray(W)
        return f(W.reshape(2, K, 128, W.shape[-1]).transpose(0, 2, 1, 3))

    shared["wup1"] = up(g("ffn1_up")); shared["wdn1"] = dn(g("ffn1_down"))
    shared["wup2"] = up(g("ffn2_up")); shared["wdn2"] = dn(g("ffn2_down"))
    bc = lambda a: f(np.broadcast_to(np.asarray(a)[:, :, None, :], (2, a.shape[1], 128, a.shape[2])))
    shared["lng"] = bc(np.stack([g("ln1_g"), g("ln2_g"), g("ln3_g")], axis=1))
    shared["lnb"] = bc(np.stack([g("ln1_b"), g("ln2_b"), g("ln3_b")], axis=1))
    shared["wpg"] = kp(g("w_ple_gate"), 8)
    shared["wpl"] = kp(g("w_ple"), 2)
    win = g("w_in")
    shared["wtm"] = f(win[:, :, 0:2208].reshape(2, 8, 128, 2208).transpose(0, 2, 1, 3))
    wgt = win[:, :, 2208:].reshape(2, 8, 128, 4, 8, 128)
    shared["wg"] = f(wgt.transpose(0, 4, 2, 1, 3, 5).reshape(2, 8, 128, 8, 512))
    wb = g("w_branch").reshape(2, 4, 2, 128, 1024)
    shared["wbr"] = f(wb.transpose(0, 3, 1, 2, 4).reshape(2, 128, 8, 1024))
    shared["wout"] = kp(g("w_out"), 8)
    shared["sguw"] = f(g("sgu_w").transpose(0, 3, 1, 2).reshape(2, 128, 512))
    shared["sgun"] = f(np.broadcast_to(np.stack([g("sgu_ln_g"), g("sgu_ln_b")], axis=1)[:, :, None, :], (2, 2, 128, 256)))
    shared["sgub"] = f(g("sgu_b").transpose(0, 2, 1))
    pw = g("pool_w")
    pbd = np.zeros((2, 128, 2, 128), np.float32)
    for c in range(2):
        for g2 in range(2):
            pbd[:, g2 * 64:(g2 + 1) * 64, c, g2 * 64:(g2 + 1) * 64] = pw[:, c * 2 + g2]
    shared["poolw"] = f(pbd.reshape(2, 128, 256))
    shared["poolsc"] = f(g("pool_scale").reshape(2, 2, 128).transpose(0, 2, 1))
    wq = g("mla_w_uq").reshape(2, 2, 128, 4, 96)
    wq = np.concatenate([wq[..., 64:96], wq[..., 0:64]], axis=-1)
    shared["wuq"] = f(wq.transpose(0, 2, 1, 3, 4).reshape(2, 128, 768))
    shared["gq"] = f(g("mla_q_norm").reshape(2, 2, 128).transpose(0, 2, 1))
    wkv = g("mla_w_ukv").reshape(2, 128, 4, 128)
    shared["wukv"] = f(np.concatenate([wkv[..., 0:64].reshape(2, 128, 256), wkv[..., 64:128].reshape(2, 128, 256)], axis=-1))
    shared["gkv"] = f(g("mla_kv_norm").reshape(2, 128, 1))
    maps = []
    x = np.asarray(inputs["x"]); p = np.asarray(inputs["p"]); pos = np.asarray(inputs["positions"])
    for c in range(NCORES):
        b = c % 4
        m = dict(shared)
        m["x"] = f(x[b]); m["xT"] = f(x[b].T); m["pT"] = f(p[:, b].transpose(0, 2, 1))
        m["pos"] = np.ascontiguousarray(pos[b].reshape(32, 128).T.astype(np.int32))
        maps.append(m)
    return maps


def get_program():
    if "nc" not in _CACHE:
        b = Builder()
        _CACHE["nc"] = b.build()
        _CACHE["din"] = b.din
        _CACHE["stats"] = getattr(b.S, "stats", None)
    return _CACHE["nc"], _CACHE["din"]


def kernel(**inputs):
    nc, din = get_program()
    maps = prep_inputs(inputs)
    maps = [{k: v for k, v in m.items() if k in din} for m in maps]
    res = run_bass_kernel_spmd(nc, maps, core_ids=list(range(NCORES)))
    _CACHE["last"] = res
    out = np.stack([np.asarray(res.results[b]["out"]) for b in range(4)], axis=0)
    return out.astype(np.float32)
```

```python
import os
import math
from contextlib import ExitStack

import numpy as np
import concourse.bass as bass
import concourse.mybir as mybir
from concourse.bass_utils import run_bass_kernel_spmd

ACT = mybir.ActivationFunctionType
ALU = mybir.AluOpType
F32 = mybir.dt.float32
BF16 = mybir.dt.bfloat16
I32 = mybir.dt.int32
AX = mybir.AxisListType

D = 1024
SEQ = 4096
HALF = 2048
NT = 16
DFF = 2816
ALPHA = 4 ** 0.25
LN_EPS = 1e-5
RMS_EPS = 1e-6
GN_EPS = 1e-5
SPLITS = [(0, 8), (8, 7), (15, 7)]
NCORES = int(os.environ.get("MK_NCORES", "4"))
DEBUG = os.environ.get("MK_DEBUG", "")


class Op:
    __slots__ = ("eng", "fn", "raw", "war", "dma", "needs_inc", "token", "idx")

    def __init__(self, eng, fn, dma, idx):
        self.eng = eng
        self.fn = fn
        self.raw = set()
        self.war = set()
        self.dma = dma
        self.needs_inc = False
        self.token = None
        self.idx = idx


COMPUTE = ("pe", "act", "dve", "pool")


class Sched:
    NDSEM = {"sp": 8, "pool": 8, "act": 2}

    def __init__(self):
        self.ops = []
        self.lastw = {}
        self.readers = {}
        self.last_of = {}
        self.open_dma = []

    def add(self, eng, fn, r=(), w=(), dma=False, after=()):
        idx = len(self.ops)
        op = Op(eng, fn, dma, idx)
        for res in r:
            lw = self.lastw.get(res)
            if lw is not None:
                op.raw.add(lw)
            if isinstance(res, tuple) and res[0] == "pf" or res == "pbt":
                rd = self.readers.get(res)
                if rd:
                    for k_, x in rd.items():
                        if k_ != eng and not isinstance(x, list):
                            op.raw.add(x)
        for res in w:
            lw = self.lastw.get(res)
            if lw is not None:
                op.war.add(lw)
            rd = self.readers.get(res)
            if rd:
                for x in rd.values():
                    if isinstance(x, list):
                        op.war.update(x)
                    else:
                        op.war.add(x)
        for a in after:
            op.raw.add(a)
        for res in w:
            self.lastw[res] = idx
            self.readers[res] = {}
        for res in r:
            if res in w:
                continue
            rd = self.readers.setdefault(res, {})
            if dma:
                rd.setdefault(("dma", eng), []).append(idx)
            else:
                rd[eng] = idx
        self.ops.append(op)
        if fn is not None:
            if dma:
                self.open_dma.append(idx)
            else:
                self.last_of[eng] = idx
        return idx

    def barrier(self):
        deps = list(self.last_of.values()) + list(self.open_dma)
        self.open_dma = []
        for e in ("pe", "act", "dve", "pool", "sp"):
            self.add(e, None, after=deps)
        self.lastw = {}
        self.readers = {}

    def emit(self, nc, stack):
        ops = self.ops
        sems = {e: stack.enter_context(nc.semaphore("s_" + e)) for e in COMPUTE}
        dsems = {
            q: [stack.enter_context(nc.semaphore("d_%s%d" % (q, i))) for i in range(n)]
            for q, n in self.NDSEM.items()
        }
        rr = {q: 0 for q in dsems}
        ncnt = {q: [0] * len(dsems[q]) for q in dsems}
        lastd = {q: [None] * len(dsems[q]) for q in dsems}
        for op in ops:
            if op.dma:
                q = op.eng
                k = rr[q] % len(dsems[q])
                rr[q] += 1
                ncnt[q][k] += 1
                op.token = (dsems[q][k], 16 * ncnt[q][k])
                if lastd[q][k] is not None:
                    op.raw.add(lastd[q][k])
                lastd[q][k] = op.idx
        for op in ops:
            deps = set()
            for d in op.raw:
                p = ops[d]
                if p.dma or op.dma or p.eng != op.eng or op.eng != "pe":
                    deps.add(d)
            for d in op.war:
                p = ops[d]
                if p.dma or op.dma or p.eng != op.eng or op.eng != "pe":
                    deps.add(d)
            deps.discard(op.idx)
            op.raw = deps
            for d in deps:
                ops[d].needs_inc = True
        cnt = {e: 0 for e in COMPUTE}
        for op in ops:
            if op.dma or op.fn is None:
                continue
            if op.needs_inc:
                cnt[op.eng] += 1
                op.token = (sems[op.eng], cnt[op.eng])
        self.stats = dict(cnt)
        self.stats["nops"] = len(ops)
        per_eng = {}
        for op in ops:
            per_eng.setdefault(op.eng, []).append(op)

        def emit_engine(ename, e):
            waited = {}
            for op in per_eng.get(ename, []):
                need = {}
                for d in op.raw:
                    tok = ops[d].token
                    if tok is None:
                        continue
                    s, v = tok
                    if waited.get(id(s), 0) >= v:
                        continue
                    if need.get(id(s), (None, 0))[1] < v:
                        need[id(s)] = (s, v)
                for s, v in need.values():
                    e.wait_ge(s, v)
                    waited[id(s)] = v
                if op.fn is None:
                    continue
                ins = op.fn(e)
                if op.dma:
                    ins.then_inc(op.token[0], 16)
                elif op.needs_inc:
                    ins.then_inc(op.token[0], 1)

        with nc.Block() as block:

            @block.tensor
            def _(e):
                emit_engine("pe", e)

            @block.scalar
            def _(e):
                emit_engine("act", e)

            @block.vector
            def _(e):
                emit_engine("dve", e)

            @block.gpsimd
            def _(e):
                emit_engine("pool", e)

            @block.sync
            def _(e):
                emit_engine("sp", e)


def host_consts():
    c = {}
    c["ident"] = np.eye(128, dtype=np.float32)
    lg = np.log1p(-np.exp2(-5.0 - np.arange(4, dtype=np.float64)))
    j = np.arange(128)[:, None]
    i = np.arange(128)[None, :]
    same = (j // 64) == (i // 64)
    dt = np.zeros((128, 4, 128), np.float64)
    for h in range(4):
        m = np.where(same, np.exp(lg[h] * np.abs(i - j)), np.where(j < i, np.exp(lg[h] * (i - j)), 0.0))
        dt[:, h, :] = m * (64 ** -0.5)
    c["ret_dt"] = dt.astype(np.float32)
    tl = np.arange(128)[:, None]
    dec = np.zeros((128, 8), np.float64)
    dec[:, 0:4] = np.exp(lg[None, :] * (tl + 1))
    dec[:, 4:8] = np.exp(lg[None, :] * (127 - tl)) * (64 ** -0.5)
    c["ret_dec"] = dec.astype(np.float32)
    c["g128"] = [float(np.exp(lg[h] * 128)) for h in range(4)]
    c["sgu_mask"] = (i >= j).astype(np.float32)
    invf = (10000.0 ** (-np.arange(0, 64, 2, dtype=np.float32) / 64)).astype(np.float32)
    c["invf"] = np.broadcast_to(invf[None, :], (128, 32)).copy()
    band = np.zeros((128, 3, 4, 128), np.float64)
    for wi, w in enumerate((2, 4, 8, 16)):
        dts = i - j
        band[:, 0, wi, :] = np.where((dts >= 0) & (dts < w), 1.0 / w, 0.0) - (dts == 0)
        band[:, 1, wi, :] = np.where((dts + 128 >= 0) & (dts + 128 < w), 1.0 / w, 0.0)
        cnt = np.minimum(i + 1, w)
        band[:, 2, wi, :] = np.where((dts >= 0) & (dts < w), 1.0 / cnt, 0.0) - (dts == 0)
    c["pool_band"] = band.astype(np.float32).reshape(128, 12 * 128)
    return c


def _bc(ap, dims):
    return bass.AP(ap.tensor, ap.offset, [list(ap.ap[0])] + [[s, n] for s, n in dims])


class Builder:
    def __init__(self):
        self.nc = bass.Bass("TRN2", target_bir_lowering=False)
        self.S = Sched()
        self.din = {}

    def dram_in(self, name, shape, dt=F32):
        t = self.nc.dram_tensor(name, list(shape), dt, kind="ExternalInput")
        self.din[name] = (tuple(shape), dt)
        return t.ap()

    def mm(self, out, lhsT, rhs, start, stop, r, w, **kw):
        self.S.add("pe", lambda e: e.matmul(out, lhsT=lhsT, rhs=rhs, start=start, stop=stop, **kw), r=r, w=w)

    def tr(self, out, in_, r, w):
        ident = self.identb
        n = in_.shape[0]
        self.S.add("pe", lambda e: e.transpose(out, in_, ident[0:n, 0:n]), r=list(r), w=w)

    def act(self, out, in_, func, r, w, **kw):
        self.S.add("act", lambda e: e.activation(out=out, in_=in_, func=func, **kw), r=r, w=w)

    def op(self, eng, name, r, w, *a, **kw):
        self.S.add(eng, lambda e: getattr(e, name)(*a, **kw), r=r, w=w)

    def dma(self, q, out, in_, r, w):
        self.S.add(q, lambda e: e.dma_start(out=out, in_=in_), r=r, w=w, dma=True)

    def carve(self, off_bytes, shape, dt):
        esz = 4 if dt in (F32, I32) else 2
        n = int(np.prod(shape[1:]))
        assert off_bytes % 4 == 0 and off_bytes + n * esz <= self.arena_bytes, (off_bytes, shape)
        a = self.arena[:, off_bytes // 2: off_bytes // 2 + n * esz // 2]
        if esz == 4:
            a = a.bitcast(dt)
        if len(shape) == 3:
            a = a.rearrange("p (a b) -> p a b", a=shape[1])
        elif len(shape) == 4:
            a = a.rearrange("p (a b c) -> p a b c", a=shape[1], b=shape[2])
        return a

    def build(self):
        nc = self.nc
        S = self.S
        B = self
        HC = host_consts()
        G128 = HC["g128"]
        x_d = B.dram_in("x", [SEQ, D])
        xT_d = B.dram_in("xT", [D, SEQ])
        pT_d = B.dram_in("pT", [2, 256, SEQ])
        pos_d = B.dram_in("pos", [128, 32], I32)
        wup_d = [B.dram_in("wup1", [2, 22, 128, 8 * 256]), B.dram_in("wup2", [2, 22, 128, 8 * 256])]
        wdn_d = [B.dram_in("wdn1", [2, 128, 22, 1024]), B.dram_in("wdn2", [2, 128, 22, 1024])]
        lng_d = B.dram_in("lng", [2, 3, 128, D])
        lnb_d = B.dram_in("lnb", [2, 3, 128, D])
        wpg_d = B.dram_in("wpg", [2, 128, 8, 1024])
        wpl_d = B.dram_in("wpl", [2, 128, 2, 1024])
        wtm_d = B.dram_in("wtm", [2, 128, 8, 2208])
        wg_d = B.dram_in("wg", [2, 8, 128, 8, 512])
        wbr_d = B.dram_in("wbr", [2, 128, 8, 1024])
        wout_d = B.dram_in("wout", [2, 128, 8, 1024])
        sguw_d = B.dram_in("sguw", [2, 128, 4 * 128])
        sgun_d = B.dram_in("sgun", [2, 2, 128, 256])
        sgub_d = B.dram_in("sgub", [2, 128, 4])
        poolw_d = B.dram_in("poolw", [2, 128, 2 * 128])
        poolsc_d = B.dram_in("poolsc", [2, 128, 2])
        wuq_d = B.dram_in("wuq", [2, 128, 2 * 384])
        gq_d = B.dram_in("gq", [2, 128, 2])
        wukv_d = B.dram_in("wukv", [2, 128, 512])
        gkv_d = B.dram_in("gkv", [2, 128, 1])
        ident_d = B.dram_in("ident", [128, 128])
        retdt_d = B.dram_in("ret_dt", [128, 4 * 128])
        retdec_d = B.dram_in("ret_dec", [128, 8])
        mask_d = B.dram_in("sgu_mask", [128, 128])
        invf_d = B.dram_in("invf", [128, 32])
        band_d = B.dram_in("pool_band", [128, 12 * 128])
        out_d = nc.dram_tensor("out", [SEQ, D], F32, kind="ExternalOutput").ap()
        xsp_d = nc.dram_tensor("xspill", [128, NT, D], F32, kind="Internal").ap()
        xl_d = nc.dram_tensor("xlayer", [2, 128, NT, D], F32, kind="Internal").ap()
        xtl_d = nc.dram_tensor("xtlayer", [2, 128, 8, HALF], BF16, kind="Internal").ap()
        kt_d = nc.dram_tensor("ktspill", [128, 4, HALF], BF16, kind="Internal").ap()
        vp_d = nc.dram_tensor("vpspill", [128, NT, 260], BF16, kind="Internal").ap()
        dbg_d = nc.dram_tensor("dbg", [HALF, D], F32, kind="ExternalOutput").ap() if DEBUG else None

        with ExitStack() as st:
            ec = st.enter_context

            def sb(name, shape, dt=F32):
                return ec(nc.sbuf_tensor(name, shape, dt))

            XT = sb("XT", [128, 8, HALF], BF16)
            self.identb = sb("identb", [128, 128], BF16)
            ST6 = [sb("ST6_%d" % i, [128, 2, 6]) for i in range(2)]
            MV = [sb("MV%d" % i, [128, 4]) for i in range(2)]
            COS = sb("COS", [128, 32, 32])
            SIN = sb("SIN", [128, 32, 32])
            DT = sb("DT", [128, 4, 128], BF16)
            DEC = sb("DEC", [128, 8])
            MASK = sb("MASK", [128, 128], BF16)
            WST = sb("WST", [128, 4, 128], BF16)
            SGN = sb("SGN", [128, 2, 256])
            SGB = sb("SGB", [128, 4])
            BAND = sb("BAND", [128, 12, 128], BF16)
            WPOOL = sb("WPOOL", [128, 2, 128], BF16)
            PSC = sb("PSC", [128, 2])
            WUQ = sb("WUQ", [128, 2, 384], BF16)
            GQ = sb("GQ", [128, 2])
            WUKV = sb("WUKV", [128, 512], BF16)
            GKV = sb("GKV", [128, 1])
            STATE = sb("STATE", [64, 4, 64])
            STATEB = sb("STATEB", [64, 4, 64], BF16)
            ZB = [sb("ZB%d" % i, [128, 256], BF16) for i in range(2)]
            SM4 = sb("SM4", [128, 16])
            self.arena_bytes = 152 * 1024
            self.arena = sb("arena", [128, self.arena_bytes // 2], BF16)
            X = B.carve(0, [128, NT, D], F32)
            A0 = 65536
            pf = [ec(nc.psum_tensor("pf%d" % i, [128, 512], F32)) for i in range(7)]
            pbt = ec(nc.psum_tensor("pbt", [128, 1024], BF16))
            PF = lambda i: ("pf", i)

            B.dma("pool", self.identb[:], ident_d, r=[], w=["ident"])
            if not os.environ.get("MK_NOROPE"):
                B.dma("pool", MASK[:], mask_d, r=[], w=["mask"])
                B.dma("pool", BAND[:].rearrange("p a b -> p (a b)"), band_d, r=[], w=["band"])
                B.dma("pool", DT[:].rearrange("p a b -> p (a b)"), retdt_d, r=[], w=["dt"])
                B.dma("sp", DEC[:], retdec_d, r=[], w=["dec"])
                POSI = B.carve(A0, [128, 32], I32)
                POSF = B.carve(A0 + 128, [128, 32], F32)
                INVF = B.carve(A0 + 256, [128, 32], F32)
                ANG = B.carve(A0 + 4096, [128, 32, 32], F32)
                T_A = B.carve(A0 + 8192, [128, 32, 32], F32)
                T_I = B.carve(A0 + 12288, [128, 32, 32], I32)
                T_F = B.carve(A0 + 16384, [128, 32, 32], F32)
                B.dma("sp", POSI, pos_d, r=[], w=["posi"])
                B.dma("sp", INVF, invf_d, r=[], w=["invf"])
                B.op("dve", "tensor_copy", ["posi"], ["posf"], out=POSF, in_=POSI)
                B.op("dve", "tensor_tensor", ["posf", "invf"], ["ang"], out=ANG,
                     in0=_bc(POSF, [(1, 32), (0, 32)]), in1=_bc(INVF, [(0, 32), (1, 32)]), op=ALU.mult)
                C1 = 6.28125
                C2 = 2.0 * math.pi - 6.28125
                for tab, shift in ((SIN, 0.0), (COS, math.pi / 2)):
                    B.op("dve", "tensor_scalar_add", ["ang"], ["ta"], out=T_A, in0=ANG, scalar1=shift)
                    B.op("dve", "tensor_scalar_mul", ["ta"], ["tf"], out=T_F, in0=T_A, scalar1=1.0 / (2 * math.pi))
                    B.op("dve", "tensor_copy", ["tf"], ["ti"], out=T_I, in_=T_F)
                    B.op("dve", "tensor_copy", ["ti"], ["tf"], out=T_F, in_=T_I)
                    B.op("dve", "scalar_tensor_tensor", ["tf", "ta"], ["ta"], out=T_A, in0=T_F, scalar=-C1, in1=T_A,
                         op0=ALU.mult, op1=ALU.add)
                    B.op("dve", "scalar_tensor_tensor", ["tf", "ta"], ["ta"], out=T_A, in0=T_F, scalar=-C2, in1=T_A,
                         op0=ALU.mult, op1=ALU.add)
                    B.op("dve", "tensor_scalar", ["ta"], ["ta"], out=T_A, in0=T_A, scalar1=3.141592, scalar2=-3.141592,
                         op0=ALU.min, op1=ALU.max)
                    B.act(tab[:], T_A, ACT.Sin, ["ta"], ["rope"])
            S.barrier()

            def set_ln(off):
                self.LG = B.carve(off, [128, D], F32)
                self.LB = B.carve(off + 4096, [128, D], F32)
                self.XN = [B.carve(off + 8192 + i * 4096, [128, D], F32) for i in range(2)]
                self.XB = [B.carve(off + 16384 + i * 2048, [128, D], BF16) for i in range(2)]

            def ln_s1(tt):
                i2 = tt % 2
                xr = ("X", tt)
                for c2 in range(2):
                    B.op("dve", "bn_stats", [xr], [("st6", i2)], out=ST6[i2][:, c2, :], in_=X[:, tt, c2 * 512:(c2 + 1) * 512])
                B.op("dve", "bn_aggr", [("st6", i2)], [("mv", i2)], out=MV[i2][:, 0:2],
                     in_=ST6[i2][:].rearrange("p a b -> p (a b)"))
                B.op("dve", "tensor_scalar_add", [("mv", i2)], [("mv", i2)], out=MV[i2][:, 1:2], in0=MV[i2][:, 1:2], scalar1=LN_EPS)
                B.act(MV[i2][:, 2:3], MV[i2][:, 1:2], ACT.Sqrt, [("mv", i2)], [("mv", i2)])
                B.op("dve", "reciprocal", [("mv", i2)], [("mv", i2)], out=MV[i2][:, 2:3], in_=MV[i2][:, 2:3])
                B.op("dve", "scalar_tensor_tensor", [("mv", i2)], [("mv", i2)], out=MV[i2][:, 3:4], in0=MV[i2][:, 0:1],
                     scalar=-1.0, in1=MV[i2][:, 2:3], op0=ALU.mult, op1=ALU.mult)

            def ln_s2(tt):
                i2 = tt % 2
                LG, LB, XN = self.LG, self.LB, self.XN
                B.act(XN[i2], X[:, tt, :], ACT.Identity, [("X", tt), ("mv", i2)], [("xn", i2)],
                      scale=MV[i2][:, 2:3], bias=MV[i2][:, 3:4])
                B.op("dve", "tensor_tensor", [("xn", i2), "LG"], [("xn", i2)], out=XN[i2], in0=XN[i2], in1=LG, op=ALU.mult)
                B.op("dve", "tensor_tensor", [("xn", i2), "LB"], [("xn", i2)], out=XN[i2], in0=XN[i2], in1=LB, op=ALU.add)

            def ln_s3(tt, seg_h, final):
                i2 = tt % 2
                XN, XB = self.XN, self.XB
                if final:
                    r0 = seg_h * HALF + tt * 128
                    B.dma("sp", out_d[r0:r0 + 128, :], XN[i2], r=[("xn", i2)], w=["out"])
                    return
                B.op("pool", "tensor_scalar_mul", [("xn", i2)], [("X", tt)], out=X[:, tt, :], in0=XN[i2], scalar1=ALPHA)
                B.op("dve", "tensor_copy", [("xn", i2)], [("xb", i2)], out=XB[i2], in_=XN[i2])
                for kc in range(8):
                    B.tr(pbt[:, kc * 128:(kc + 1) * 128], XB[i2][:, kc * 128:(kc + 1) * 128], [("xb", i2)], ["pbt"])
                B.op("act", "copy", ["pbt"], [("XT", tt)], out=XT[:, :, tt * 128:(tt + 1) * 128],
                     in_=pbt[:].rearrange("p (k t) -> p k t", k=8))

            def ln_pipe(tt, seg_h, final):
                if tt < NT:
                    ln_s1(tt)
                if 0 <= tt - 1 < NT:
                    ln_s2(tt - 1)
                if 0 <= tt - 2 < NT:
                    ln_s3(tt - 2, seg_h, final)

            def load_ln(L, which):
                B.dma("sp", self.LG, lng_d[L, which], r=[], w=["LG"])
                B.dma("sp", self.LB, lnb_d[L, which], r=[], w=["LB"])

            def ffn(L, which, seg_h, final):
                lnw = 0 if which == 0 else 2
                S.barrier()
                HT = B.carve(A0, [128, 8, HALF], BF16)
                WU = [B.carve(A0 + 32768 + i * 4096, [128, 8, 256], BF16) for i in range(3)]
                WD = B.carve(A0 + 32768 + 12288, [128, 8, 1024], BF16)
                SIL = [B.carve(A0 + 32768 + 12288 + 16384 + i * 2048, [128, 512], F32) for i in range(2)]
                set_ln(131072)
                load_ln(L, lnw)
                ci = 0
                for si, (j0, nj) in enumerate(SPLITS):
                    last = si == len(SPLITS) - 1
                    for jc in range(nj):
                        j = j0 + jc
                        wb = WU[ci % 3]
                        wr = ("WU", ci % 3)
                        ci += 1
                        B.dma("pool", wb.rearrange("p a b -> p (a b)"), wup_d[which][L, j], r=[], w=[wr])
                        if jc == min(2, nj - 1):
                            for q in range(nj):
                                B.dma("pool", WD[:, q, :], wdn_d[which][L, :, j0 + q, :], r=[], w=[("WD", q)])
                        for tg in range(4):
                            pa = pf[(tg % 2) * 2]
                            pb_ = pf[(tg % 2) * 2 + 1]
                            ra = PF((tg % 2) * 2)
                            rb = PF((tg % 2) * 2 + 1)
                            tsl = slice(tg * 512, (tg + 1) * 512)
                            xr = [("XT", t) for t in range(tg * 4, tg * 4 + 4)]
                            for kc in range(8):
                                B.mm(pa[:], wb[:, kc, 0:128], XT[:, kc, tsl], kc == 0, kc == 7, r=[wr] + xr, w=[ra])
                            for kc in range(8):
                                B.mm(pb_[:], wb[:, kc, 128:256], XT[:, kc, tsl], kc == 0, kc == 7, r=[wr] + xr, w=[rb])
                            sl = SIL[tg % 2]
                            B.act(sl[:], pa[:], ACT.Silu, [ra], [("sil", tg % 2)])
                            B.op("dve", "tensor_tensor", [("sil", tg % 2), rb], [("HT", jc, tg)],
                                 out=HT[:, jc, tsl], in0=sl[:], in1=pb_[:], op=ALU.mult)
                    for tt in range(NT):
                        for nh in range(2):
                            po = pf[4 + nh]
                            rp = PF(4 + nh)
                            for jc in range(nj):
                                B.mm(po[:], HT[:, jc, tt * 128:(tt + 1) * 128], WD[:, jc, nh * 512:(nh + 1) * 512],
                                     jc == 0, jc == nj - 1, r=[("HT", jc, tt // 4), ("WD", jc)], w=[rp])
                            B.op("dve", "scalar_tensor_tensor", [rp, ("X", tt)], [("X", tt)],
                                 out=X[:, tt, nh * 512:(nh + 1) * 512], in0=po[:], scalar=0.5,
                                 in1=X[:, tt, nh * 512:(nh + 1) * 512], op0=ALU.mult, op1=ALU.add)
                        if last:
                            ln_pipe(tt, seg_h, final)
                    if last:
                        ln_pipe(NT, seg_h, final)
                        ln_pipe(NT + 1, seg_h, final)

            def ple(L, h):
                S.barrier()
                WPG = B.carve(A0, [128, 8, 1024], BF16)
                WPL = B.carve(A0 + 16384, [128, 2, 1024], BF16)
                PT = B.carve(A0 + 20480, [128, 2, HALF], BF16)
                SG = [B.carve(A0 + 28672 + i * 2048, [128, 512], F32) for i in range(3)]
                for kc in range(8):
                    B.dma("pool", WPG[:, kc, :], wpg_d[L, :, kc, :], r=[], w=[("wpg", kc)])
                for kc in range(2):
                    B.dma("pool", WPL[:, kc, :], wpl_d[L, :, kc, :], r=[], w=[("wpl", kc)])
                    B.dma("pool", PT[:, kc, :], pT_d[L, kc * 128:(kc + 1) * 128, h * HALF:(h + 1) * HALF], r=[], w=[("pt", kc)])
                for tt in range(NT):
                    for nh in range(2):
                        k2 = (tt * 2 + nh) % 3
                        pg, pp = pf[k2 * 2], pf[k2 * 2 + 1]
                        for kc in range(8):
                            B.mm(pg[:], XT[:, kc, tt * 128:(tt + 1) * 128], WPG[:, kc, nh * 512:(nh + 1) * 512], kc == 0, kc == 7,
                                 r=[("XT", tt), ("wpg", kc)], w=[PF(k2 * 2)])
                        for kc in range(2):
                            B.mm(pp[:], PT[:, kc, tt * 128:(tt + 1) * 128], WPL[:, kc, nh * 512:(nh + 1) * 512], kc == 0, kc == 1,
                                 r=[("pt", kc), ("wpl", kc)], w=[PF(k2 * 2 + 1)])
                        B.act(SG[k2][:], pg[:], ACT.Sigmoid, [PF(k2 * 2)], [("sg", k2)])
                        B.op("dve", "tensor_tensor", [("sg", k2), PF(k2 * 2 + 1)], [("sg", k2)], out=SG[k2][:], in0=SG[k2][:], in1=pp[:], op=ALU.mult)
                        B.op("dve", "tensor_tensor", [("sg", k2), ("X", tt)], [("X", tt)], out=X[:, tt, nh * 512:(nh + 1) * 512],
                             in0=X[:, tt, nh * 512:(nh + 1) * 512], in1=SG[k2][:], op=ALU.add)

            def load_layer_consts(L):
                S.barrier()
                TMPW = B.carve(A0, [128, 4, 128], F32)
                B.dma("sp", TMPW.rearrange("p a b -> p (a b)"), sguw_d[L], r=[], w=["tmpw"])
                B.op("dve", "tensor_tensor", ["tmpw", "mask"], ["wst"], out=WST[:], in0=TMPW,
                     in1=_bc(MASK[:], [(0, 4), (1, 128)]), op=ALU.mult)
                B.dma("sp", SGN[:, 0, :], sgun_d[L, 0], r=[], w=["sgn0"])
                B.dma("sp", SGN[:, 1, :], sgun_d[L, 1], r=[], w=["sgn1"])
                B.dma("sp", SGB[:], sgub_d[L], r=[], w=["sgb"])
                B.dma("pool", WPOOL[:].rearrange("p a b -> p (a b)"), poolw_d[L], r=[], w=["wpool"])
                B.dma("sp", PSC[:], poolsc_d[L], r=[], w=["psc"])
                TQ = B.carve(A0 + 4096, [128, 2, 384], F32)
                TKV = B.carve(A0 + 8192, [128, 512], F32)
                B.dma("sp", TQ.rearrange("p a b -> p (a b)"), wuq_d[L], r=[], w=["tq"])
                B.dma("sp", GQ[:], gq_d[L], r=[], w=["gq"])
                B.dma("sp", TKV, wukv_d[L], r=[], w=["tkv"])
                B.dma("sp", GKV[:], gkv_d[L], r=[], w=["gkv"])
                for kc in range(2):
                    B.op("dve", "tensor_scalar", ["tq", "gq"], [("wuq", kc)], out=WUQ[:, kc, :], in0=TQ[:, kc, :],
                         scalar1=GQ[:, kc:kc + 1], scalar2=96 ** -0.5, op0=ALU.mult, op1=ALU.mult)
                B.op("dve", "tensor_scalar", ["tkv", "gkv"], ["wukv"], out=WUKV[:], in0=TKV, scalar1=GKV[:, 0:1], scalar2=None, op0=ALU.mult)
                B.op("pool", "memset", [], ["state"], STATE[:], 0.0)
                B.op("pool", "memset", [], ["stateb"], STATEB[:], 0.0)
                S.barrier()

            def mixer(L, h):
                S.barrier()
                for tt in range(NT):
                    B.dma("sp", xsp_d[:, tt, :], X[:, tt, :], r=[], w=[("xsp", tt)])
                S.barrier()
                YS = B.carve(0, [128, 4, 2, HALF], BF16)
                KT = B.carve(32768, [128, 4, SEQ], BF16)
                WG = [B.carve(32768 + i * 8192, [128, 8, 512], BF16) for i in range(2)]
                WBRn = [B.carve(32768 + 16384 + i * 2048, [128, 8, 128], BF16) for i in range(2)]
                MS = [B.carve(32768 + 20480 + i * 2048, [128, 512], F32) for i in range(3)]
                VP = B.carve(65536, [128, 32, 260], BF16)
                MERG = B.carve(65536, [128, 8, HALF], BF16)
                WO = B.carve(98304, [128, 8, 1024], BF16)
                WTM = B.carve(65536 + 16640, [128, 8, 2208], BF16)
                o = 65536 + 16640 + 35328
                QT = B.carve(o, [128, 4, 512], BF16); o += 4096
                ET = [B.carve(o + i * 1024, [128, 512], BF16) for i in range(3)]; o += 3072
                t0 = o

                def tmp(shape, dt):
                    nonlocal o
                    a = B.carve(o, shape, dt)
                    o += int(np.prod(shape[1:])) * (4 if dt in (F32, I32) else 2)
                    o = (o + 31) // 32 * 32
                    return a

                GU = tmp([128, 256], F32); GV = tmp([128, 256], F32); VBs = tmp([128, 256], BF16); Y0 = tmp([128, 256], BF16)
                QKR = tmp([128, 8, 64], F32); RT = [tmp([128, 8, 32], F32) for _ in range(4)]
                QKB = tmp([128, 8, 64], BF16); QSB = tmp([128, 4, 64], BF16); KDB = tmp([128, 4, 64], BF16)
                VBr = tmp([128, 256], BF16); TRS = tmp([128, 12, 128], BF16); SMK = tmp([128, 4, 128], BF16)
                YN = tmp([128, 256], F32); SGT = tmp([128, 256], F32); Y1 = tmp([128, 256], BF16)
                GNS = tmp([128, 4, 6], F32); GNM = tmp([128, 4, 2], F32); GNR = tmp([128, 4], F32)
                PLT = tmp([128, 2, 128], BF16)
                SQJ = tmp([128, 256], F32); CQN = tmp([128, 256], BF16); CKVN = tmp([128, 128], BF16)
                CT = tmp([128, 3, 128], BF16); QF = tmp([128, 4, 96], BF16); KF = tmp([128, 4, 96], BF16)
                MR = [tmp([128, 4, 16], F32) for _ in range(4)]; KR_ = [tmp([128, 16], F32) for _ in range(4)]
                KPE = tmp([128, 32], BF16); RD = tmp([128, 4], F32); Y3 = tmp([128, 4, 64], BF16)
                assert o <= self.arena_bytes, o

                for kc in range(8):
                    B.dma("pool", WTM[:, kc, 0:2048], wtm_d[L, :, kc, 0:2048], r=[], w=[("wtm", kc, 0)])
                    B.dma("pool", WTM[:, kc, 2048:2208], wtm_d[L, :, kc, 2048:2208], r=[], w=[("wtm", kc, 1)])
                B.op("pool", "memset", [], ["vp1"], VP[:, h * NT:(h + 1) * NT, :], 1.0)
                if h == 1:
                    for hd in range(4):
                        B.dma("sp", KT[0:96, hd, 0:HALF], kt_d[0:96, hd, :], r=[], w=[("ktl", hd)])
                    B.dma("sp", VP[:, 0:NT, :], vp_d, r=[], w=["vpl"])
                S.barrier()
                RW = lambda blk: [("wtm", kc_, i_) for kc_ in range(8) for i_ in range(2)]
                BC0 = [0, 512, 1024, 1536, 1792]

                def proj_tm(tt, blk, bank, ncols=512):
                    for kc in range(8):
                        B.mm(pf[bank][:, 0:ncols], XT[:, kc, tt * 128:(tt + 1) * 128], WTM[:, kc, BC0[blk]:BC0[blk] + ncols], kc == 0, kc == 7,
                             r=[("XT", tt)] + RW(blk), w=[PF(bank)])

                def rstd_of(var_ap, eps, res):
                    B.op("dve", "tensor_scalar_add", [res], [res], out=var_ap, in0=var_ap, scalar1=eps)
                    B.act(var_ap, var_ap, ACT.Sqrt, [res], [res])
                    B.op("dve", "reciprocal", [res], [res], out=var_ap, in_=var_ap)

                def to_fm(src_bf, dst, r, w, col0=0):
                    n = dst.shape[1]
                    for c in range(n):
                        B.tr(pbt[:, col0 + c * 128: col0 + (c + 1) * 128], src_bf[:, c * 128:(c + 1) * 128], r, ["pbt"])
                    B.op("act", "copy", ["pbt"], w, out=dst, in_=pbt[:, col0:col0 + n * 128].rearrange("p (k t) -> p k t", k=n))

                SKIP = os.environ.get("MK_SKIP", "").split(",")
                STOPT = int(os.environ.get("MK_STOPT", "99"))
                for tt in range(NT):
                    if tt > STOPT:
                        break
                    gt = h * NT + tt
                    tq = tt % 4
                    tsl = slice(tt * 128, (tt + 1) * 128)
                    cosr = _bc(COS[:, gt, :], [(0, 8), (1, 32)])
                    sinr = _bc(SIN[:, gt, :], [(0, 8), (1, 32)])
                    if 'sgu' not in SKIP:
                        proj_tm(tt, 0, 0)
                        B.act(GU[:], pf[0][:, 0:256], ACT.Gelu, [PF(0)], ["gu"])
                        B.act(GV[:], pf[0][:, 256:512], ACT.Gelu, [PF(0)], ["gv"])
                        B.op("dve", "bn_stats", ["gv"], ["gns"], out=GNS[:, 0, :], in_=GV[:])
                        B.op("dve", "bn_aggr", ["gns"], ["gnm"], out=GNM[:, 0, :], in_=GNS[:, 0, :])
                        rstd_of(GNM[:, 0, 1:2], LN_EPS, "gnm")
                        B.op("dve", "tensor_scalar", ["gv", "gnm"], ["gv"], out=GV[:], in0=GV[:], scalar1=GNM[:, 0, 0:1],
                             scalar2=GNM[:, 0, 1:2], op0=ALU.subtract, op1=ALU.mult)
                        B.op("pool", "tensor_tensor", ["gv", "sgn0"], ["gv"], out=GV[:], in0=GV[:], in1=SGN[:, 0, :], op=ALU.mult)
                        B.op("dve", "tensor_tensor", ["gv", "sgn1"], ["vbs"], out=VBs[:], in0=GV[:], in1=SGN[:, 1, :], op=ALU.add)
                        for g in range(4):
                            B.mm(pf[1][:, g * 64:(g + 1) * 64], WST[:, g, :], VBs[:, g * 64:(g + 1) * 64], True, True,
                                 r=["vbs", "wst"], w=[PF(1)])
                        for g in range(4):
                            B.op("dve", "scalar_tensor_tensor", [PF(1), "gu", "sgb"], ["y0"], out=Y0[:, g * 64:(g + 1) * 64],
                                 in0=pf[1][:, g * 64:(g + 1) * 64], scalar=SGB[:, g:g + 1], in1=GU[:, g * 64:(g + 1) * 64],
                                 op0=ALU.add, op1=ALU.mult)
                        to_fm(Y0, YS[:, 0, :, tsl], ["y0"], [("ys", 0, tt)])
                    if 'ret' not in SKIP:
                        proj_tm(tt, 1, 2)
                        proj_tm(tt, 2, 3)
                        qk = pf[2][:].rearrange("p (h two d) -> p h two d", h=8, two=2)
                        x1, x2 = qk[:, :, 0, :], qk[:, :, 1, :]
                        qkr = QKR[:].rearrange("p h (two d) -> p h two d", two=2)
                        B.op("dve", "tensor_tensor", [PF(2)], ["rt0"], out=RT[0][:], in0=x1, in1=cosr, op=ALU.mult)
                        B.op("dve", "tensor_tensor", [PF(2)], ["rt1"], out=RT[1][:], in0=x2, in1=sinr, op=ALU.mult)
                        B.op("dve", "tensor_tensor", [PF(2)], ["rt2"], out=RT[2][:], in0=x1, in1=sinr, op=ALU.mult)
                        B.op("dve", "tensor_tensor", [PF(2)], ["rt3"], out=RT[3][:], in0=x2, in1=cosr, op=ALU.mult)
                        B.op("dve", "tensor_tensor", ["rt0", "rt1"], ["qkr0"], out=qkr[:, :, 0, :], in0=RT[0][:], in1=RT[1][:], op=ALU.subtract)
                        B.op("dve", "tensor_tensor", ["rt2", "rt3"], ["qkr1"], out=qkr[:, :, 1, :], in0=RT[2][:], in1=RT[3][:], op=ALU.add)
                        B.op("dve", "tensor_copy", ["qkr0", "qkr1"], ["qkb"], out=QKB[:], in_=QKR[:])
                        for hd in range(4):
                            B.op("dve", "tensor_scalar", ["qkr0", "qkr1", "dec"], ["qsb"], out=QSB[:, hd, :], in0=QKR[:, hd, :],
                                 scalar1=DEC[:, hd:hd + 1], scalar2=None, op0=ALU.mult)
                            B.op("dve", "tensor_scalar", ["qkr0", "qkr1", "dec"], ["kdb"], out=KDB[:, hd, :], in0=QKR[:, 4 + hd, :],
                                 scalar1=DEC[:, 4 + hd:5 + hd], scalar2=None, op0=ALU.mult)
                        B.op("act", "copy", [PF(3)], ["vbr"], out=VBr[:], in_=pf[3][:, 0:256])
                        B.act(SGT[:], pf[3][:, 256:512], ACT.Silu, [PF(3)], ["sgt"])
                        if int(os.environ.get("MK_RETCUT", "9")) <= 1:
                            continue
                        for c in range(8):
                            B.tr(pbt[0:64, c * 128:(c + 1) * 128], QKB[:, c, :], ["qkb"], ["pbt"])
                        B.op("act", "copy", ["pbt"], ["trs"], out=TRS[0:64, 0:8, :], in_=pbt[0:64, :].rearrange("p (k t) -> p k t", k=8))
                        for c in range(4):
                            B.tr(pbt[0:64, c * 128:(c + 1) * 128], QSB[:, c, :], ["qsb"], ["pbt"])
                        B.op("act", "copy", ["pbt"], ["trs2"], out=TRS[0:64, 8:12, :], in_=pbt[0:64, 0:512].rearrange("p (k t) -> p k t", k=4))
                        if int(os.environ.get("MK_RETCUT", "9")) <= 2:
                            continue
                        for hd in range(4):
                            B.mm(pf[4][:, hd * 128:(hd + 1) * 128], TRS[0:64, 4 + hd, :], TRS[0:64, hd, :], True, True,
                                 r=["trs"], w=[PF(4)])
                        B.op("dve", "tensor_tensor", [PF(4), "dt"], ["smk"], out=SMK[:].rearrange("p a b -> p (a b)"), in0=pf[4][:],
                             in1=DT[:].rearrange("p a b -> p (a b)"), op=ALU.mult)
                        if int(os.environ.get("MK_RETCUT", "9")) <= 3:
                            continue
                        for hd in range(4):
                            B.mm(pf[5][:, hd * 64:(hd + 1) * 64], SMK[:, hd, :], VBr[:, hd * 64:(hd + 1) * 64], True, False,
                                 r=["smk", "vbr"], w=[PF(5)])
                            B.mm(pf[5][:, hd * 64:(hd + 1) * 64], TRS[0:64, 8 + hd, :], STATEB[:, hd, :], False, True,
                                 r=["trs2", "stateb"], w=[PF(5)])
                        if int(os.environ.get("MK_RETCUT", "9")) <= 4:
                            continue
                        for hd in range(4):
                            B.mm(pf[6][0:64, hd * 64:(hd + 1) * 64], KDB[:, hd, :], VBr[:, hd * 64:(hd + 1) * 64], True, True,
                                 r=["kdb", "vbr"], w=[PF(6)])
                        for hd in range(4):
                            B.op("dve", "scalar_tensor_tensor", [PF(6), "state"], ["state"], out=STATE[:, hd, :],
                                 in0=STATE[:, hd, :], scalar=G128[hd], in1=pf[6][0:64, hd * 64:(hd + 1) * 64],
                                 op0=ALU.mult, op1=ALU.add)
                        B.op("dve", "tensor_copy", ["state"], ["stateb"], out=STATEB[:], in_=STATE[:])
                        if int(os.environ.get("MK_RETCUT", "9")) <= 5:
                            continue
                        for hd in range(4):
                            B.op("dve", "bn_stats", [PF(5)], ["gns"], out=GNS[:, hd, :], in_=pf[5][:, hd * 64:(hd + 1) * 64])
                        for hd in range(4):
                            B.op("dve", "bn_aggr", ["gns"], ["gnm"], out=GNM[:, hd, :], in_=GNS[:, hd, :])
                        B.op("dve", "tensor_copy", ["gnm"], ["gnr"], out=GNR[:], in_=GNM[:, :, 1])
                        rstd_of(GNR[:], GN_EPS, "gnr")
                        for hd in range(4):
                            B.op("dve", "tensor_scalar", [PF(5), "gnm", "gnr"], ["yn"], out=YN[:, hd * 64:(hd + 1) * 64],
                                 in0=pf[5][:, hd * 64:(hd + 1) * 64], scalar1=GNM[:, hd, 0:1], scalar2=GNR[:, hd:hd + 1],
                                 op0=ALU.subtract, op1=ALU.mult)
                        B.op("dve", "tensor_tensor", ["yn", "sgt"], ["y1"], out=Y1[:], in0=YN[:], in1=SGT[:], op=ALU.mult)
                        to_fm(Y1, YS[:, 1, :, tsl], ["y1"], [("ys", 1, tt)])
                    if 'pool' not in SKIP:
                        proj_tm(tt, 3, 0, 256)
                        zc, zp = ZB[gt % 2], ZB[(gt + 1) % 2]
                        B.op("act", "copy", [PF(0)], [("zb", gt % 2)], out=zc[:], in_=pf[0][:, 0:256])
                        for c in range(2):
                            for g2 in range(2):
                                wi = c * 2 + g2
                                osl = pf[1][g2 * 64:(g2 + 1) * 64, c * 128:(c + 1) * 128]
                                kind = 2 if gt == 0 else 0
                                B.mm(osl, zc[:, wi * 64:(wi + 1) * 64], BAND[:, kind * 4 + wi, :], True, gt == 0,
                                     r=[("zb", gt % 2), "band"], w=[PF(1)], tile_position=(0, g2 * 64))
                                if gt > 0:
                                    B.mm(osl, zp[:, wi * 64:(wi + 1) * 64], BAND[:, 4 + wi, :], False, True,
                                         r=[("zb", (gt + 1) % 2), "band"], w=[PF(1)], tile_position=(0, g2 * 64))
                        B.op("act", "copy", [PF(1)], ["plt"], out=PLT[:], in_=pf[1][:, 0:256].rearrange("p (c t) -> p c t", c=2))
                        for c in range(2):
                            B.mm(pf[0][:, 256 + c * 128:256 + (c + 1) * 128], WPOOL[:, c, :], PLT[:, c, :], True, True, r=["plt", "wpool"], w=[PF(0)])
                            B.op("dve", "tensor_scalar", [PF(0), "psc"], [("ys", 2, tt)], out=YS[:, 2, c, tsl],
                                 in0=pf[0][:, 256 + c * 128:256 + (c + 1) * 128], scalar1=PSC[:, c:c + 1], scalar2=None, op0=ALU.mult)
                    if 'mla' not in SKIP:
                        proj_tm(tt, 4, 2, 416)
                        B.act(SQJ[:, 0:256], pf[2][:, 0:256], ACT.Square, [PF(2)], ["sm4"], accum_out=SM4[:, 0:1])
                        B.act(SQJ[:, 0:128], pf[2][:, 256:384], ACT.Square, [PF(2), "sm4"], ["sm4"], accum_out=SM4[:, 1:2])
                        B.op("dve", "tensor_scalar", ["sm4"], ["sm4"], out=SM4[:, 0:1], in0=SM4[:, 0:1], scalar1=1.0 / 256, scalar2=RMS_EPS, op0=ALU.mult, op1=ALU.add)
                        B.op("dve", "tensor_scalar", ["sm4"], ["sm4"], out=SM4[:, 1:2], in0=SM4[:, 1:2], scalar1=1.0 / 128, scalar2=RMS_EPS, op0=ALU.mult, op1=ALU.add)
                        B.act(SM4[:, 0:2], SM4[:, 0:2], ACT.Sqrt, ["sm4"], ["sm4"])
                        B.op("dve", "reciprocal", ["sm4"], ["sm4"], out=SM4[:, 0:2], in_=SM4[:, 0:2])
                        B.op("dve", "tensor_scalar", [PF(2), "sm4"], ["cqn"], out=CQN[:], in0=pf[2][:, 0:256], scalar1=SM4[:, 0:1], scalar2=None, op0=ALU.mult)
                        B.op("dve", "tensor_scalar", [PF(2), "sm4"], ["ckvn"], out=CKVN[:], in0=pf[2][:, 256:384], scalar1=SM4[:, 1:2], scalar2=None, op0=ALU.mult)
                        if int(os.environ.get("MK_MLACUT", "9")) <= 1:
                            continue
                        kr = pf[2][:, 384:416]
                        cm = _bc(COS[:, gt, :], [(2, 16)])
                        sm_ = _bc(SIN[:, gt, :], [(2, 16)])
                        B.op("dve", "tensor_tensor", [PF(2)], ["kr0"], out=KR_[0][:], in0=kr[:, 0:16], in1=cm, op=ALU.mult)
                        B.op("dve", "tensor_tensor", [PF(2)], ["kr1"], out=KR_[1][:], in0=kr[:, 16:32], in1=sm_, op=ALU.mult)
                        B.op("dve", "tensor_tensor", [PF(2)], ["kr2"], out=KR_[2][:], in0=kr[:, 0:16], in1=sm_, op=ALU.mult)
                        B.op("dve", "tensor_tensor", [PF(2)], ["kr3"], out=KR_[3][:], in0=kr[:, 16:32], in1=cm, op=ALU.mult)
                        B.op("dve", "tensor_tensor", ["kr0", "kr1"], ["kpe0"], out=KPE[:, 0:16], in0=KR_[0][:], in1=KR_[1][:], op=ALU.subtract)
                        B.op("dve", "tensor_tensor", ["kr2", "kr3"], ["kpe1"], out=KPE[:, 16:32], in0=KR_[2][:], in1=KR_[3][:], op=ALU.add)
                        if int(os.environ.get("MK_MLACUT", "9")) <= 2:
                            continue
                        for c in range(2):
                            B.tr(pbt[:, c * 128:(c + 1) * 128], CQN[:, c * 128:(c + 1) * 128], ["cqn"], ["pbt"])
                        B.tr(pbt[:, 256:384], CKVN[:], ["ckvn"], ["pbt"])
                        B.op("act", "copy", ["pbt"], ["ct"], out=CT[:], in_=pbt[:, 0:384].rearrange("p (k t) -> p k t", k=3))
                        for kc in range(2):
                            B.mm(pf[3][:, 0:384], CT[:, kc, :], WUQ[:, kc, :], kc == 0, kc == 1, r=["ct", ("wuq", 0), ("wuq", 1)], w=[PF(3)])
                        B.mm(pf[4][:], CT[:, 2, :], WUKV[:], True, True, r=["ct", "wukv"], w=[PF(4)])
                        if int(os.environ.get("MK_MLACUT", "9")) <= 3:
                            continue
                        q4 = pf[3][:, 0:384].rearrange("p (h d) -> p h d", h=4)
                        cm4 = _bc(COS[:, gt, :], [(0, 4), (2, 16)])
                        sm4_ = _bc(SIN[:, gt, :], [(0, 4), (2, 16)])
                        B.op("dve", "tensor_tensor", [PF(3)], ["mr0"], out=MR[0][:], in0=q4[:, :, 0:16], in1=cm4, op=ALU.mult)
                        B.op("dve", "tensor_tensor", [PF(3)], ["mr1"], out=MR[1][:], in0=q4[:, :, 16:32], in1=sm4_, op=ALU.mult)
                        B.op("dve", "tensor_tensor", [PF(3)], ["mr2"], out=MR[2][:], in0=q4[:, :, 0:16], in1=sm4_, op=ALU.mult)
                        B.op("dve", "tensor_tensor", [PF(3)], ["mr3"], out=MR[3][:], in0=q4[:, :, 16:32], in1=cm4, op=ALU.mult)
                        B.op("dve", "tensor_tensor", ["mr0", "mr1"], ["qf0"], out=QF[:, :, 0:16], in0=MR[0][:], in1=MR[1][:], op=ALU.subtract)
                        B.op("dve", "tensor_tensor", ["mr2", "mr3"], ["qf1"], out=QF[:, :, 16:32], in0=MR[2][:], in1=MR[3][:], op=ALU.add)
                        for hd in range(4):
                            B.op("act", "copy", [PF(3)], ["qf2"], out=QF[:, hd, 32:96], in_=pf[3][:, hd * 96 + 32:hd * 96 + 96])
                        for hd in range(4):
                            B.op("dve", "tensor_copy", ["kpe0", "kpe1"], ["kf0"], out=KF[:, hd, 0:32], in_=KPE[:])
                        for hd in range(4):
                            B.op("act", "copy", [PF(4)], ["kf1"], out=KF[:, hd, 32:96], in_=pf[4][:, hd * 64:(hd + 1) * 64])
                        for hd in range(4):
                            B.op("dve", "tensor_copy", [PF(4)], [("vp", gt)], out=VP[:, gt, hd * 65:hd * 65 + 64], in_=pf[4][:, 256 + hd * 64:256 + (hd + 1) * 64])
                        if int(os.environ.get("MK_MLACUT", "9")) <= 4:
                            continue
                        for hd in range(4):
                            B.tr(pbt[0:96, hd * 128:(hd + 1) * 128], QF[:, hd, :], ["qf0", "qf1", "qf2"], ["pbt"])
                        for hd in range(4):
                            B.tr(pbt[0:96, (4 + hd) * 128:(5 + hd) * 128], KF[:, hd, :], ["kf0", "kf1"], ["pbt"])
                        tq = tt % 4
                        B.op("act", "copy", ["pbt"], [("qt", tq)], out=QT[0:96, :, tq * 128:(tq + 1) * 128],
                             in_=pbt[0:96, 0:512].rearrange("p (k t) -> p k t", k=4))
                        B.op("act", "copy", ["pbt"], [("kt", gt)], out=KT[0:96, :, gt * 128:(gt + 1) * 128],
                             in_=pbt[0:96, 512:1024].rearrange("p (k t) -> p k t", k=4))
                    if 'attn' not in SKIP:
                        if tq == 3:
                            Qb = tt // 4
                            g0 = h * NT + Qb * 4
                            ob = [3, 4, 5, 6]
                            ei = 0
                            for hd in range(4):
                                for ki in range(g0 + 4):
                                    rr_ = ki - g0
                                    q0 = max(rr_, 0)
                                    ncol = (4 - q0) * 128
                                    sb_ = ei % 2
                                    et = ET[ei % 3]
                                    er = ("et", ei % 3)
                                    ei += 1
                                    B.mm(pf[sb_][:, 0:ncol], KT[0:96, hd, ki * 128:(ki + 1) * 128], QT[0:96, hd, q0 * 128:512], True, True,
                                         r=[("kt", ki), ("ktl", hd)] + [("qt", q) for q in range(q0, 4)], w=[PF(sb_)])
                                    B.act(et[:, 0:ncol], pf[sb_][:, 0:ncol], ACT.Exp, [PF(sb_)], [er])
                                    if rr_ >= 0:
                                        B.op("pool", "memset", [er], [er], et[64:128, 0:64], 0.0)
                                    for qs in range(q0, 4):
                                        B.mm(pf[ob[qs]][:, hd * 65:(hd + 1) * 65], et[:, (qs - q0) * 128:(qs - q0 + 1) * 128], VP[:, ki, hd * 65:(hd + 1) * 65],
                                             ki == 0, ki == g0 + qs, r=[er, ("vp", ki), "vp1", "vpl"], w=[PF(ob[qs])])
                            for qs in range(4):
                                t2 = Qb * 4 + qs
                                o4 = pf[ob[qs]][:, 0:260].rearrange("p (h e) -> p h e", h=4)
                                B.op("dve", "reciprocal", [PF(ob[qs])], ["rd"], out=RD[:], in_=o4[:, :, 64])
                                for hd in range(4):
                                    B.op("dve", "tensor_scalar", [PF(ob[qs]), "rd"], ["y3"], out=Y3[:, hd, :], in0=pf[ob[qs]][:, hd * 65:hd * 65 + 64],
                                         scalar1=RD[:, hd:hd + 1], scalar2=None, op0=ALU.mult)
                                to_fm(Y3[:].rearrange("p h e -> p (h e)"), YS[:, 3, :, t2 * 128:(t2 + 1) * 128], ["y3"], [("ys", 3, t2)])
                if STOPT < 99:
                    S.barrier()
                    return
                if h == 0:
                    S.barrier()
                    for hd in range(4):
                        B.dma("sp", kt_d[0:96, hd, :], KT[0:96, hd, 0:HALF], r=[], w=[("ktd", hd)])
                    B.dma("sp", vp_d, VP[:, 0:NT, :], r=[], w=["vpd"])
                S.barrier()
                gi = 0
                for n_ in range(8):
                    wg = WG[n_ % 2]
                    for kc in range(0, 8, 4):
                        B.dma("pool", wg[:, kc:kc + 4, :], wg_d[L, n_, :, kc:kc + 4, :], r=[], w=[("wg", n_ % 2, kc)])
                    wbn = WBRn[n_ % 2]
                    B.dma("pool", wbn, wbr_d[L, :, :, n_ * 128:(n_ + 1) * 128], r=[], w=[("wbr", n_ % 2)])
                    for tg in range(4):
                        tsl = slice(tg * 512, (tg + 1) * 512)
                        xr = [("XT", t) for t in range(tg * 4, tg * 4 + 4)]
                        ms = MS[(n_ * 4 + tg) % 2]
                        msr = ("ms", (n_ * 4 + tg) % 2)
                        for b in range(4):
                            k2 = gi % 2
                            gi += 1
                            pg, pp = pf[k2 * 2], pf[k2 * 2 + 1]
                            for kc in range(8):
                                B.mm(pg[:], wg[:, kc, b * 128:(b + 1) * 128], XT[:, kc, tsl], kc == 0, kc == 7,
                                     r=xr + [("wg", n_ % 2, 0), ("wg", n_ % 2, 4)], w=[PF(k2 * 2)])
                            for kc in range(2):
                                B.mm(pp[:], wbn[:, b * 2 + kc, :], YS[:, b, kc, tsl], kc == 0, kc == 1,
                                     r=[("wbr", n_ % 2)], w=[PF(k2 * 2 + 1)])
                            sg = MS[2]
                            B.act(sg[:], pg[:], ACT.Sigmoid, [PF(k2 * 2)], ["sgm"])
                            if b == 0:
                                B.op("dve", "tensor_tensor", ["sgm", PF(k2 * 2 + 1)], [msr], out=ms[:], in0=sg[:], in1=pp[:], op=ALU.mult)
                            else:
                                B.op("dve", "tensor_tensor", ["sgm", PF(k2 * 2 + 1)], ["sgm"], out=sg[:], in0=sg[:], in1=pp[:], op=ALU.mult)
                                if b < 3:
                                    B.op("pool", "tensor_tensor", ["sgm", msr], [msr], out=ms[:], in0=ms[:], in1=sg[:], op=ALU.add)
                                else:
                                    B.op("pool", "tensor_tensor", ["sgm", msr], [("merg", n_, tg)], out=MERG[:, n_, tsl], in0=ms[:], in1=sg[:], op=ALU.add)
                S.barrier()
                for kc in range(8):
                    B.dma("pool", WO[:, kc, :], wout_d[L, :, kc, :], r=[], w=[("wo", kc)])
                for tt in range(NT):
                    B.dma("sp", X[:, tt, :], xsp_d[:, tt, :], r=[], w=[("X", tt)])
                set_ln(114688)
                load_ln(L, 1)
                for tt in range(NT):
                    for nh in range(2):
                        po = pf[4 + nh]
                        for kc in range(8):
                            B.mm(po[:], MERG[:, kc, tt * 128:(tt + 1) * 128], WO[:, kc, nh * 512:(nh + 1) * 512], kc == 0, kc == 7,
                                 r=[("wo", kc)], w=[PF(4 + nh)])
                        B.op("dve", "tensor_tensor", [PF(4 + nh), ("X", tt)], [("X", tt)], out=X[:, tt, nh * 512:(nh + 1) * 512],
                             in0=X[:, tt, nh * 512:(nh + 1) * 512], in1=po[:], op=ALU.add)
                    ln_pipe(tt, h, False)
                ln_pipe(NT, h, False)
                ln_pipe(NT + 1, h, False)

            def dump_dbg():
                S.barrier()
                for tt in range(NT):
                    B.dma("sp", dbg_d[tt * 128:(tt + 1) * 128, :], X[:, tt, :], r=[], w=["dbg"])
                S.barrier()

            stages = DEBUG.split(",") if DEBUG else []
            for L in range(2):
                if not os.environ.get("MK_NOCONST"):
                    load_layer_consts(L)
                for h in range(2):
                    S.barrier()
                    if L == 0:
                        for kc in range(8):
                            B.dma("pool", XT[:, kc, :], xT_d[kc * 128:(kc + 1) * 128, h * HALF:(h + 1) * HALF], r=[], w=[("XTl", kc)])
                        for tt in range(NT):
                            r0 = h * HALF + tt * 128
                            B.dma("sp", X[:, tt, :], x_d[r0:r0 + 128, :], r=[], w=[("X", tt)])
                            B.op("pool", "tensor_scalar_mul", [("X", tt)], [("X", tt)], out=X[:, tt, :], in0=X[:, tt, :], scalar1=ALPHA)
                    else:
                        for tt in range(NT):
                            B.dma("sp", X[:, tt, :], xl_d[h, :, tt, :], r=[], w=[("X", tt)])
                        for kc in range(8):
                            B.dma("sp", XT[:, kc, :], xtl_d[h, :, kc, :], r=[], w=[("XTl", kc)])
                    S.barrier()
                    def stop_here(tag):
                        if stages and stages[-1] == tag % (L, h):
                            dump_dbg()
                            S.barrier()
                            S.emit(nc, st)
                            return True
                        return False

                    ffn(L, 0, h, False)
                    if stop_here("ffn1_%d%d"):
                        return nc
                    mixer(L, h)
                    if stop_here("mix_%d%d"):
                        return nc
                    ple(L, h)
                    if stop_here("ple_%d%d"):
                        return nc
                    ffn(L, 1, h, L == 1)
                    if stop_here("end_%d%d"):
                        return nc
                    if L == 0:
                        S.barrier()
                        for tt in range(NT):
                            B.dma("sp", xl_d[h, :, tt, :], X[:, tt, :], r=[], w=[("xl", tt)])
                        for kc in range(8):
                            B.dma("sp", xtl_d[h, :, kc, :], XT[:, kc, :], r=[], w=[("xtl", kc)])
            S.barrier()
            S.emit(nc, st)
        return nc


_CACHE = {}


def prep_inputs(inputs):
    f = lambda a: np.ascontiguousarray(np.asarray(a, dtype=np.float32))
    hc = host_consts()
    shared = {k: hc[k] for k in ("ident", "ret_dec", "sgu_mask", "invf", "pool_band")}
    shared["ret_dt"] = f(hc["ret_dt"].reshape(128, 512))
    g = lambda k: np.asarray(inputs[k])

    def up(W):
        W = np.asarray(W).reshape(2, 8, 128, 2, 22, 128)
        return f(W.transpose(0, 4, 2, 1, 3, 5).reshape(2, 22, 128, 8 * 256))

    def dn(W):
        return f(np.asarray(W).reshape(2, 22, 128, 1024).transpose(0, 2, 1, 3))

    def kp(W, K):
        W = np.asarray(W)
        return f(W.reshape(2, K, 128, W.shape[-1]).transpose(0, 2, 1, 3))

    shared["wup1"] = up(g("ffn1_up")); shared["wdn1"] = dn(g("ffn1_down"))
    shared["wup2"] = up(g("ffn2_up")); shared["wdn2"] = dn(g("ffn2_down"))
    bc = lambda a: f(np.broadcast_to(np.asarray(a)[:, :, None, :], (2, a.shape[1], 128, a.shape[2])))
    shared["lng"] = bc(np.stack([g("ln1_g"), g("ln2_g"), g("ln3_g")], axis=1))
    shared["lnb"] = bc(np.stack([g("ln1_b"), g("ln2_b"), g("ln3_b")], axis=1))
    shared["wpg"] = kp(g("w_ple_gate"), 8)
    shared["wpl"] = kp(g("w_ple"), 2)
    win = g("w_in")
    shared["wtm"] = f(win[:, :, 0:2208].reshape(2, 8, 128, 2208).transpose(0, 2, 1, 3))
    wgt = win[:, :, 2208:].reshape(2, 8, 128, 4, 8, 128)
    shared["wg"] = f(wgt.transpose(0, 4, 2, 1, 3, 5).reshape(2, 8, 128, 8, 512))
    wb = g("w_branch").reshape(2, 4, 2, 128, 1024)
    shared["wbr"] = f(wb.transpose(0, 3, 1, 2, 4).reshape(2, 128, 8, 1024))
    shared["wout"] = kp(g("w_out"), 8)
    shared["sguw"] = f(g("sgu_w").transpose(0, 3, 1, 2).reshape(2, 128, 512))
    shared["sgun"] = f(np.broadcast_to(np.stack([g("sgu_ln_g"), g("sgu_ln_b")], axis=1)[:, :, None, :], (2, 2, 128, 256)))
    shared["sgub"] = f(g("sgu_b").transpose(0, 2, 1))
    pw = g("pool_w")
    pbd = np.zeros((2, 128, 2, 128), np.float32)
    for c in range(2):
        for g2 in range(2):
            pbd[:, g2 * 64:(g2 + 1) * 64, c, g2 * 64:(g2 + 1) * 64] = pw[:, c * 2 + g2]
    shared["poolw"] = f(pbd.reshape(2, 128, 256))
    shared["poolsc"] = f(g("pool_scale").reshape(2, 2, 128).transpose(0, 2, 1))
    wq = g("mla_w_uq").reshape(2, 2, 128, 4, 96)
    wq = np.concatenate([wq[..., 64:96], wq[..., 0:64]], axis=-1)
    shared["wuq"] = f(wq.transpose(0, 2, 1, 3, 4).reshape(2, 128, 768))
    shared["gq"] = f(g("mla_q_norm").reshape(2, 2, 128).transpose(0, 2, 1))
    wkv = g("mla_w_ukv").reshape(2, 128, 4, 128)
    shared["wukv"] = f(np.concatenate([wkv[..., 0:64].reshape(2, 128, 256), wkv[..., 64:128].reshape(2, 128, 256)], axis=-1))
    shared["gkv"] = f(g("mla_kv_norm").reshape(2, 128, 1))
    maps = []
    x = np.asarray(inputs["x"]); p = np.asarray(inputs["p"]); pos = np.asarray(inputs["positions"])
    for c in range(NCORES):
        b = c % 4
        m = dict(shared)
        m["x"] = f(x[b]); m["xT"] = f(x[b].T); m["pT"] = f(p[:, b].transpose(0, 2, 1))
        m["pos"] = np.ascontiguousarray(pos[b].reshape(32, 128).T.astype(np.int32))
        maps.append(m)
    return maps


def get_program():
    if "nc" not in _CACHE:
        b = Builder()
        _CACHE["nc"] = b.build()
        _CACHE["din"] = b.din
        _CACHE["stats"] = getattr(b.S, "stats", None)
    return _CACHE["nc"], _CACHE["din"]


def kernel(**inputs):
    nc, din = get_program()
    maps = prep_inputs(inputs)
    maps = [{k: v for k, v in m.items() if k in din} for m in maps]
    res = run_bass_kernel_spmd(nc, maps, core_ids=list(range(NCORES)))
    _CACHE["last"] = res
    out = np.stack([np.asarray(res.results[b]["out"]) for b in range(4)], axis=0)
    return out.astype(np.float32)
```

```python
import os
import math
from contextlib import ExitStack

import numpy as np
import concourse.bass as bass
import concourse.mybir as mybir
from concourse.bass_utils import run_bass_kernel_spmd

ACT = mybir.ActivationFunctionType
ALU = mybir.AluOpType
F32 = mybir.dt.float32
BF16 = mybir.dt.bfloat16
I32 = mybir.dt.int32
AX = mybir.AxisListType

D = 1024
SEQ = 4096
HALF = 2048
NT = 16
DFF = 2816
ALPHA = 4 ** 0.25
LN_EPS = 1e-5
RMS_EPS = 1e-6
GN_EPS = 1e-5
SPLITS = [(0, 8), (8, 7), (15, 7)]
NCORES = int(os.environ.get("MK_NCORES", "4"))
DEBUG = os.environ.get("MK_DEBUG", "")


class Op:
    __slots__ = ("eng", "fn", "raw", "war", "dma", "needs_inc", "token", "idx")

    def __init__(self, eng, fn, dma, idx):
        self.eng = eng
        self.fn = fn
        self.raw = set()
        self.war = set()
        self.dma = dma
        self.needs_inc = False
        self.token = None
        self.idx = idx


COMPUTE = ("pe", "act", "dve", "pool")


class Sched:
    NDSEM = {"sp": 8, "pool": 8, "act": 2}

    def __init__(self):
        self.ops = []
        self.lastw = {}
        self.readers = {}
        self.last_of = {}
        self.open_dma = []

    def add(self, eng, fn, r=(), w=(), dma=False, after=()):
        idx = len(self.ops)
        op = Op(eng, fn, dma, idx)
        for res in r:
            lw = self.lastw.get(res)
            if lw is not None:
                op.raw.add(lw)
            if isinstance(res, tuple) and res[0] == "pf" or res == "pbt":
                rd = self.readers.get(res)
                if rd:
                    for k_, x in rd.items():
                        if k_ != eng and not isinstance(x, list):
                            op.raw.add(x)
        for res in w:
            lw = self.lastw.get(res)
            if lw is not None:
                op.war.add(lw)
            rd = self.readers.get(res)
            if rd:
                for x in rd.values():
                    if isinstance(x, list):
                        op.war.update(x)
                    else:
                        op.war.add(x)
        for a in after:
            op.raw.add(a)
        for res in w:
            self.lastw[res] = idx
            self.readers[res] = {}
        for res in r:
            if res in w:
                continue
            rd = self.readers.setdefault(res, {})
            if dma:
                rd.setdefault(("dma", eng), []).append(idx)
            else:
                rd[eng] = idx
        self.ops.append(op)
        if fn is not None:
            if dma:
                self.open_dma.append(idx)
            else:
                self.last_of[eng] = idx
        return idx

    def barrier(self):
        deps = list(self.last_of.values()) + list(self.open_dma)
        self.open_dma = []
        for e in ("pe", "act", "dve", "pool", "sp"):
            self.add(e, None, after=deps)
        self.lastw = {}
        self.readers = {}

    def emit(self, nc, stack):
        ops = self.ops
        sems = {e: stack.enter_context(nc.semaphore("s_" + e)) for e in COMPUTE}
        dsems = {
            q: [stack.enter_context(nc.semaphore("d_%s%d" % (q, i))) for i in range(n)]
            for q, n in self.NDSEM.items()
        }
        rr = {q: 0 for q in dsems}
        ncnt = {q: [0] * len(dsems[q]) for q in dsems}
        lastd = {q: [None] * len(dsems[q]) for q in dsems}
        for op in ops:
            if op.dma:
                q = op.eng
                k = rr[q] % len(dsems[q])
                rr[q] += 1
                ncnt[q][k] += 1
                op.token = (dsems[q][k], 16 * ncnt[q][k])
                if lastd[q][k] is not None:
                    op.raw.add(lastd[q][k])
                lastd[q][k] = op.idx
        for op in ops:
            deps = set()
            for d in op.raw:
                p = ops[d]
                if p.dma or op.dma or p.eng != op.eng or op.eng != "pe":
                    deps.add(d)
            for d in op.war:
                p = ops[d]
                if p.dma or op.dma or p.eng != op.eng or op.eng != "pe":
                    deps.add(d)
            deps.discard(op.idx)
            op.raw = deps
            for d in deps:
                ops[d].needs_inc = True
        cnt = {e: 0 for e in COMPUTE}
        for op in ops:
            if op.dma or op.fn is None:
                continue
            if op.needs_inc:
                cnt[op.eng] += 1
                op.token = (sems[op.eng], cnt[op.eng])
        self.stats = dict(cnt)
        self.stats["nops"] = len(ops)
        per_eng = {}
        for op in ops:
            per_eng.setdefault(op.eng, []).append(op)

        def emit_engine(ename, e):
            waited = {}
            for op in per_eng.get(ename, []):
                need = {}
                for d in op.raw:
                    tok = ops[d].token
                    if tok is None:
                        continue
                    s, v = tok
                    if waited.get(id(s), 0) >= v:
                        continue
                    if need.get(id(s), (None, 0))[1] < v:
                        need[id(s)] = (s, v)
                for s, v in need.values():
                    e.wait_ge(s, v)
                    waited[id(s)] = v
                if op.fn is None:
                    continue
                ins = op.fn(e)
                if op.dma:
                    ins.then_inc(op.token[0], 16)
                elif op.needs_inc:
                    ins.then_inc(op.token[0], 1)

        with nc.Block() as block:

            @block.tensor
            def _(e):
                emit_engine("pe", e)

            @block.scalar
            def _(e):
                emit_engine("act", e)

            @block.vector
            def _(e):
                emit_engine("dve", e)

            @block.gpsimd
            def _(e):
                emit_engine("pool", e)

            @block.sync
            def _(e):
                emit_engine("sp", e)


def host_consts():
    c = {}
    c["ident"] = np.eye(128, dtype=np.float32)
    lg = np.log1p(-np.exp2(-5.0 - np.arange(4, dtype=np.float64)))
    j = np.arange(128)[:, None]
    i = np.arange(128)[None, :]
    same = (j // 64) == (i // 64)
    dt = np.zeros((128, 4, 128), np.float64)
    for h in range(4):
        m = np.where(same, np.exp(lg[h] * np.abs(i - j)), np.where(j < i, np.exp(lg[h] * (i - j)), 0.0))
        dt[:, h, :] = m * (64 ** -0.5)
    c["ret_dt"] = dt.astype(np.float32)
    tl = np.arange(128)[:, None]
    dec = np.zeros((128, 8), np.float64)
    dec[:, 0:4] = np.exp(lg[None, :] * (tl + 1))
    dec[:, 4:8] = np.exp(lg[None, :] * (127 - tl)) * (64 ** -0.5)
    c["ret_dec"] = dec.astype(np.float32)
    c["g128"] = [float(np.exp(lg[h] * 128)) for h in range(4)]
    c["sgu_mask"] = (i >= j).astype(np.float32)
    invf = (10000.0 ** (-np.arange(0, 64, 2, dtype=np.float32) / 64)).astype(np.float32)
    c["invf"] = np.broadcast_to(invf[None, :], (128, 32)).copy()
    band = np.zeros((128, 3, 4, 128), np.float64)
    for wi, w in enumerate((2, 4, 8, 16)):
        dts = i - j
        band[:, 0, wi, :] = np.where((dts >= 0) & (dts < w), 1.0 / w, 0.0) - (dts == 0)
        band[:, 1, wi, :] = np.where((dts + 128 >= 0) & (dts + 128 < w), 1.0 / w, 0.0)
        cnt = np.minimum(i + 1, w)
        band[:, 2, wi, :] = np.where((dts >= 0) & (dts < w), 1.0 / cnt, 0.0) - (dts == 0)
    c["pool_band"] = band.astype(np.float32).reshape(128, 12 * 128)
    return c


def _bc(ap, dims):
    return bass.AP(ap.tensor, ap.offset, [list(ap.ap[0])] + [[s, n] for s, n in dims])


class Builder:
    def __init__(self):
        self.nc = bass.Bass("TRN2", target_bir_lowering=False)
        self.S = Sched()
        self.din = {}

    def dram_in(self, name, shape, dt=F32):
        t = self.nc.dram_tensor(name, list(shape), dt, kind="ExternalInput")
        self.din[name] = (tuple(shape), dt)
        return t.ap()

    def mm(self, out, lhsT, rhs, start, stop, r, w, **kw):
        self.S.add("pe", lambda e: e.matmul(out, lhsT=lhsT, rhs=rhs, start=start, stop=stop, **kw), r=r, w=w)

    def tr(self, out, in_, r, w):
        ident = self.identb
        n = in_.shape[0]
        self.S.add("pe", lambda e: e.transpose(out, in_, ident[0:n, 0:n]), r=list(r), w=w)

    def act(self, out, in_, func, r, w, **kw):
        self.S.add("act", lambda e: e.activation(out=out, in_=in_, func=func, **kw), r=r, w=w)

    def op(self, eng, name, r, w, *a, **kw):
        self.S.add(eng, lambda e: getattr(e, name)(*a, **kw), r=r, w=w)

    def dma(self, q, out, in_, r, w):
        self.S.add(q, lambda e: e.dma_start(out=out, in_=in_), r=r, w=w, dma=True)

    def carve(self, off_bytes, shape, dt):
        esz = 4 if dt in (F32, I32) else 2
        n = int(np.prod(shape[1:]))
        assert off_bytes % 4 == 0 and off_bytes + n * esz <= self.arena_bytes, (off_bytes, shape)
        a = self.arena[:, off_bytes // 2: off_bytes // 2 + n * esz // 2]
        if esz == 4:
            a = a.bitcast(dt)
        if len(shape) == 3:
            a = a.rearrange("p (a b) -> p a b", a=shape[1])
        elif len(shape) == 4:
            a = a.rearrange("p (a b c) -> p a b c", a=shape[1], b=shape[2])
        return a

    def build(self):
        nc = self.nc
        S = self.S
        B = self
        HC = host_consts()
        G128 = HC["g128"]
        x_d = B.dram_in("x", [SEQ, D])
        xT_d = B.dram_in("xT", [D, SEQ])
        pT_d = B.dram_in("pT", [2, 256, SEQ])
        pos_d = B.dram_in("pos", [128, 32], I32)
        wup_d = [B.dram_in("wup1", [2, 22, 128, 8 * 256]), B.dram_in("wup2", [2, 22, 128, 8 * 256])]
        wdn_d = [B.dram_in("wdn1", [2, 128, 22, 1024]), B.dram_in("wdn2", [2, 128, 22, 1024])]
        lng_d = B.dram_in("lng", [2, 3, 128, D])
        lnb_d = B.dram_in("lnb", [2, 3, 128, D])
        wpg_d = B.dram_in("wpg", [2, 128, 8, 1024])
        wpl_d = B.dram_in("wpl", [2, 128, 2, 1024])
        wtm_d = B.dram_in("wtm", [2, 128, 8, 2208])
        wg_d = B.dram_in("wg", [2, 8, 128, 8, 512])
        wbr_d = B.dram_in("wbr", [2, 128, 8, 1024])
        wout_d = B.dram_in("wout", [2, 128, 8, 1024])
        sguw_d = B.dram_in("sguw", [2, 128, 4 * 128])
        sgun_d = B.dram_in("sgun", [2, 2, 128, 256])
        sgub_d = B.dram_in("sgub", [2, 128, 4])
        poolw_d = B.dram_in("poolw", [2, 128, 2 * 128])
        poolsc_d = B.dram_in("poolsc", [2, 128, 2])
        wuq_d = B.dram_in("wuq", [2, 128, 2 * 384])
        gq_d = B.dram_in("gq", [2, 128, 2])
        wukv_d = B.dram_in("wukv", [2, 128, 512])
        gkv_d = B.dram_in("gkv", [2, 128, 1])
        ident_d = B.dram_in("ident", [128, 128])
        retdt_d = B.dram_in("ret_dt", [128, 4 * 128])
        retdec_d = B.dram_in("ret_dec", [128, 8])
        mask_d = B.dram_in("sgu_mask", [128, 128])
        invf_d = B.dram_in("invf", [128, 32])
        band_d = B.dram_in("pool_band", [128, 12 * 128])
        out_d = nc.dram_tensor("out", [SEQ, D], F32, kind="ExternalOutput").ap()
        xsp_d = nc.dram_tensor("xspill", [128, NT, D], F32, kind="Internal").ap()
        xl_d = nc.dram_tensor("xlayer", [2, 128, NT, D], F32, kind="Internal").ap()
        xtl_d = nc.dram_tensor("xtlayer", [2, 128, 8, HALF], BF16, kind="Internal").ap()
        kt_d = nc.dram_tensor("ktspill", [128, 4, HALF], BF16, kind="Internal").ap()
        vp_d = nc.dram_tensor("vpspill", [128, NT, 260], BF16, kind="Internal").ap()
        dbg_d = nc.dram_tensor("dbg", [HALF, D], F32, kind="ExternalOutput").ap() if DEBUG else None

        with ExitStack() as st:
            ec = st.enter_context

            def sb(name, shape, dt=F32):
                return ec(nc.sbuf_tensor(name, shape, dt))

            XT = sb("XT", [128, 8, HALF], BF16)
            self.identb = sb("identb", [128, 128], BF16)
            ST6 = [sb("ST6_%d" % i, [128, 2, 6]) for i in range(2)]
            MV = [sb("MV%d" % i, [128, 4]) for i in range(2)]
            COS = sb("COS", [128, 32, 32])
            SIN = sb("SIN", [128, 32, 32])
            DT = sb("DT", [128, 4, 128], BF16)
            DEC = sb("DEC", [128, 8])
            MASK = sb("MASK", [128, 128], BF16)
            WST = sb("WST", [128, 4, 128], BF16)
            SGN = sb("SGN", [128, 2, 256])
            SGB = sb("SGB", [128, 4])
            BAND = sb("BAND", [128, 12, 128], BF16)
            WPOOL = sb("WPOOL", [128, 2, 128], BF16)
            PSC = sb("PSC", [128, 2])
            WUQ = sb("WUQ", [128, 2, 384], BF16)
            GQ = sb("GQ", [128, 2])
            WUKV = sb("WUKV", [128, 512], BF16)
            GKV = sb("GKV", [128, 1])
            STATE = sb("STATE", [64, 4, 64])
            STATEB = sb("STATEB", [64, 4, 64], BF16)
            ZB = [sb("ZB%d" % i, [128, 256], BF16) for i in range(2)]
            SM4 = sb("SM4", [128, 16])
            self.arena_bytes = 152 * 1024
            self.arena = sb("arena", [128, self.arena_bytes // 2], BF16)
            X = B.carve(0, [128, NT, D], F32)
            A0 = 65536
            pf = [ec(nc.psum_tensor("pf%d" % i, [128, 512], F32)) for i in range(7)]
            pbt = ec(nc.psum_tensor("pbt", [128, 1024], BF16))
            PF = lambda i: ("pf", i)

            B.dma("pool", self.identb[:], ident_d, r=[], w=["ident"])
            if not os.environ.get("MK_NOROPE"):
                B.dma("pool", MASK[:], mask_d, r=[], w=["mask"])
                B.dma("pool", BAND[:].rearrange("p a b -> p (a b)"), band_d, r=[], w=["band"])
                B.dma("pool", DT[:].rearrange("p a b -> p (a b)"), retdt_d, r=[], w=["dt"])
                B.dma("sp", DEC[:], retdec_d, r=[], w=["dec"])
                POSI = B.carve(A0, [128, 32], I32)
                POSF = B.carve(A0 + 128, [128, 32], F32)
                INVF = B.carve(A0 + 256, [128, 32], F32)
                ANG = B.carve(A0 + 4096, [128, 32, 32], F32)
                T_A = B.carve(A0 + 8192, [128, 32, 32], F32)
                T_I = B.carve(A0 + 12288, [128, 32, 32], I32)
                T_F = B.carve(A0 + 16384, [128, 32, 32], F32)
                B.dma("sp", POSI, pos_d, r=[], w=["posi"])
                B.dma("sp", INVF, invf_d, r=[], w=["invf"])
                B.op("dve", "tensor_copy", ["posi"], ["posf"], out=POSF, in_=POSI)
                B.op("dve", "tensor_tensor", ["posf", "invf"], ["ang"], out=ANG,
                     in0=_bc(POSF, [(1, 32), (0, 32)]), in1=_bc(INVF, [(0, 32), (1, 32)]), op=ALU.mult)
                C1 = 6.28125
                C2 = 2.0 * math.pi - 6.28125
                for tab, shift in ((SIN, 0.0), (COS, math.pi / 2)):
                    B.op("dve", "tensor_scalar_add", ["ang"], ["ta"], out=T_A, in0=ANG, scalar1=shift)
                    B.op("dve", "tensor_scalar_mul", ["ta"], ["tf"], out=T_F, in0=T_A, scalar1=1.0 / (2 * math.pi))
                    B.op("dve", "tensor_copy", ["tf"], ["ti"], out=T_I, in_=T_F)
                    B.op("dve", "tensor_copy", ["ti"], ["tf"], out=T_F, in_=T_I)
                    B.op("dve", "scalar_tensor_tensor", ["tf", "ta"], ["ta"], out=T_A, in0=T_F, scalar=-C1, in1=T_A,
                         op0=ALU.mult, op1=ALU.add)
                    B.op("dve", "scalar_tensor_tensor", ["tf", "ta"], ["ta"], out=T_A, in0=T_F, scalar=-C2, in1=T_A,
                         op0=ALU.mult, op1=ALU.add)
                    B.op("dve", "tensor_scalar", ["ta"], ["ta"], out=T_A, in0=T_A, scalar1=3.141592, scalar2=-3.141592,
                         op0=ALU.min, op1=ALU.max)
                    B.act(tab[:], T_A, ACT.Sin, ["ta"], ["rope"])
            S.barrier()

            def set_ln(off):
                self.LG = B.carve(off, [128, D], F32)
                self.LB = B.carve(off + 4096, [128, D], F32)
                self.XN = [B.carve(off + 8192 + i * 4096, [128, D], F32) for i in range(2)]
                self.XB = [B.carve(off + 16384 + i * 2048, [128, D], BF16) for i in range(2)]

            def ln_s1(tt):
                i2 = tt % 2
                xr = ("X", tt)
                for c2 in range(2):
                    B.op("dve", "bn_stats", [xr], [("st6", i2)], out=ST6[i2][:, c2, :], in_=X[:, tt, c2 * 512:(c2 + 1) * 512])
                B.op("dve", "bn_aggr", [("st6", i2)], [("mv", i2)], out=MV[i2][:, 0:2],
                     in_=ST6[i2][:].rearrange("p a b -> p (a b)"))
                B.op("dve", "tensor_scalar_add", [("mv", i2)], [("mv", i2)], out=MV[i2][:, 1:2], in0=MV[i2][:, 1:2], scalar1=LN_EPS)
                B.act(MV[i2][:, 2:3], MV[i2][:, 1:2], ACT.Sqrt, [("mv", i2)], [("mv", i2)])
                B.op("dve", "reciprocal", [("mv", i2)], [("mv", i2)], out=MV[i2][:, 2:3], in_=MV[i2][:, 2:3])

            def ln_s2(tt):
                i2 = tt % 2
                LG, LB, XN = self.LG, self.LB, self.XN
                B.op("dve", "tensor_scalar", [("X", tt), ("mv", i2)], [("xn", i2)], out=XN[i2], in0=X[:, tt, :],
                     scalar1=MV[i2][:, 0:1], scalar2=MV[i2][:, 2:3], op0=ALU.subtract, op1=ALU.mult)
                B.op("dve", "tensor_tensor", [("xn", i2), "LG"], [("xn", i2)], out=XN[i2], in0=XN[i2], in1=LG, op=ALU.mult)
                B.op("dve", "tensor_tensor", [("xn", i2), "LB"], [("xn", i2)], out=XN[i2], in0=XN[i2], in1=LB, op=ALU.add)

            def ln_s3(tt, seg_h, final):
                i2 = tt % 2
                XN, XB = self.XN, self.XB
                if final:
                    r0 = seg_h * HALF + tt * 128
                    B.dma("sp", out_d[r0:r0 + 128, :], XN[i2], r=[("xn", i2)], w=["out"])
                    return
                B.op("act", "mul", [("xn", i2)], [("X", tt)], out=X[:, tt, :], in_=XN[i2], mul=ALPHA)
                B.op("dve", "tensor_copy", [("xn", i2)], [("xb", i2)], out=XB[i2], in_=XN[i2])
                for kc in range(8):
                    B.tr(pbt[:, kc * 128:(kc + 1) * 128], XB[i2][:, kc * 128:(kc + 1) * 128], [("xb", i2)], ["pbt"])
                B.op("act", "copy", ["pbt"], [("XT", tt)], out=XT[:, :, tt * 128:(tt + 1) * 128],
                     in_=pbt[:].rearrange("p (k t) -> p k t", k=8))

            def ln_pipe(tt, seg_h, final):
                if tt < NT:
                    ln_s1(tt)
                if 0 <= tt - 1 < NT:
                    ln_s2(tt - 1)
                if 0 <= tt - 2 < NT:
                    ln_s3(tt - 2, seg_h, final)

            def load_ln(L, which):
                B.dma("sp", self.LG, lng_d[L, which], r=[], w=["LG"])
                B.dma("sp", self.LB, lnb_d[L, which], r=[], w=["LB"])

            def ffn(L, which, seg_h, final):
                lnw = 0 if which == 0 else 2
                S.barrier()
                HT = B.carve(A0, [128, 8, HALF], BF16)
                WU = [B.carve(A0 + 32768 + i * 4096, [128, 8, 256], BF16) for i in range(3)]
                WD = B.carve(A0 + 32768 + 12288, [128, 8, 1024], BF16)
                SIL = [B.carve(A0 + 32768 + 12288 + 16384 + i * 2048, [128, 512], F32) for i in range(2)]
                set_ln(131072)
                load_ln(L, lnw)
                ci = 0
                for si, (j0, nj) in enumerate(SPLITS):
                    last = si == len(SPLITS) - 1
                    for jc in range(nj):
                        j = j0 + jc
                        wb = WU[ci % 3]
                        wr = ("WU", ci % 3)
                        ci += 1
                        B.dma("pool", wb.rearrange("p a b -> p (a b)"), wup_d[which][L, j], r=[], w=[wr])
                        if jc == min(2, nj - 1):
                            for q in range(nj):
                                B.dma("pool", WD[:, q, :], wdn_d[which][L, :, j0 + q, :], r=[], w=[("WD", q)])
                        for tg in range(4):
                            pa = pf[(tg % 2) * 2]
                            pb_ = pf[(tg % 2) * 2 + 1]
                            ra = PF((tg % 2) * 2)
                            rb = PF((tg % 2) * 2 + 1)
                            tsl = slice(tg * 512, (tg + 1) * 512)
                            xr = [("XT", t) for t in range(tg * 4, tg * 4 + 4)]
                            for kc in range(8):
                                B.mm(pa[:], wb[:, kc, 0:128], XT[:, kc, tsl], kc == 0, kc == 7, r=[wr] + xr, w=[ra])
                            for kc in range(8):
                                B.mm(pb_[:], wb[:, kc, 128:256], XT[:, kc, tsl], kc == 0, kc == 7, r=[wr] + xr, w=[rb])
                            sl = SIL[tg % 2]
                            B.act(sl[:], pa[:], ACT.Silu, [ra], [("sil", tg % 2)])
                            B.op("dve", "tensor_tensor", [("sil", tg % 2), rb], [("HT", jc, tg)],
                                 out=HT[:, jc, tsl], in0=sl[:], in1=pb_[:], op=ALU.mult)
                    for tt in range(NT):
                        for nh in range(2):
                            po = pf[4 + nh]
                            rp = PF(4 + nh)
                            for jc in range(nj):
                                B.mm(po[:], HT[:, jc, tt * 128:(tt + 1) * 128], WD[:, jc, nh * 512:(nh + 1) * 512],
                                     jc == 0, jc == nj - 1, r=[("HT", jc, tt // 4), ("WD", jc)], w=[rp])
                            B.op("dve", "scalar_tensor_tensor", [rp, ("X", tt)], [("X", tt)],
                                 out=X[:, tt, nh * 512:(nh + 1) * 512], in0=po[:], scalar=0.5,
                                 in1=X[:, tt, nh * 512:(nh + 1) * 512], op0=ALU.mult, op1=ALU.add)
                        if last:
                            ln_pipe(tt, seg_h, final)
                    if last:
                        ln_pipe(NT, seg_h, final)
                        ln_pipe(NT + 1, seg_h, final)

            def ple(L, h):
                S.barrier()
                WPG = B.carve(A0, [128, 8, 1024], BF16)
                WPL = B.carve(A0 + 16384, [128, 2, 1024], BF16)
                PT = B.carve(A0 + 20480, [128, 2, HALF], BF16)
                SG = [B.carve(A0 + 28672 + i * 2048, [128, 512], F32) for i in range(3)]
                for kc in range(8):
                    B.dma("pool", WPG[:, kc, :], wpg_d[L, :, kc, :], r=[], w=[("wpg", kc)])
                for kc in range(2):
                    B.dma("pool", WPL[:, kc, :], wpl_d[L, :, kc, :], r=[], w=[("wpl", kc)])
                    B.dma("pool", PT[:, kc, :], pT_d[L, kc * 128:(kc + 1) * 128, h * HALF:(h + 1) * HALF], r=[], w=[("pt", kc)])
                for tt in range(NT):
                    for nh in range(2):
                        k2 = (tt * 2 + nh) % 3
                        pg, pp = pf[k2 * 2], pf[k2 * 2 + 1]
                        for kc in range(8):
                            B.mm(pg[:], XT[:, kc, tt * 128:(tt + 1) * 128], WPG[:, kc, nh * 512:(nh + 1) * 512], kc == 0, kc == 7,
                                 r=[("XT", tt), ("wpg", kc)], w=[PF(k2 * 2)])
                        for kc in range(2):
                            B.mm(pp[:], PT[:, kc, tt * 128:(tt + 1) * 128], WPL[:, kc, nh * 512:(nh + 1) * 512], kc == 0, kc == 1,
                                 r=[("pt", kc), ("wpl", kc)], w=[PF(k2 * 2 + 1)])
                        B.act(SG[k2][:], pg[:], ACT.Sigmoid, [PF(k2 * 2)], [("sg", k2)])
                        B.op("dve", "tensor_tensor", [("sg", k2), PF(k2 * 2 + 1)], [("sg", k2)], out=SG[k2][:], in0=SG[k2][:], in1=pp[:], op=ALU.mult)
                        B.op("dve", "tensor_tensor", [("sg", k2), ("X", tt)], [("X", tt)], out=X[:, tt, nh * 512:(nh + 1) * 512],
                             in0=X[:, tt, nh * 512:(nh + 1) * 512], in1=SG[k2][:], op=ALU.add)

            def load_layer_consts(L):
                S.barrier()
                TMPW = B.carve(A0, [128, 4, 128], F32)
                B.dma("sp", TMPW.rearrange("p a b -> p (a b)"), sguw_d[L], r=[], w=["tmpw"])
                B.op("dve", "tensor_tensor", ["tmpw", "mask"], ["wst"], out=WST[:], in0=TMPW,
                     in1=_bc(MASK[:], [(0, 4), (1, 128)]), op=ALU.mult)
                B.dma("sp", SGN[:, 0, :], sgun_d[L, 0], r=[], w=["sgn0"])
                B.dma("sp", SGN[:, 1, :], sgun_d[L, 1], r=[], w=["sgn1"])
                B.dma("sp", SGB[:], sgub_d[L], r=[], w=["sgb"])
                B.dma("pool", WPOOL[:].rearrange("p a b -> p (a b)"), poolw_d[L], r=[], w=["wpool"])
                B.dma("sp", PSC[:], poolsc_d[L], r=[], w=["psc"])
                TQ = B.carve(A0 + 4096, [128, 2, 384], F32)
                TKV = B.carve(A0 + 8192, [128, 512], F32)
                B.dma("sp", TQ.rearrange("p a b -> p (a b)"), wuq_d[L], r=[], w=["tq"])
                B.dma("sp", GQ[:], gq_d[L], r=[], w=["gq"])
                B.dma("sp", TKV, wukv_d[L], r=[], w=["tkv"])
                B.dma("sp", GKV[:], gkv_d[L], r=[], w=["gkv"])
                for kc in range(2):
                    B.op("dve", "tensor_scalar", ["tq", "gq"], [("wuq", kc)], out=WUQ[:, kc, :], in0=TQ[:, kc, :],
                         scalar1=GQ[:, kc:kc + 1], scalar2=96 ** -0.5, op0=ALU.mult, op1=ALU.mult)
                B.op("dve", "tensor_scalar", ["tkv", "gkv"], ["wukv"], out=WUKV[:], in0=TKV, scalar1=GKV[:, 0:1], scalar2=None, op0=ALU.mult)
                B.op("pool", "memset", [], ["state"], STATE[:], 0.0)
                B.op("pool", "memset", [], ["stateb"], STATEB[:], 0.0)
                S.barrier()

            def mixer(L, h):
                S.barrier()
                for tt in range(NT):
                    B.dma("sp", xsp_d[:, tt, :], X[:, tt, :], r=[], w=[("xsp", tt)])
                S.barrier()
                YS = B.carve(0, [128, 4, 2, HALF], BF16)
                KT = B.carve(32768, [128, 4, SEQ], BF16)
                WG = [B.carve(32768 + i * 8192, [128, 8, 512], BF16) for i in range(2)]
                WBRn = [B.carve(32768 + 16384 + i * 2048, [128, 8, 128], BF16) for i in range(2)]
                MS = [B.carve(32768 + 20480 + i * 2048, [128, 512], F32) for i in range(3)]
                VP = B.carve(65536, [128, 32, 260], BF16)
                MERG = B.carve(65536, [128, 8, HALF], BF16)
                WO = B.carve(98304, [128, 8, 1024], BF16)
                WTM = B.carve(65536 + 16640, [128, 8, 2208], BF16)
                o = 65536 + 16640 + 35328
                QT = B.carve(o, [128, 4, 512], BF16); o += 4096
                ET = [B.carve(o + i * 1024, [128, 512], BF16) for i in range(3)]; o += 3072
                t0 = o

                def tmp(shape, dt):
                    nonlocal o
                    a = B.carve(o, shape, dt)
                    o += int(np.prod(shape[1:])) * (4 if dt in (F32, I32) else 2)
                    o = (o + 31) // 32 * 32
                    return a

                GU = tmp([128, 256], F32); GV = tmp([128, 256], F32); VBs = tmp([128, 256], BF16); Y0 = tmp([128, 256], BF16)
                QKR = tmp([128, 8, 64], F32); RT = [tmp([128, 8, 32], F32) for _ in range(4)]
                QKB = tmp([128, 8, 64], BF16); QSB = tmp([128, 4, 64], BF16); KDB = tmp([128, 4, 64], BF16)
                VBr = tmp([128, 256], BF16); TRS = tmp([128, 12, 128], BF16); SMK = tmp([128, 4, 128], BF16)
                YN = tmp([128, 256], F32); SGT = tmp([128, 256], F32); Y1 = tmp([128, 256], BF16)
                GNS = tmp([128, 4, 6], F32); GNM = tmp([128, 4, 2], F32); GNR = tmp([128, 4], F32)
                PLT = tmp([128, 2, 128], BF16)
                SQJ = tmp([128, 256], F32); CQN = tmp([128, 256], BF16); CKVN = tmp([128, 128], BF16)
                CT = tmp([128, 3, 128], BF16); QF = tmp([128, 4, 96], BF16); KF = tmp([128, 4, 96], BF16)
                MR = [tmp([128, 4, 16], F32) for _ in range(4)]; KR_ = [tmp([128, 16], F32) for _ in range(4)]
                KPE = tmp([128, 32], BF16); RD = tmp([128, 4], F32); Y3 = tmp([128, 4, 64], BF16)
                assert o <= self.arena_bytes, o

                for kc in range(8):
                    B.dma("pool", WTM[:, kc, 0:2048], wtm_d[L, :, kc, 0:2048], r=[], w=[("wtm", kc, 0)])
                    B.dma("pool", WTM[:, kc, 2048:2208], wtm_d[L, :, kc, 2048:2208], r=[], w=[("wtm", kc, 1)])
                B.op("pool", "memset", [], ["vp1"], VP[:, h * NT:(h + 1) * NT, :], 1.0)
                if h == 1:
                    for hd in range(4):
                        B.dma("sp", KT[0:96, hd, 0:HALF], kt_d[0:96, hd, :], r=[], w=[("ktl", hd)])
                    B.dma("sp", VP[:, 0:NT, :], vp_d, r=[], w=["vpl"])
                S.barrier()
                RW = lambda blk: [("wtm", kc_, i_) for kc_ in range(8) for i_ in range(2)]
                BC0 = [0, 512, 1024, 1536, 1792]

                def proj_tm(tt, blk, bank, ncols=512):
                    for kc in range(8):
                        B.mm(pf[bank][:, 0:ncols], XT[:, kc, tt * 128:(tt + 1) * 128], WTM[:, kc, BC0[blk]:BC0[blk] + ncols], kc == 0, kc == 7,
                             r=[("XT", tt)] + RW(blk), w=[PF(bank)])

                def rstd_of(var_ap, eps, res):
                    B.op("dve", "tensor_scalar_add", [res], [res], out=var_ap, in0=var_ap, scalar1=eps)
                    B.act(var_ap, var_ap, ACT.Sqrt, [res], [res])
                    B.op("dve", "reciprocal", [res], [res], out=var_ap, in_=var_ap)

                def to_fm(src_bf, dst, r, w, col0=0):
                    n = dst.shape[1]
                    for c in range(n):
                        B.tr(pbt[:, col0 + c * 128: col0 + (c + 1) * 128], src_bf[:, c * 128:(c + 1) * 128], r, ["pbt"])
                    B.op("act", "copy", ["pbt"], w, out=dst, in_=pbt[:, col0:col0 + n * 128].rearrange("p (k t) -> p k t", k=n))

                SKIP = os.environ.get("MK_SKIP", "").split(",")
                STOPT = int(os.environ.get("MK_STOPT", "99"))
                for tt in range(NT):
                    if tt > STOPT:
                        break
                    gt = h * NT + tt
                    tq = tt % 4
                    tsl = slice(tt * 128, (tt + 1) * 128)
                    cosr = _bc(COS[:, gt, :], [(0, 8), (1, 32)])
                    sinr = _bc(SIN[:, gt, :], [(0, 8), (1, 32)])
                    if 'sgu' not in SKIP:
                        proj_tm(tt, 0, 0)
                        B.act(GU[:], pf[0][:, 0:256], ACT.Gelu, [PF(0)], ["gu"])
                        B.act(GV[:], pf[0][:, 256:512], ACT.Gelu, [PF(0)], ["gv"])
                        B.op("dve", "bn_stats", ["gv"], ["gns"], out=GNS[:, 0, :], in_=GV[:])
                        B.op("dve", "bn_aggr", ["gns"], ["gnm"], out=GNM[:, 0, :], in_=GNS[:, 0, :])
                        rstd_of(GNM[:, 0, 1:2], LN_EPS, "gnm")
                        B.op("dve", "tensor_scalar", ["gv", "gnm"], ["gv"], out=GV[:], in0=GV[:], scalar1=GNM[:, 0, 0:1],
                             scalar2=GNM[:, 0, 1:2], op0=ALU.subtract, op1=ALU.mult)
                        B.op("dve", "tensor_tensor", ["gv", "sgn0"], ["gv"], out=GV[:], in0=GV[:], in1=SGN[:, 0, :], op=ALU.mult)
                        B.op("dve", "tensor_tensor", ["gv", "sgn1"], ["vbs"], out=VBs[:], in0=GV[:], in1=SGN[:, 1, :], op=ALU.add)
                        for g in range(4):
                            B.mm(pf[1][:, g * 64:(g + 1) * 64], WST[:, g, :], VBs[:, g * 64:(g + 1) * 64], True, True,
                                 r=["vbs", "wst"], w=[PF(1)])
                        for g in range(4):
                            B.op("dve", "scalar_tensor_tensor", [PF(1), "gu", "sgb"], ["y0"], out=Y0[:, g * 64:(g + 1) * 64],
                                 in0=pf[1][:, g * 64:(g + 1) * 64], scalar=SGB[:, g:g + 1], in1=GU[:, g * 64:(g + 1) * 64],
                                 op0=ALU.add, op1=ALU.mult)
                        to_fm(Y0, YS[:, 0, :, tsl], ["y0"], [("ys", 0, tt)])
                    if 'ret' not in SKIP:
                        proj_tm(tt, 1, 2)
                        proj_tm(tt, 2, 3)
                        qk = pf[2][:].rearrange("p (h two d) -> p h two d", h=8, two=2)
                        x1, x2 = qk[:, :, 0, :], qk[:, :, 1, :]
                        qkr = QKR[:].rearrange("p h (two d) -> p h two d", two=2)
                        B.op("dve", "tensor_tensor", [PF(2)], ["rt0"], out=RT[0][:], in0=x1, in1=cosr, op=ALU.mult)
                        B.op("dve", "tensor_tensor", [PF(2)], ["rt1"], out=RT[1][:], in0=x2, in1=sinr, op=ALU.mult)
                        B.op("dve", "tensor_tensor", [PF(2)], ["rt2"], out=RT[2][:], in0=x1, in1=sinr, op=ALU.mult)
                        B.op("dve", "tensor_tensor", [PF(2)], ["rt3"], out=RT[3][:], in0=x2, in1=cosr, op=ALU.mult)
                        B.op("dve", "tensor_tensor", ["rt0", "rt1"], ["qkr0"], out=qkr[:, :, 0, :], in0=RT[0][:], in1=RT[1][:], op=ALU.subtract)
                        B.op("dve", "tensor_tensor", ["rt2", "rt3"], ["qkr1"], out=qkr[:, :, 1, :], in0=RT[2][:], in1=RT[3][:], op=ALU.add)
                        B.op("dve", "tensor_copy", ["qkr0", "qkr1"], ["qkb"], out=QKB[:], in_=QKR[:])
                        for hd in range(4):
                            B.op("dve", "tensor_scalar", ["qkr0", "qkr1", "dec"], ["qsb"], out=QSB[:, hd, :], in0=QKR[:, hd, :],
                                 scalar1=DEC[:, hd:hd + 1], scalar2=None, op0=ALU.mult)
                            B.op("dve", "tensor_scalar", ["qkr0", "qkr1", "dec"], ["kdb"], out=KDB[:, hd, :], in0=QKR[:, 4 + hd, :],
                                 scalar1=DEC[:, 4 + hd:5 + hd], scalar2=None, op0=ALU.mult)
                        B.op("act", "copy", [PF(3)], ["vbr"], out=VBr[:], in_=pf[3][:, 0:256])
                        B.act(SGT[:], pf[3][:, 256:512], ACT.Silu, [PF(3)], ["sgt"])
                        if int(os.environ.get("MK_RETCUT", "9")) <= 1:
                            continue
                        for c in range(8):
                            B.tr(pbt[0:64, c * 128:(c + 1) * 128], QKB[:, c, :], ["qkb"], ["pbt"])
                        B.op("act", "copy", ["pbt"], ["trs"], out=TRS[0:64, 0:8, :], in_=pbt[0:64, :].rearrange("p (k t) -> p k t", k=8))
                        for c in range(4):
                            B.tr(pbt[0:64, c * 128:(c + 1) * 128], QSB[:, c, :], ["qsb"], ["pbt"])
                        B.op("act", "copy", ["pbt"], ["trs2"], out=TRS[0:64, 8:12, :], in_=pbt[0:64, 0:512].rearrange("p (k t) -> p k t", k=4))
                        if int(os.environ.get("MK_RETCUT", "9")) <= 2:
                            continue
                        for hd in range(4):
                            B.mm(pf[4][:, hd * 128:(hd + 1) * 128], TRS[0:64, 4 + hd, :], TRS[0:64, hd, :], True, True,
                                 r=["trs"], w=[PF(4)])
                        B.op("dve", "tensor_tensor", [PF(4), "dt"], ["smk"], out=SMK[:].rearrange("p a b -> p (a b)"), in0=pf[4][:],
                             in1=DT[:].rearrange("p a b -> p (a b)"), op=ALU.mult)
                        if int(os.environ.get("MK_RETCUT", "9")) <= 3:
                            continue
                        for hd in range(4):
                            B.mm(pf[5][:, hd * 64:(hd + 1) * 64], SMK[:, hd, :], VBr[:, hd * 64:(hd + 1) * 64], True, False,
                                 r=["smk", "vbr"], w=[PF(5)])
                            B.mm(pf[5][:, hd * 64:(hd + 1) * 64], TRS[0:64, 8 + hd, :], STATEB[:, hd, :], False, True,
                                 r=["trs2", "stateb"], w=[PF(5)])
                        if int(os.environ.get("MK_RETCUT", "9")) <= 4:
                            continue
                        for hd in range(4):
                            B.mm(pf[6][0:64, hd * 64:(hd + 1) * 64], KDB[:, hd, :], VBr[:, hd * 64:(hd + 1) * 64], True, True,
                                 r=["kdb", "vbr"], w=[PF(6)])
                        for hd in range(4):
                            B.op("dve", "scalar_tensor_tensor", [PF(6), "state"], ["state"], out=STATE[:, hd, :],
                                 in0=STATE[:, hd, :], scalar=G128[hd], in1=pf[6][0:64, hd * 64:(hd + 1) * 64],
                                 op0=ALU.mult, op1=ALU.add)
                        B.op("dve", "tensor_copy", ["state"], ["stateb"], out=STATEB[:], in_=STATE[:])
                        if int(os.environ.get("MK_RETCUT", "9")) <= 5:
                            continue
                        for hd in range(4):
                            B.op("dve", "bn_stats", [PF(5)], ["gns"], out=GNS[:, hd, :], in_=pf[5][:, hd * 64:(hd + 1) * 64])
                        for hd in range(4):
                            B.op("dve", "bn_aggr", ["gns"], ["gnm"], out=GNM[:, hd, :], in_=GNS[:, hd, :])
                        B.op("dve", "tensor_copy", ["gnm"], ["gnr"], out=GNR[:], in_=GNM[:, :, 1])
                        rstd_of(GNR[:], GN_EPS, "gnr")
                        for hd in range(4):
                            B.op("dve", "tensor_scalar", [PF(5), "gnm", "gnr"], ["yn"], out=YN[:, hd * 64:(hd + 1) * 64],
                                 in0=pf[5][:, hd * 64:(hd + 1) * 64], scalar1=GNM[:, hd, 0:1], scalar2=GNR[:, hd:hd + 1],
                                 op0=ALU.subtract, op1=ALU.mult)
                        B.op("dve", "tensor_tensor", ["yn", "sgt"], ["y1"], out=Y1[:], in0=YN[:], in1=SGT[:], op=ALU.mult)
                        to_fm(Y1, YS[:, 1, :, tsl], ["y1"], [("ys", 1, tt)])
                    if 'pool' not in SKIP:
                        proj_tm(tt, 3, 0, 256)
                        zc, zp = ZB[gt % 2], ZB[(gt + 1) % 2]
                        B.op("act", "copy", [PF(0)], [("zb", gt % 2)], out=zc[:], in_=pf[0][:, 0:256])
                        for c in range(2):
                            for g2 in range(2):
                                wi = c * 2 + g2
                                osl = pf[1][g2 * 64:(g2 + 1) * 64, c * 128:(c + 1) * 128]
                                kind = 2 if gt == 0 else 0
                                B.mm(osl, zc[:, wi * 64:(wi + 1) * 64], BAND[:, kind * 4 + wi, :], True, gt == 0,
                                     r=[("zb", gt % 2), "band"], w=[PF(1)], tile_position=(0, g2 * 64))
                                if gt > 0:
                                    B.mm(osl, zp[:, wi * 64:(wi + 1) * 64], BAND[:, 4 + wi, :], False, True,
                                         r=[("zb", (gt + 1) % 2), "band"], w=[PF(1)], tile_position=(0, g2 * 64))
                        B.op("act", "copy", [PF(1)], ["plt"], out=PLT[:], in_=pf[1][:, 0:256].rearrange("p (c t) -> p c t", c=2))
                        for c in range(2):
                            B.mm(pf[0][:, 256 + c * 128:256 + (c + 1) * 128], WPOOL[:, c, :], PLT[:, c, :], True, True, r=["plt", "wpool"], w=[PF(0)])
                            B.op("dve", "tensor_scalar", [PF(0), "psc"], [("ys", 2, tt)], out=YS[:, 2, c, tsl],
                                 in0=pf[0][:, 256 + c * 128:256 + (c + 1) * 128], scalar1=PSC[:, c:c + 1], scalar2=None, op0=ALU.mult)
                    if 'mla' not in SKIP:
                        proj_tm(tt, 4, 2, 416)
                        B.act(SQJ[:, 0:256], pf[2][:, 0:256], ACT.Square, [PF(2)], ["sm4"], accum_out=SM4[:, 0:1])
                        B.act(SQJ[:, 0:128], pf[2][:, 256:384], ACT.Square, [PF(2), "sm4"], ["sm4"], accum_out=SM4[:, 1:2])
                        B.op("dve", "tensor_scalar", ["sm4"], ["sm4"], out=SM4[:, 0:1], in0=SM4[:, 0:1], scalar1=1.0 / 256, scalar2=RMS_EPS, op0=ALU.mult, op1=ALU.add)
                        B.op("dve", "tensor_scalar", ["sm4"], ["sm4"], out=SM4[:, 1:2], in0=SM4[:, 1:2], scalar1=1.0 / 128, scalar2=RMS_EPS, op0=ALU.mult, op1=ALU.add)
                        B.act(SM4[:, 0:2], SM4[:, 0:2], ACT.Sqrt, ["sm4"], ["sm4"])
                        B.op("dve", "reciprocal", ["sm4"], ["sm4"], out=SM4[:, 0:2], in_=SM4[:, 0:2])
                        B.op("dve", "tensor_scalar", [PF(2), "sm4"], ["cqn"], out=CQN[:], in0=pf[2][:, 0:256], scalar1=SM4[:, 0:1], scalar2=None, op0=ALU.mult)
                        B.op("dve", "tensor_scalar", [PF(2), "sm4"], ["ckvn"], out=CKVN[:], in0=pf[2][:, 256:384], scalar1=SM4[:, 1:2], scalar2=None, op0=ALU.mult)
                        if int(os.environ.get("MK_MLACUT", "9")) <= 1:
                            continue
                        kr = pf[2][:, 384:416]
                        cm = _bc(COS[:, gt, :], [(2, 16)])
                        sm_ = _bc(SIN[:, gt, :], [(2, 16)])
                        B.op("dve", "tensor_tensor", [PF(2)], ["kr0"], out=KR_[0][:], in0=kr[:, 0:16], in1=cm, op=ALU.mult)
                        B.op("dve", "tensor_tensor", [PF(2)], ["kr1"], out=KR_[1][:], in0=kr[:, 16:32], in1=sm_, op=ALU.mult)
                        B.op("dve", "tensor_tensor", [PF(2)], ["kr2"], out=KR_[2][:], in0=kr[:, 0:16], in1=sm_, op=ALU.mult)
                        B.op("dve", "tensor_tensor", [PF(2)], ["kr3"], out=KR_[3][:], in0=kr[:, 16:32], in1=cm, op=ALU.mult)
                        B.op("dve", "tensor_tensor", ["kr0", "kr1"], ["kpe0"], out=KPE[:, 0:16], in0=KR_[0][:], in1=KR_[1][:], op=ALU.subtract)
                        B.op("dve", "tensor_tensor", ["kr2", "kr3"], ["kpe1"], out=KPE[:, 16:32], in0=KR_[2][:], in1=KR_[3][:], op=ALU.add)
                        if int(os.environ.get("MK_MLACUT", "9")) <= 2:
                            continue
                        for c in range(2):
                            B.tr(pbt[:, c * 128:(c + 1) * 128], CQN[:, c * 128:(c + 1) * 128], ["cqn"], ["pbt"])
                        B.tr(pbt[:, 256:384], CKVN[:], ["ckvn"], ["pbt"])
                        B.op("act", "copy", ["pbt"], ["ct"], out=CT[:], in_=pbt[:, 0:384].rearrange("p (k t) -> p k t", k=3))
                        for kc in range(2):
                            B.mm(pf[3][:, 0:384], CT[:, kc, :], WUQ[:, kc, :], kc == 0, kc == 1, r=["ct", ("wuq", 0), ("wuq", 1)], w=[PF(3)])
                        B.mm(pf[4][:], CT[:, 2, :], WUKV[:], True, True, r=["ct", "wukv"], w=[PF(4)])
                        if int(os.environ.get("MK_MLACUT", "9")) <= 3:
                            continue
                        q4 = pf[3][:, 0:384].rearrange("p (h d) -> p h d", h=4)
                        cm4 = _bc(COS[:, gt, :], [(0, 4), (2, 16)])
                        sm4_ = _bc(SIN[:, gt, :], [(0, 4), (2, 16)])
                        B.op("dve", "tensor_tensor", [PF(3)], ["mr0"], out=MR[0][:], in0=q4[:, :, 0:16], in1=cm4, op=ALU.mult)
                        B.op("dve", "tensor_tensor", [PF(3)], ["mr1"], out=MR[1][:], in0=q4[:, :, 16:32], in1=sm4_, op=ALU.mult)
                        B.op("dve", "tensor_tensor", [PF(3)], ["mr2"], out=MR[2][:], in0=q4[:, :, 0:16], in1=sm4_, op=ALU.mult)
                        B.op("dve", "tensor_tensor", [PF(3)], ["mr3"], out=MR[3][:], in0=q4[:, :, 16:32], in1=cm4, op=ALU.mult)
                        B.op("dve", "tensor_tensor", ["mr0", "mr1"], ["qf0"], out=QF[:, :, 0:16], in0=MR[0][:], in1=MR[1][:], op=ALU.subtract)
                        B.op("dve", "tensor_tensor", ["mr2", "mr3"], ["qf1"], out=QF[:, :, 16:32], in0=MR[2][:], in1=MR[3][:], op=ALU.add)
                        for hd in range(4):
                            B.op("act", "copy", [PF(3)], ["qf2"], out=QF[:, hd, 32:96], in_=pf[3][:, hd * 96 + 32:hd * 96 + 96])
                        for hd in range(4):
                            B.op("dve", "tensor_copy", ["kpe0", "kpe1"], ["kf0"], out=KF[:, hd, 0:32], in_=KPE[:])
                        for hd in range(4):
                            B.op("act", "copy", [PF(4)], ["kf1"], out=KF[:, hd, 32:96], in_=pf[4][:, hd * 64:(hd + 1) * 64])
                        for hd in range(4):
                            B.op("dve", "tensor_copy", [PF(4)], [("vp", gt)], out=VP[:, gt, hd * 65:hd * 65 + 64], in_=pf[4][:, 256 + hd * 64:256 + (hd + 1) * 64])
                        if int(os.environ.get("MK_MLACUT", "9")) <= 4:
                            continue
                        for hd in range(4):
                            B.tr(pbt[0:96, hd * 128:(hd + 1) * 128], QF[:, hd, :], ["qf0", "qf1", "qf2"], ["pbt"])
                        for hd in range(4):
                            B.tr(pbt[0:96, (4 + hd) * 128:(5 + hd) * 128], KF[:, hd, :], ["kf0", "kf1"], ["pbt"])
                        tq = tt % 4
                        B.op("act", "copy", ["pbt"], [("qt", tq)], out=QT[0:96, :, tq * 128:(tq + 1) * 128],
                             in_=pbt[0:96, 0:512].rearrange("p (k t) -> p k t", k=4))
                        B.op("act", "copy", ["pbt"], [("kt", gt)], out=KT[0:96, :, gt * 128:(gt + 1) * 128],
                             in_=pbt[0:96, 512:1024].rearrange("p (k t) -> p k t", k=4))
                    if 'attn' not in SKIP:
                        if tq == 3:
                            Qb = tt // 4
                            g0 = h * NT + Qb * 4
                            ob = [3, 4, 5, 6]
                            ei = 0
                            for hd in range(4):
                                for ki in range(g0 + 4):
                                    rr_ = ki - g0
                                    q0 = max(rr_, 0)
                                    ncol = (4 - q0) * 128
                                    sb_ = ei % 2
                                    et = ET[ei % 3]
                                    er = ("et", ei % 3)
                                    ei += 1
                                    B.mm(pf[sb_][:, 0:ncol], KT[0:96, hd, ki * 128:(ki + 1) * 128], QT[0:96, hd, q0 * 128:512], True, True,
                                         r=[("kt", ki), ("ktl", hd)] + [("qt", q) for q in range(q0, 4)], w=[PF(sb_)])
                                    B.act(et[:, 0:ncol], pf[sb_][:, 0:ncol], ACT.Exp, [PF(sb_)], [er])
                                    if rr_ >= 0:
                                        B.op("pool", "memset", [er], [er], et[64:128, 0:64], 0.0)
                                    for qs in range(q0, 4):
                                        B.mm(pf[ob[qs]][:, hd * 65:(hd + 1) * 65], et[:, (qs - q0) * 128:(qs - q0 + 1) * 128], VP[:, ki, hd * 65:(hd + 1) * 65],
                                             ki == 0, ki == g0 + qs, r=[er, ("vp", ki), "vp1", "vpl"], w=[PF(ob[qs])])
                            for qs in range(4):
                                t2 = Qb * 4 + qs
                                o4 = pf[ob[qs]][:, 0:260].rearrange("p (h e) -> p h e", h=4)
                                B.op("dve", "reciprocal", [PF(ob[qs])], ["rd"], out=RD[:], in_=o4[:, :, 64])
                                for hd in range(4):
                                    B.op("dve", "tensor_scalar", [PF(ob[qs]), "rd"], ["y3"], out=Y3[:, hd, :], in0=pf[ob[qs]][:, hd * 65:hd * 65 + 64],
                                         scalar1=RD[:, hd:hd + 1], scalar2=None, op0=ALU.mult)
                                to_fm(Y3[:].rearrange("p h e -> p (h e)"), YS[:, 3, :, t2 * 128:(t2 + 1) * 128], ["y3"], [("ys", 3, t2)])
                if STOPT < 99:
                    S.barrier()
                    return
                if h == 0:
                    S.barrier()
                    for hd in range(4):
                        B.dma("sp", kt_d[0:96, hd, :], KT[0:96, hd, 0:HALF], r=[], w=[("ktd", hd)])
                    B.dma("sp", vp_d, VP[:, 0:NT, :], r=[], w=["vpd"])
                S.barrier()
                gi = 0
                for n_ in range(8):
                    wg = WG[n_ % 2]
                    for kc in range(0, 8, 4):
                        B.dma("pool", wg[:, kc:kc + 4, :], wg_d[L, n_, :, kc:kc + 4, :], r=[], w=[("wg", n_ % 2, kc)])
                    wbn = WBRn[n_ % 2]
                    B.dma("pool", wbn, wbr_d[L, :, :, n_ * 128:(n_ + 1) * 128], r=[], w=[("wbr", n_ % 2)])
                    for tg in range(4):
                        tsl = slice(tg * 512, (tg + 1) * 512)
                        xr = [("XT", t) for t in range(tg * 4, tg * 4 + 4)]
                        ms = MS[(n_ * 4 + tg) % 2]
                        msr = ("ms", (n_ * 4 + tg) % 2)
                        for b in range(4):
                            k2 = gi % 2
                            gi += 1
                            pg, pp = pf[k2 * 2], pf[k2 * 2 + 1]
                            for kc in range(8):
                                B.mm(pg[:], wg[:, kc, b * 128:(b + 1) * 128], XT[:, kc, tsl], kc == 0, kc == 7,
                                     r=xr + [("wg", n_ % 2, 0), ("wg", n_ % 2, 4)], w=[PF(k2 * 2)])
                            for kc in range(2):
                                B.mm(pp[:], wbn[:, b * 2 + kc, :], YS[:, b, kc, tsl], kc == 0, kc == 1,
                                     r=[("wbr", n_ % 2)], w=[PF(k2 * 2 + 1)])
                            sg = MS[2]
                            B.act(sg[:], pg[:], ACT.Sigmoid, [PF(k2 * 2)], ["sgm"])
                            if b == 0:
                                B.op("dve", "tensor_tensor", ["sgm", PF(k2 * 2 + 1)], [msr], out=ms[:], in0=sg[:], in1=pp[:], op=ALU.mult)
                            else:
                                B.op("dve", "tensor_tensor", ["sgm", PF(k2 * 2 + 1)], ["sgm"], out=sg[:], in0=sg[:], in1=pp[:], op=ALU.mult)
                                if b < 3:
                                    B.op("pool", "tensor_tensor", ["sgm", msr], [msr], out=ms[:], in0=ms[:], in1=sg[:], op=ALU.add)
                                else:
                                    B.op("pool", "tensor_tensor", ["sgm", msr], [("merg", n_, tg)], out=MERG[:, n_, tsl], in0=ms[:], in1=sg[:], op=ALU.add)
                S.barrier()
                for kc in range(8):
                    B.dma("pool", WO[:, kc, :], wout_d[L, :, kc, :], r=[], w=[("wo", kc)])
                for tt in range(NT):
                    B.dma("sp", X[:, tt, :], xsp_d[:, tt, :], r=[], w=[("X", tt)])
                set_ln(114688)
                load_ln(L, 1)
                for tt in range(NT):
                    for nh in range(2):
                        po = pf[4 + nh]
                        for kc in range(8):
                            B.mm(po[:], MERG[:, kc, tt * 128:(tt + 1) * 128], WO[:, kc, nh * 512:(nh + 1) * 512], kc == 0, kc == 7,
                                 r=[("wo", kc)], w=[PF(4 + nh)])
                        B.op("dve", "tensor_tensor", [PF(4 + nh), ("X", tt)], [("X", tt)], out=X[:, tt, nh * 512:(nh + 1) * 512],
                             in0=X[:, tt, nh * 512:(nh + 1) * 512], in1=po[:], op=ALU.add)
                    ln_pipe(tt, h, False)
                ln_pipe(NT, h, False)
                ln_pipe(NT + 1, h, False)

            def dump_dbg():
                S.barrier()
                for tt in range(NT):
                    B.dma("sp", dbg_d[tt * 128:(tt + 1) * 128, :], X[:, tt, :], r=[], w=["dbg"])
                S.barrier()

            stages = DEBUG.split(",") if DEBUG else []
            for L in range(2):
                if not os.environ.get("MK_NOCONST"):
                    load_layer_consts(L)
                for h in range(2):
                    S.barrier()
                    if L == 0:
                        for kc in range(8):
                            B.dma("pool", XT[:, kc, :], xT_d[kc * 128:(kc + 1) * 128, h * HALF:(h + 1) * HALF], r=[], w=[("XTl", kc)])
                        for tt in range(NT):
                            r0 = h * HALF + tt * 128
                            B.dma("sp", X[:, tt, :], x_d[r0:r0 + 128, :], r=[], w=[("X", tt)])
                            B.op("act", "mul", [("X", tt)], [("X", tt)], out=X[:, tt, :], in_=X[:, tt, :], mul=ALPHA)
                    else:
                        for tt in range(NT):
                            B.dma("sp", X[:, tt, :], xl_d[h, :, tt, :], r=[], w=[("X", tt)])
                        for kc in range(8):
                            B.dma("sp", XT[:, kc, :], xtl_d[h, :, kc, :], r=[], w=[("XTl", kc)])
                    S.barrier()
                    def stop_here(tag):
                        if stages and stages[-1] == tag % (L, h):
                            dump_dbg()
                            S.barrier()
                            S.emit(nc, st)
                            return True
                        return False

                    ffn(L, 0, h, False)
                    if stop_here("ffn1_%d%d"):
                        return nc
                    mixer(L, h)
                    if stop_here("mix_%d%d"):
                        return nc
                    ple(L, h)
                    if stop_here("ple_%d%d"):
                        return nc
                    ffn(L, 1, h, L == 1)
                    if stop_here("end_%d%d"):
                        return nc
                    if L == 0:
                        S.barrier()
                        for tt in range(NT):
                            B.dma("sp", xl_d[h, :, tt, :], X[:, tt, :], r=[], w=[("xl", tt)])
                        for kc in range(8):
                            B.dma("sp", xtl_d[h, :, kc, :], XT[:, kc, :], r=[], w=[("xtl", kc)])
            S.barrier()
            S.emit(nc, st)
        return nc


_CACHE = {}


def prep_inputs(inputs):
    f = lambda a: np.ascontiguousarray(np.asarray(a, dtype=np.float32))
    hc = host_consts()
    shared = {k: hc[k] for k in ("ident", "ret_dec", "sgu_mask", "invf", "pool_band")}
    shared["ret_dt"] = f(hc["ret_dt"].reshape(128, 512))
    g = lambda k: np.asarray(inputs[k])

    def up(W):
        W = np.asarray(W).reshape(2, 8, 128, 2, 22, 128)
        return f(W.transpose(0, 4, 2, 1, 3, 5).reshape(2, 22, 128, 8 * 256))

    def dn(W):
        return f(np.asarray(W).reshape(2, 22, 128, 1024).transpose(0, 2, 1, 3))

    def kp(W, K):
        W = np.asarray(W)
        return f(W.reshape(2, K, 128, W.shape[-1]).transpose(0, 2, 1, 3))

    shared["wup1"] = up(g("ffn1_up")); shared["wdn1"] = dn(g("ffn1_down"))
    shared["wup2"] = up(g("ffn2_up")); shared["wdn2"] = dn(g("ffn2_down"))
    bc = lambda a: f(np.broadcast_to(np.asarray(a)[:, :, None, :], (2, a.shape[1], 128, a.shape[2])))
    shared["lng"] = bc(np.stack([g("ln1_g"), g("ln2_g"), g("ln3_g")], axis=1))
    shared["lnb"] = bc(np.stack([g("ln1_b"), g("ln2_b"), g("ln3_b")], axis=1))
    shared["wpg"] = kp(g("w_ple_gate"), 8)
    shared["wpl"] = kp(g("w_ple"), 2)
    win = g("w_in")
    shared["wtm"] = f(win[:, :, 0:2208].reshape(2, 8, 128, 2208).transpose(0, 2, 1, 3))
    wgt = win[:, :, 2208:].reshape(2, 8, 128, 4, 8, 128)
    shared["wg"] = f(wgt.transpose(0, 4, 2, 1, 3, 5).reshape(2, 8, 128, 8, 512))
    wb = g("w_branch").reshape(2, 4, 2, 128, 1024)
    shared["wbr"] = f(wb.transpose(0, 3, 1, 2, 4).reshape(2, 128, 8, 1024))
    shared["wout"] = kp(g("w_out"), 8)
    shared["sguw"] = f(g("sgu_w").transpose(0, 3, 1, 2).reshape(2, 128, 512))
    shared["sgun"] = f(np.broadcast_to(np.stack([g("sgu_ln_g"), g("sgu_ln_b")], axis=1)[:, :, None, :], (2, 2, 128, 256)))
    shared["sgub"] = f(g("sgu_b").transpose(0, 2, 1))
    pw = g("pool_w")
    pbd = np.zeros((2, 128, 2, 128), np.float32)
    for c in range(2):
        for g2 in range(2):
            pbd[:, g2 * 64:(g2 + 1) * 64, c, g2 * 64:(g2 + 1) * 64] = pw[:, c * 2 + g2]
    shared["poolw"] = f(pbd.reshape(2, 128, 256))
    shared["poolsc"] = f(g("pool_scale").reshape(2, 2, 128).transpose(0, 2, 1))
    wq = g("mla_w_uq").reshape(2, 2, 128, 4, 96)
    wq = np.concatenate([wq[..., 64:96], wq[..., 0:64]], axis=-1)
    shared["wuq"] = f(wq.transpose(0, 2, 1, 3, 4).reshape(2, 128, 768))
    shared["gq"] = f(g("mla_q_norm").reshape(2, 2, 128).transpose(0, 2, 1))
    wkv = g("mla_w_ukv").reshape(2, 128, 4, 128)
    shared["wukv"] = f(np.concatenate([wkv[..., 0:64].reshape(2, 128, 256), wkv[..., 64:128].reshape(2, 128, 256)], axis=-1))
    shared["gkv"] = f(g("mla_kv_norm").reshape(2, 128, 1))
    maps = []
    x = np.asarray(inputs["x"]); p = np.asarray(inputs["p"]); pos = np.asarray(inputs["positions"])
    for c in range(NCORES):
        b = c % 4
        m = dict(shared)
        m["x"] = f(x[b]); m["xT"] = f(x[b].T); m["pT"] = f(p[:, b].transpose(0, 2, 1))
        m["pos"] = np.ascontiguousarray(pos[b].reshape(32, 128).T.astype(np.int32))
        maps.append(m)
    return maps


def get_program():
    if "nc" not in _CACHE:
        b = Builder()
        _CACHE["nc"] = b.build()
        _CACHE["din"] = b.din
        _CACHE["stats"] = getattr(b.S, "stats", None)
    return _CACHE["nc"], _CACHE["din"]


def kernel(**inputs):
    nc, din = get_program()
    maps = prep_inputs(inputs)
    maps = [{k: v for k, v in m.items() if k in din} for m in maps]
    res = run_bass_kernel_spmd(nc, maps, core_ids=list(range(NCORES)))
    _CACHE["last"] = res
    out = np.stack([np.asarray(res.results[b]["out"]) for b in range(4)], axis=0)
    return out.astype(np.float32)
```
